# Optimizing a Trainium2 kernel written in Bass

```python
import math
import jax, jax.numpy as jnp
from jax import lax
import numpy as np

D_MODEL = 1024
BATCH = 2
SEQ = 8192
DEPTH = 2
DEC_BATCH = 32
DEC_SEQ = 1
PAST_LEN = 16384
PAGE_SIZE = 128

HEAD_DIM = 64
HEADS_PER_GROUP = 4
DIL_GROUPS = ((128, 1), (512, 4), (2048, 16))
N_ATTN_HEADS = HEADS_PER_GROUP * len(DIL_GROUPS)
QKV_W = N_ATTN_HEADS * HEAD_DIM
ATTN_OUT = HEADS_PER_GROUP * HEAD_DIM
Q_BLOCK = 128
NUM_BUCKETS = 32
MAX_DISTANCE = 2048
D_INNER = 2 * D_MODEL
SSD_HEAD_DIM = 64
SSD_HEADS = D_INNER // SSD_HEAD_DIM
SSD_GROUPS = 8
D_STATE = 128
D_CONV = 4
CONV_DIM = D_INNER + 2 * SSD_GROUPS * D_STATE
SSD_CHUNK = 128
D_FF = 4 * D_MODEL
EPS = 1e-6
N_IN = 3 * QKV_W + D_INNER + CONV_DIM + SSD_HEADS + 2 * D_MODEL

kernel_name = 'hybrid_dilated_attn_ssd_decoder_step'


def _rmsnorm(x, g):
    xf = x.astype(jnp.float32)
    y = xf * lax.rsqrt(jnp.mean(xf * xf, axis=-1, keepdims=True) + EPS)
    return (y * g.astype(jnp.float32)).astype(x.dtype)


def _gated_rmsnorm(y, z, g):
    b, L = y.shape[:2]
    h = (y * jax.nn.silu(z)).astype(jnp.float32).reshape(b, L, SSD_GROUPS, D_INNER // SSD_GROUPS)
    h = h * lax.rsqrt(jnp.mean(h * h, axis=-1, keepdims=True) + EPS)
    return (h.reshape(b, L, D_INNER) * g.astype(jnp.float32)).astype(y.dtype)


def _split_cols(t):
    sizes = (QKV_W, QKV_W, QKV_W, D_INNER, CONV_DIM, SSD_HEADS, 2 * D_MODEL)
    outs, start = [], 0
    for n in sizes:
        outs.append(t[..., start:start + n])
        start += n
    return outs


def _t5_bucket(dist):
    max_exact = NUM_BUCKETS // 2
    df = jnp.maximum(dist, 1).astype(jnp.float32)
    large = max_exact + (jnp.log(df / max_exact) / math.log(MAX_DISTANCE / max_exact)
                         * (NUM_BUCKETS - max_exact)).astype(jnp.int32)
    large = jnp.minimum(large, NUM_BUCKETS - 1)
    return jnp.where(dist < max_exact, dist, large)


def _dilated_group(q, k, v, bias, dil, q_off):
    b, Lq, H, hd = q.shape
    Lk = k.shape[1]
    nk = bias.shape[0]
    qb = min(Q_BLOCK, Lq)
    nb = -(-Lq // qb)
    pad = nb * qb - Lq
    qp = jnp.pad(q, ((0, 0), (0, pad), (0, 0), (0, 0)))
    qp = qp.reshape(b, nb, qb, H, hd).transpose(1, 0, 2, 3, 4)
    steps = jnp.arange(nk, dtype=jnp.int32) * dil
    scale = hd ** -0.5
    bias_t = bias.astype(jnp.float32).T

    def block(args):
        bi, qblk = args
        pos = q_off + bi * qb + jnp.arange(qb, dtype=jnp.int32)[:, None] - steps[None, :]
        valid = pos >= 0
        idx = jnp.clip(pos, 0, Lk - 1)
        kg = k[:, idx]
        vg = v[:, idx]
        s = jnp.einsum('bqhd,bqjhd->bqhj', qblk, kg).astype(jnp.float32) * scale + bias_t
        s = jnp.where(valid[None, :, None, :], s, -jnp.inf)
        lse = jax.nn.logsumexp(s, axis=-1)
        p = jnp.exp(s - lse[..., None]).astype(v.dtype)
        o = jnp.einsum('bqhj,bqjhd->bqhd', p, vg)
        return o, lse

    o, lse = lax.map(block, (jnp.arange(nb, dtype=jnp.int32), qp))
    o = o.transpose(1, 0, 2, 3, 4).reshape(b, nb * qb, H, hd)[:, :Lq]
    lse = lse.transpose(1, 0, 2, 3).reshape(b, nb * qb, H)[:, :Lq]
    return o, lse


def _dilated_mixer(q, k, v, rel_bias, kv_bufs):
    b, L = q.shape[:2]
    outs, lses, new_bufs = [], [], []
    for g, (win, dil) in enumerate(DIL_GROUPS):
        hs = slice(g * HEADS_PER_GROUP, (g + 1) * HEADS_PER_GROUP)
        qg, kg, vg = q[:, :, hs], k[:, :, hs], v[:, :, hs]
        nk = win // dil + 1
        bias = rel_bias[_t5_bucket(jnp.arange(nk, dtype=jnp.int32) * dil)][:, hs]
        rows = jnp.stack([kg, vg], axis=2)
        if kv_bufs is None:
            kk, vv, off = kg, vg, 0
            keep = min(win, L)
            new_bufs.append(rows[:, L - keep:])
        else:
            buf = kv_bufs[g].astype(q.dtype)
            off = buf.shape[1]
            kk = jnp.concatenate([buf[:, :, 0], kg], axis=1)
            vv = jnp.concatenate([buf[:, :, 1], vg], axis=1)
            new_bufs.append(jnp.concatenate([buf, rows], axis=1)[:, -off:])
        o, lse = _dilated_group(qg, kk, vv, bias, dil, off)
        outs.append(o)
        lses.append(lse)
    w = jax.nn.softmax(jnp.stack(lses, axis=0), axis=0)
    o = jnp.einsum('gblh,gblhd->blhd', w.astype(q.dtype), jnp.stack(outs, axis=0))
    return o.reshape(b, L, ATTN_OUT), new_bufs


def _segsum(a):
    T = a.shape[-1]
    x = jnp.broadcast_to(a[..., None], a.shape + (T,))
    x = jnp.where(jnp.tril(jnp.ones((T, T), bool), -1), x, 0.0)
    s = jnp.cumsum(x, axis=-2)
    return jnp.where(jnp.tril(jnp.ones((T, T), bool)), s, -jnp.inf)


def _ssd(x, dt, A, Bm, Cm, h0):
    b, L, H, P = x.shape
    G, N = Bm.shape[2], Bm.shape[3]
    E = H // G
    Q = min(SSD_CHUNK, L)
    nc = -(-L // Q)
    pad = nc * Q - L

    def padf(t):
        return jnp.pad(t, [(0, 0), (0, pad)] + [(0, 0)] * (t.ndim - 2))

    xf = padf(x.astype(jnp.float32) * dt[..., None]).reshape(b, nc, Q, G, E, P)
    a = padf(dt * A).reshape(b, nc, Q, G, E).transpose(0, 1, 3, 4, 2)
    Bc = padf(Bm.astype(jnp.float32)).reshape(b, nc, Q, G, N)
    Cc = padf(Cm.astype(jnp.float32)).reshape(b, nc, Q, G, N)
    a_cs = jnp.cumsum(a, axis=-1)
    lmat = jnp.exp(_segsum(a))
    cb = jnp.einsum('bclgn,bcsgn->bcgls', Cc, Bc)
    y_diag = jnp.einsum('bcgels,bcsgep->bclgep', cb[:, :, :, None] * lmat, xf)
    decay_states = jnp.exp(a_cs[..., -1:] - a_cs).transpose(0, 1, 4, 2, 3)
    states = jnp.einsum('bclgn,bclgep->bcgepn', Bc, xf * decay_states[..., None])
    states = jnp.concatenate([h0.reshape(b, 1, G, E, P, N), states], axis=1)
    chunk_decay = jnp.pad(a_cs[..., -1].transpose(0, 2, 3, 1), ((0, 0), (0, 0), (0, 0), (1, 0)))
    dc = jnp.exp(_segsum(chunk_decay))
    new_states = jnp.einsum('bgezc,bcgepn->bzgepn', dc, states)
    y_off = jnp.einsum('bclgn,bcgepn->bclgep', Cc, new_states[:, :-1]) \
        * jnp.exp(a_cs).transpose(0, 1, 4, 2, 3)[..., None]
    y = (y_diag + y_off).reshape(b, nc * Q, H, P)[:, :L]
    return y, new_states[:, -1].reshape(b, H, P, N)


def _causal_conv(xbc, buf, w, bias):
    L = xbc.shape[1]
    xp = jnp.concatenate([buf, xbc], axis=1)
    y = bias + xp[:, 0:L] * w[0]
    for t in range(1, D_CONV):
        y = y + xp[:, t:t + L] * w[t]
    return jax.nn.silu(y), xp[:, L:]


def _ssd_mixer(z, xbc, dt_raw, conv_buf, h0, conv_w, conv_b, dt_bias, a_log, d_skip, norm_g):
    b, L = z.shape[:2]
    xbc, new_buf = _causal_conv(xbc, conv_buf, conv_w, conv_b)
    xs = xbc[..., :D_INNER].reshape(b, L, SSD_HEADS, SSD_HEAD_DIM)
    Bm = xbc[..., D_INNER:D_INNER + SSD_GROUPS * D_STATE].reshape(b, L, SSD_GROUPS, D_STATE)
    Cm = xbc[..., D_INNER + SSD_GROUPS * D_STATE:].reshape(b, L, SSD_GROUPS, D_STATE)
    dt = jax.nn.softplus(dt_raw.astype(jnp.float32) + dt_bias.astype(jnp.float32))
    A = -jnp.exp(a_log.astype(jnp.float32))
    y, h = _ssd(xs, dt, A, Bm, Cm, h0)
    y = y + d_skip.astype(jnp.float32)[:, None] * xs.astype(jnp.float32)
    y = _gated_rmsnorm(y.reshape(b, L, D_INNER).astype(z.dtype), z, norm_g)
    return y, new_buf, h


def _trunk(x, c, kv_caches, ssm_state, conv_state, rel_bias, w_ada, b_ada, norm1_g, norm2_g,
           w_in, conv_w, conv_b, dt_bias, a_log, d_skip, ssd_norm_g, w_o_attn, w_o_ssd,
           w_out, w_up, w_down, final_g):
    b, L, _ = x.shape
    new_kv = [[] for _ in DIL_GROUPS]
    new_ssm, new_conv = [], []
    for l in range(DEPTH):
        if kv_caches is None:
            bufs = None
            conv_buf = jnp.zeros((b, D_CONV - 1, CONV_DIM), x.dtype)
            h0 = jnp.zeros((b, SSD_HEADS, SSD_HEAD_DIM, D_STATE), jnp.float32)
        else:
            bufs = [cache[l] for cache in kv_caches]
            conv_buf = conv_state[l].astype(x.dtype)
            h0 = ssm_state[l].astype(jnp.float32)
        mod = jax.nn.silu(c) @ w_ada[l] + b_ada[l]
        sh1, sc1, g1, sh2, sc2, g2 = jnp.split(mod[:, None, :], 6, axis=-1)
        h = _rmsnorm(x, norm1_g[l]) * (1 + sc1) + sh1
        q, k, v, z, xbc, dt_raw, gates = _split_cols(h @ w_in[l])
        q = q.reshape(b, L, N_ATTN_HEADS, HEAD_DIM)
        k = k.reshape(b, L, N_ATTN_HEADS, HEAD_DIM)
        v = v.reshape(b, L, N_ATTN_HEADS, HEAD_DIM)
        attn, bufs_new = _dilated_mixer(q, k, v, rel_bias, bufs)
        ssd, conv_new, h_new = _ssd_mixer(z, xbc, dt_raw, conv_buf, h0, conv_w[l], conv_b[l],
                                          dt_bias[l], a_log[l], d_skip[l], ssd_norm_g[l])
        g_attn, g_ssd = jnp.split(jax.nn.sigmoid(gates), 2, axis=-1)
        merged = g_attn * (attn @ w_o_attn[l]) + g_ssd * (ssd @ w_o_ssd[l])
        x = x + g1 * (merged @ w_out[l])
        h = _rmsnorm(x, norm2_g[l]) * (1 + sc2) + sh2
        x = x + g2 * (jnp.square(jax.nn.relu(h @ w_up[l])) @ w_down[l])
        for g in range(len(DIL_GROUPS)):
            new_kv[g].append(bufs_new[g])
        new_ssm.append(h_new.astype(x.dtype))
        new_conv.append(conv_new)
    y = _rmsnorm(x, final_g)
    return y, [jnp.stack(n, axis=0) for n in new_kv], jnp.stack(new_ssm, axis=0), jnp.stack(new_conv, axis=0)


def setup_inputs(seed: int = 0) -> dict:
    key = jax.random.key(seed)
    ks = jax.random.split(key, 32)
    f32 = jnp.float32

    def nrm(k, shape, s):
        return jax.random.normal(k, shape, f32) * s

    lb = [min(w, PAST_LEN) for (w, _) in DIL_GROUPS]
    dt0 = jnp.exp(jax.random.uniform(ks[17], (DEPTH, SSD_HEADS), f32, math.log(1e-3), math.log(1e-1)))
    return {
        'x_prompt': nrm(ks[0], (BATCH, SEQ, D_MODEL), 1.0),
        'x_sample': nrm(ks[1], (DEC_BATCH, DEC_SEQ, D_MODEL), 1.0),
        'cache_kv_g0': nrm(ks[2], (DEPTH, DEC_BATCH, lb[0], 2, HEADS_PER_GROUP, HEAD_DIM), 1.0),
        'cache_kv_g1': nrm(ks[3], (DEPTH, DEC_BATCH, lb[1], 2, HEADS_PER_GROUP, HEAD_DIM), 1.0),
        'cache_kv_g2': nrm(ks[4], (DEPTH, DEC_BATCH, lb[2], 2, HEADS_PER_GROUP, HEAD_DIM), 1.0),
        'state_ssm': nrm(ks[5], (DEPTH, DEC_BATCH, SSD_HEADS, SSD_HEAD_DIM, D_STATE), 0.1),
        'state_conv': nrm(ks[6], (DEPTH, DEC_BATCH, D_CONV - 1, CONV_DIM), 1.0),
        'c_prompt': nrm(ks[7], (BATCH, D_MODEL), 1.0),
        'c_sample': nrm(ks[8], (DEC_BATCH, D_MODEL), 1.0),
        'rel_bias': nrm(ks[9], (NUM_BUCKETS, N_ATTN_HEADS), 0.5),
        'w_ada': nrm(ks[10], (DEPTH, D_MODEL, 6 * D_MODEL), 0.5 * D_MODEL ** -0.5),
        'b_ada': nrm(ks[11], (DEPTH, 6 * D_MODEL), 0.02),
        'norm1_g': 1.0 + nrm(ks[12], (DEPTH, D_MODEL), 0.02),
        'norm2_g': 1.0 + nrm(ks[13], (DEPTH, D_MODEL), 0.02),
        'w_in': nrm(ks[14], (DEPTH, D_MODEL, N_IN), D_MODEL ** -0.5),
        'conv_w': nrm(ks[15], (DEPTH, D_CONV, CONV_DIM), D_CONV ** -0.5),
        'conv_b': nrm(ks[16], (DEPTH, CONV_DIM), 0.02),
        'dt_bias': dt0 + jnp.log(-jnp.expm1(-dt0)),
        'a_log': jnp.log(jax.random.uniform(ks[18], (DEPTH, SSD_HEADS), f32, 1.0, 16.0)),
        'd_skip': 1.0 + nrm(ks[19], (DEPTH, SSD_HEADS), 0.02),
        'ssd_norm_g': 1.0 + nrm(ks[20], (DEPTH, D_INNER), 0.02),
        'w_o_attn': nrm(ks[21], (DEPTH, ATTN_OUT, D_MODEL), ATTN_OUT ** -0.5),
        'w_o_ssd': nrm(ks[22], (DEPTH, D_INNER, D_MODEL), D_INNER ** -0.5),
        'w_out': nrm(ks[23], (DEPTH, D_MODEL, D_MODEL), D_MODEL ** -0.5),
        'w_up': nrm(ks[24], (DEPTH, D_MODEL, D_FF), D_MODEL ** -0.5),
        'w_down': nrm(ks[25], (DEPTH, D_FF, D_MODEL), D_FF ** -0.5),
        'final_g': 1.0 + nrm(ks[26], (D_MODEL,), 0.02),
    }


def reference(x_prompt, x_sample, cache_kv_g0, cache_kv_g1, cache_kv_g2, state_ssm, state_conv,
              c_prompt, c_sample, rel_bias, w_ada, b_ada, norm1_g, norm2_g, w_in, conv_w, conv_b,
              dt_bias, a_log, d_skip, ssd_norm_g, w_o_attn, w_o_ssd, w_out, w_up, w_down, final_g):
    weights = (rel_bias, w_ada, b_ada, norm1_g, norm2_g, w_in, conv_w, conv_b, dt_bias, a_log,
               d_skip, ssd_norm_g, w_o_attn, w_o_ssd, w_out, w_up, w_down, final_g)
    y_prompt, kv_p, ssm_prompt, conv_prompt = _trunk(x_prompt, c_prompt, None, None, None, *weights)
    y_sample, kv_s, ssm_sample, conv_sample = _trunk(
        x_sample, c_sample, (cache_kv_g0, cache_kv_g1, cache_kv_g2), state_ssm, state_conv, *weights)
    return (y_prompt, y_sample, kv_p[0], kv_p[1], kv_p[2], ssm_prompt, conv_prompt,
            kv_s[0], kv_s[1], kv_s[2], ssm_sample, conv_sample)
```

```python
import math
import os as _os_l
from contextlib import ExitStack

import numpy as np
import concourse.bass as bass
import concourse.mybir as mybir
from concourse.bass_utils import run_bass_kernel_spmd

F32 = mybir.dt.float32
BF = mybir.dt.bfloat16
AF = mybir.ActivationFunctionType
ALU = mybir.AluOpType
AX = mybir.AxisListType

D = 1024
L = int(_os_l.environ.get("KL", "8192"))
NB = L // 128
DEPTH = 2
NS = 4
NIN = 10528
QKV = 768
DIN = 2048
CONV = 4096
NH = 32
DFF = 4096
EPS = 1e-6
DILS = (1, 4, 16)
WINS = (128, 512, 2048)
C_Q, C_K, C_V, C_Z, C_X, C_DT, C_G = 0, 768, 1536, 2304, 4352, 8448, 8480
NEG8 = -240000.0

COMPUTE = {'pe': 0, 'act': 1, 'dve': 2, 'pool': 3}
DMA_ENGS = ['sp', 'pool', 'act']
NSLOT = 8
NSTREAM = 4 + len(DMA_ENGS) * NSLOT
SAME_ENGINE_SYNC = True
import os as _os
KVENG = _os.environ.get('KVENG', 'dve')
DVE_COPY_TS = _os.environ.get('DVE_COPY_TS', '0') == '1'
SAME_ENGINE_SYNC = _os.environ.get('SES', '1') == '1'
SKIP = set(_os.environ.get('KSKIP', '').split(','))


class Buf:
    __slots__ = ('lw', 'rd', 'name', 'excl')

    def __init__(self, name='', excl=False):
        self.lw = None
        self.rd = {}
        self.name = name
        self.excl = excl


class Op:
    __slots__ = ('eng', 'fn', 'sid', 'ord', 'ndma', 'ms', 'cnt', 'waits', 'deps', 'gi')


class Prog:
    def __init__(self):
        self.ops = []
        self.stream_ops = [[] for _ in range(NSTREAM)]
        self.eng_ops = {e: [] for e in ['pe', 'act', 'dve', 'pool', 'sp']}
        self.dma_count = {e: 0 for e in DMA_ENGS}
        self.barrier_deps = []

    def op(self, eng, fn, reads=(), writes=(), ndma=0):
        o = Op()
        o.eng, o.fn, o.ndma, o.ms, o.cnt, o.waits = eng, fn, ndma, False, 0, None
        deps = {}

        def add(d):
            if d is None:
                return
            cur = deps.get(d.sid)
            if cur is None or cur.ord < d.ord:
                deps[d.sid] = d

        if any(b.excl for b in reads):
            writes = list(writes) + [b for b in reads if b.excl and b not in writes]
            reads = [b for b in reads if not b.excl]
        for b in reads:
            add(b.lw)
        for b in writes:
            add(b.lw)
            for r in b.rd.values():
                add(r)
        for d in self.barrier_deps:
            add(d)
        if ndma:
            j = self.dma_count[eng]
            self.dma_count[eng] = j + 1
            o.sid = 4 + DMA_ENGS.index(eng) * NSLOT + (j % NSLOT)
            prev = self.stream_ops[o.sid]
            if prev:
                add(prev[-1])
        elif eng in COMPUTE:
            o.sid = COMPUTE[eng]
        else:
            o.sid = -1
        if o.sid >= 0:
            so = self.stream_ops[o.sid]
            o.ord = len(so) + 1
            so.append(o)
        else:
            o.ord = 0
        o.deps = list(deps.values())
        if o.sid >= 0:
            for b in reads:
                b.rd[o.sid] = o
            for b in writes:
                b.lw = o
                b.rd = {}
        o.gi = len(self.ops)
        self.ops.append(o)
        self.eng_ops[eng].append(o)
        return o

    def barrier(self):
        self.barrier_deps = [s[-1] for s in self.stream_ops if s]

    def finalize(self, nc, es):
        self.barrier()
        self.op('sp', None)
        n = len(self.ops)
        clocks = np.zeros((n, NSTREAM), np.int32)
        engclk = {e: np.zeros(NSTREAM, np.int32) for e in self.eng_ops}
        for o in self.ops:
            clk = engclk[o.eng]
            waits = []
            own = COMPUTE.get(o.eng, -2)
            for d in o.deps:
                if clk[d.sid] >= d.ord:
                    continue
                if d.sid == own and (o.eng == 'pe' or not SAME_ENGINE_SYNC):
                    continue
                waits.append(d)
                d.ms = True
                np.maximum(clk, clocks[d.gi], out=clk)
                if clk[d.sid] < d.ord:
                    clk[d.sid] = d.ord
            clocks[o.gi] = clk
            o.waits = waits
        for sid, so in enumerate(self.stream_ops):
            c = 0
            for o in so:
                if sid < 4:
                    if o.ms:
                        c += 1
                else:
                    c += 16 * o.ndma
                o.cnt = c
        sems = [es.enter_context(nc.semaphore("s%d" % i)) for i in range(NSTREAM)]
        eng_ops = self.eng_ops

        def emit(name, h):
            for o in eng_ops[name]:
                for d in o.waits:
                    h.wait_ge(sems[d.sid], d.cnt)
                if o.fn is None:
                    continue
                r = o.fn(h)
                if o.ndma:
                    rl = r if isinstance(r, (list, tuple)) else [r]
                    assert len(rl) == o.ndma, (len(rl), o.ndma)
                    for ins in rl:
                        ins.then_inc(sems[o.sid], 16)
                elif o.ms:
                    r.then_inc(sems[o.sid], 1)

        with nc.Block() as block:
            @block.tensor
            def _(e):
                emit('pe', e)

            @block.scalar
            def _(e):
                emit('act', e)

            @block.vector
            def _(e):
                emit('dve', e)

            @block.gpsimd
            def _(e):
                emit('pool', e)

            @block.sync
            def _(e):
                emit('sp', e)

    def mm(self, items, reads, writes):
        items = list(items)

        def fn(e):
            ins = None
            for (o, l, r, st, sp) in items:
                ins = e.matmul(o, l, r, start=st, stop=sp)
            return ins
        return self.op('pe', fn, reads, writes)

    def tr(self, items, reads, writes):
        items = list(items)

        def fn(e):
            ins = None
            for (o, i, idn) in items:
                ins = e.transpose(o, i, idn)
            return ins
        return self.op('pe', fn, reads, writes)

    def act(self, out, in_, func, reads, writes, **kw):
        return self.op('act', lambda e: e.activation(out=out, in_=in_, func=func, **kw), reads, writes)

    def tt(self, eng, out, in0, in1, op, reads, writes):
        return self.op(eng, lambda e: e.tensor_tensor(out=out, in0=in0, in1=in1, op=op), reads, writes)

    def ts(self, eng, out, in0, s1, s2, op0, op1, reads, writes):
        if s2 is None:
            return self.op(eng, lambda e: e.tensor_scalar(out=out, in0=in0, scalar1=s1, scalar2=None, op0=op0),
                           reads, writes)
        return self.op(eng, lambda e: e.tensor_scalar(out=out, in0=in0, scalar1=s1, scalar2=s2, op0=op0, op1=op1),
                       reads, writes)

    def stt(self, out, in0, scalar, in1, op0, op1, reads, writes):
        return self.op('dve', lambda e: e.scalar_tensor_tensor(out=out, in0=in0, scalar=scalar, in1=in1,
                                                               op0=op0, op1=op1), reads, writes)

    def copy(self, eng, out, in_, reads, writes):
        if eng == 'act':
            return self.op('act', lambda e: e.activation(out=out, in_=in_, func=AF.Copy), reads, writes)
        if eng == 'dve' and DVE_COPY_TS:
            return self.op(eng, lambda e: e.tensor_scalar(out=out, in0=in_, scalar1=1.0, scalar2=None, op0=ALU.mult),
                           reads, writes)
        return self.op(eng, lambda e: e.tensor_copy(out=out, in_=in_), reads, writes)

    def memset(self, eng, ap, val, writes):
        return self.op(eng, lambda e: e.memset(ap, val), (), writes)

    def dma(self, eng, pairs, reads, writes, slow=False):
        pairs = list(pairs)

        def fn(e):
            if slow:
                return [e.dma_start(out=o, in_=i, allow_slow_non_contiguous=True) for (o, i) in pairs]
            return [e.dma_start(out=o, in_=i) for (o, i) in pairs]
        return self.op(eng, fn, reads, writes, ndma=len(pairs))


class TB:
    __slots__ = ('t', 'b')

    def __init__(self, t, name=''):
        self.t = t
        self.b = Buf(name)


class Ctx:
    def __init__(self, nc, es):
        self.nc = nc
        self.es = es
        self.P = Prog()
        self.n = 0
        ps = es.enter_context(nc.psum_tensor("psum_all", [128, 8, 512], F32))
        self.ps = ps
        self.psb = [Buf("psb%d" % i, excl=True) for i in range(8)]
        self.psp = 0
        self.dbufs = {}

    def sb(self, stack, shape, dtype, name=None):
        self.n += 1
        nm = "%s_%d" % (name or "t", self.n)
        t = stack.enter_context(self.nc.sbuf_tensor(nm, list(shape), dtype))
        return TB(t, nm)

    def psum(self, nbanks):
        p = self.psp
        if p % nbanks:
            p += nbanks - (p % nbanks)
        if p + nbanks > 8:
            p = 0
        self.psp = (p + nbanks) % 8
        return self.ps[:, p:p + nbanks, :], self.psb[p:p + nbanks]

    def dbuf(self, key):
        b = self.dbufs.get(key)
        if b is None:
            b = Buf(str(key))
            self.dbufs[key] = b
        return b


def t5_bucket_np(dist):
    dist = np.asarray(dist, np.int32)
    max_exact = 16
    df = np.maximum(dist, 1).astype(np.float32)
    large = max_exact + (np.log(df / np.float32(max_exact)) / np.float32(math.log(2048 / max_exact))
                         * np.float32(32 - max_exact)).astype(np.int32)
    large = np.minimum(large, 31)
    return np.where(dist < max_exact, dist, large)


def make_consts():
    c = {}
    c['c_ident'] = np.eye(128, dtype=np.float32)
    c['c_anti'] = np.ascontiguousarray(np.eye(128, dtype=np.float32)[::-1])
    k = np.arange(128)
    c['c_triu'] = (k[:, None] <= k[None, :]).astype(np.float32)
    c['c_su'] = (k[:, None] > k[None, :]).astype(np.float32)
    sel = np.zeros((3, 33, 384), np.float32)
    for g, dil in enumerate(DILS):
        bk = t5_bucket_np(np.arange(129) * dil)
        sel[g, 32, :] = NEG8
        for j in range(129):
            sel[g, bk[j], j + 127] = 8.0
            sel[g, 32, j + 127] = 0.0
    c['c_sel'] = sel
    seld = np.zeros((3, 33, 128), np.float32)
    for g, dil in enumerate(DILS):
        bk = t5_bucket_np(np.arange(129) * dil)
        for p in range(128):
            seld[g, bk[128 - p], p] = 1.0
    c['c_seld'] = seld
    sel4 = np.zeros((NS, NS, 128), np.float32)
    colsel = np.zeros((NS, 4, NS), np.float32)
    for b in range(NS):
        sel4[b, b, :] = 1.0
        colsel[b, :, b] = 1.0
    c['c_sel4'] = sel4
    c['c_colsel'] = colsel
    dmask = np.zeros((4, 260), np.float32)
    for h in range(4):
        dmask[h, h * 64:(h + 1) * 64] = 1.0
        dmask[h, 256 + h] = 1.0
    c['c_dmask'] = dmask
    e5 = np.zeros((5, 5, 128), np.float32)
    for b in range(5):
        e5[b, b, :] = 1.0
    c['c_e5'] = e5
    return c


def build(stages=('all',), dbg=()):
    nc = bass.Bass("TRN2", target_bir_lowering=False)
    es = ExitStack()
    with es:
        _build(nc, es, stages, dbg)
    return nc


def _build(nc, es, stages, dbg):
    C = Ctx(nc, es)
    P = C.P
    allst = 'all' in stages
    DECODE = 'nodec' not in stages

    def din(name, shape, dt=F32):
        return nc.dram_tensor(name, list(shape), dt, kind="ExternalInput").ap()

    def dout(name, shape, dt=F32):
        return nc.dram_tensor(name, list(shape), dt, kind="ExternalOutput").ap()

    def dscr(name, shape, dt):
        kind = "ExternalOutput" if name in dbg else "Internal"
        return nc.dram_tensor(name, list(shape), dt, kind=kind).ap()

    x_in = din("x", [L, D])
    c5_in = din("c5", [5, D])
    xs_in = din("xs", [NS, D])
    kvc = [din("kvc%d" % g, [DEPTH, NS, WINS[g], 512]) for g in range(3)]
    ssm_in = din("ssm_in", [DEPTH, NS, NH * 64, 128])
    conv_in = din("conv_in", [DEPTH, NS, 3, CONV])
    rel_bias = din("rel_bias", [32, 12])
    w_ada = din("w_ada", [DEPTH, D, 6 * D])
    b_ada = din("b_ada", [DEPTH, 6 * D])
    norm1_g = din("norm1_g", [DEPTH, D])
    norm2_g = din("norm2_g", [DEPTH, D])
    w_in = din("w_in", [DEPTH, D, NIN])
    conv_w = din("conv_w", [DEPTH, 4, CONV])
    conv_b = din("conv_b", [DEPTH, CONV])
    dt_bias = din("dt_bias", [DEPTH, NH])
    a_log = din("a_log", [DEPTH, NH])
    d_skip = din("d_skip", [DEPTH, NH])
    ssd_norm_g = din("ssd_norm_g", [DEPTH, DIN])
    w_o_attn = din("w_o_attn", [DEPTH, 256, D])
    w_o_ssd = din("w_o_ssd", [DEPTH, DIN, D])
    w_out = din("w_out", [DEPTH, D, D])
    w_up = din("w_up", [DEPTH, D, DFF])
    w_down = din("w_down", [DEPTH, DFF, D])
    final_g = din("final_g", [D])
    c_ident = din("c_ident", [128, 128])
    c_anti = din("c_anti", [128, 128])
    c_triu = din("c_triu", [128, 128])
    c_su = din("c_su", [128, 128])
    c_sel = din("c_sel", [3, 33, 384])
    c_seld = din("c_seld", [3, 33, 128])
    c_e5 = din("c_e5", [5, 5, 128])
    c_sel4 = din("c_sel4", [NS, NS, 128])
    c_colsel = din("c_colsel", [NS, 4, NS])
    c_dmask = din("c_dmask", [4, 260])

    y_out = dout("y", [L, D])
    ys_out = dout("ys", [NS, D])
    kvp = [dout("kvp%d" % g, [DEPTH, WINS[g], 512]) for g in range(3)]
    ssmp = dout("ssmp", [DEPTH, NH * 64, 128])
    convp = dout("convp", [DEPTH, 3, CONV])
    kvs = [dout("kvs%d" % g, [DEPTH, NS, WINS[g], 512]) for g in range(3)]
    ssms = dout("ssms", [DEPTH, NS, NH * 64, 128])
    convs = dout("convs", [DEPTH, NS, 3, CONV])

    hT_d = dscr("hT_d", [8, 128, L], BF)
    qT_d = dscr("qT_d", [QKV, L], BF)
    kT_d = dscr("kT_d", [QKV, L], BF)
    xbcT_d = dscr("xbcT_d", [CONV, L], BF)
    gT_d = dscr("gT_d", [DIN, L], BF)
    v_d = dscr("v_d", [L, QKV], BF)
    zs_d = dscr("zs_d", [L, DIN], BF)
    dta_d = dscr("dta_d", [L, 64], F32)
    U_d = [dscr("U_d%d" % g, [L, 260], F32) for g in range(3)]
    sT_d = dscr("sT_d", [DIN, L], BF)
    x1_d = dscr("x1_d", [L, D], F32)
    xm_d = dscr("xm_d", [L, D], F32)
    vec_d = dscr("vec_d", [12, 384], F32)
    grows_d = dscr("grows_d", [DEPTH, 5, 2 * D], F32)
    dproj_d = dscr("dproj_d", [NS, NIN], F32)

    ident = C.sb(es, [128, 128], F32, "ident")
    identb = C.sb(es, [128, 128], BF, "identb")
    antib = C.sb(es, [128, 128], BF, "antib")
    triu = C.sb(es, [128, 128], F32, "triu")
    su = C.sb(es, [128, 128], F32, "su")
    ones = C.sb(es, [128, 128], F32, "ones")
    epsT = C.sb(es, [128, 1], F32, "eps")
    cT = C.sb(es, [128, 8, 5], F32, "cT")
    hank = [C.sb(es, [128, 4, 2, 128], BF, "hank%d" % g) for g in range(3)]
    A1 = C.sb(es, [128, 8, 5], F32, "A1")
    B1 = C.sb(es, [128, 8, 5], F32, "B1")
    A2 = C.sb(es, [128, 8, 5], F32, "A2")
    B2 = C.sb(es, [128, 8, 5], F32, "B2")
    g1bc = C.sb(es, [128, D], F32, "g1bc")
    g2bc = C.sb(es, [128, D], F32, "g2bc")
    cwT = C.sb(es, [128, 4, 32], F32, "cwT")
    colp = C.sb(es, [128, 96], F32, "colp")
    dtb_bc = C.sb(es, [128, NH], F32, "dtb")
    A_bc = C.sb(es, [128, NH], F32, "Abc")
    D_bc = C.sb(es, [128, NH], F32, "Dbc")
    fg_bc = C.sb(es, [128, D], F32, "fg")
    xd = C.sb(es, [NS, D], F32, "xd")
    hdT = C.sb(es, [128, 8, NS], BF, "hdT")
    adT = C.sb(es, [128, 2, NS], BF, "adT")
    sdT = C.sb(es, [128, 16, NS], BF, "sdT")
    biasd = C.sb(es, [128, 12], F32, "biasd")
    sel4 = C.sb(es, [NS, NS, 128], F32, "sel4")
    colsel = C.sb(es, [4, NS, NS], F32, "colsel")
    dmask = C.sb(es, [4, 260], F32, "dmask")

    P.dma('sp', [(ident.t[:], c_ident), (triu.t[:], c_triu), (su.t[:], c_su)], [], [ident.b, triu.b, su.b])
    P.memset('dve', ones.t[:], 1.0, [ones.b])
    P.memset('dve', epsT.t[:], EPS, [epsT.b])
    P.copy('dve', identb.t[:], ident.t[:], [ident.b], [identb.b])
    with ExitStack() as ph:
        tmp = C.sb(ph, [128, 128], F32, "tmp")
        P.dma('sp', [(tmp.t[:], c_anti)], [], [tmp.b])
        P.copy('dve', antib.t[:], tmp.t[:], [tmp.b], [antib.b])
        c5 = C.sb(ph, [5, D], F32, "c5")
        P.dma('sp', [(c5.t[:], c5_in)], [], [c5.b])
        P.act(c5.t[:], c5.t[:], AF.Silu, [c5.b], [c5.b])
        pv, pb = C.psum(1)
        P.tr([(pv[:, 0, kc * 5:(kc + 1) * 5], c5.t[:, kc * 128:(kc + 1) * 128], ident.t[0:5, 0:5]) for kc in range(8)],
             [c5.b, ident.b], pb)
        P.copy('dve', cT.t[:].rearrange("p a b -> p (a b)"), pv[:, 0, 0:40], pb, [cT.b])
        rb33 = C.sb(ph, [33, 12], F32, "rb33")
        selt = C.sb(ph, [33, 3, 384], F32, "selt")
        P.memset('dve', rb33.t[32:33, :], 1.0, [rb33.b])
        P.dma('sp', [(rb33.t[0:32, :], rel_bias), (selt.t[:], c_sel.rearrange("g r m -> r g m"))], [],
              [rb33.b, selt.b])
        seldt = C.sb(ph, [32, 3, 128], F32, "seldt")
        P.dma('sp', [(seldt.t[:], c_seld[:, 0:32, :].rearrange("g r m -> r g m")),
                     (sel4.t[:], c_sel4.rearrange("b k m -> k b m")),
                     (colsel.t[:], c_colsel.rearrange("b k m -> k b m")),
                     (dmask.t[:], c_dmask), (xd.t[:], xs_in)], [], [seldt.b, sel4.b, colsel.b, dmask.b, xd.b])
        pv, pb = C.psum(1)
        P.mm([(pv[:, 0, g * 4:(g + 1) * 4], seldt.t[:, g, :], rb33.t[0:32, g * 4:(g + 1) * 4], True, True) for g in range(3)],
             [seldt.b, rb33.b], pb)
        P.copy('act', biasd.t[:], pv[:, 0, 0:12], pb, [biasd.b])
        vecs = C.sb(ph, [4, 3, 384], F32, "vecs")
        for g in range(3):
            pv, pb = C.psum(1)
            P.mm([(pv[0:4, 0, 0:384], rb33.t[:, g * 4:(g + 1) * 4], selt.t[:, g, :], True, True)],
                 [rb33.b, selt.b], pb)
            P.copy('dve', vecs.t[:, g, :], pv[0:4, 0, 0:384], pb, [vecs.b])
        vb = C.dbuf('vec')
        P.dma('sp', [(vec_d.rearrange("(g h) m -> h g m", g=3), vecs.t[:])], [vecs.b], [vb])
        hk = C.sb(ph, [128, 24, 128], F32, "hk")
        pairs = []
        for g in range(3):
            for h in range(4):
                for kb in range(2):
                    src = bass.AP(vec_d.tensor, (g * 4 + h) * 384 + kb * 128, [[1, 128], [1, 128]])
                    pairs.append((hk.t[:, (g * 4 + h) * 2 + kb, :], src))
        for i in range(0, 24, 8):
            P.dma('sp', pairs[i:i + 8], [vb], [hk.b])
        for g in range(3):
            P.copy('dve', hank[g].t[:].rearrange("p h k q -> p (h k) q"), hk.t[:, g * 8:(g + 1) * 8, :],
                   [hk.b], [hank[g].b])
    P.barrier()

    if 'setup_only' in stages:
        dbg_o = dout("dbg_cT", [128, 40])
        P.dma('sp', [(dbg_o, cT.t[:].rearrange("p a b -> p (a b)"))], [cT.b], [C.dbuf('dbg')])
        dbg_h = dout("dbg_hank", [128, 3, 1024], BF)
        for g in range(3):
            P.dma('sp', [(dbg_h[:, g, :], hank[g].t[:].rearrange("p h k q -> p (h k q)"))], [hank[g].b],
                  [C.dbuf('dbg')])
        P.finalize(nc, es)
        return


    wbytes = lambda: None

    def phase_params(l):
        with ExitStack() as ph:
            rows = C.sb(ph, [128, 128], F32, "rows")
            rows2 = C.sb(ph, [128, 128], F32, "rows2")
            P.memset('dve', rows.t[:], 0.0, [rows.b])
            P.dma('sp', [(rows.t[0:8, :], norm1_g[l].rearrange("(a p) -> a p", p=128)),
                         (rows.t[8:16, :], norm2_g[l].rearrange("(a p) -> a p", p=128)),
                         (rows.t[16:64, :], b_ada[l].rearrange("(a p) -> a p", p=128)),
                         (rows.t[64:96, :], conv_b[l].rearrange("(a p) -> a p", p=128)),
                         (rows2.t[:], conv_w[l].rearrange("j (a p) -> (j a) p", p=128))], [], [rows.b, rows2.b])
            pv, pb = C.psum(1)
            P.tr([(pv[:, 0, 0:128], rows.t[:], ident.t[:]), (pv[:, 0, 128:256], rows2.t[:], ident.t[:])],
                 [rows.b, rows2.b, ident.b], pb)
            P.copy('dve', colp.t[:], pv[:, 0, 0:96], pb, [colp.b])
            P.copy('dve', cwT.t[:].rearrange("p j a -> p (j a)"), pv[:, 0, 128:256], pb, [cwT.b])
            P.dma('sp', [(dtb_bc.t[:], dt_bias[l:l + 1, :].to_broadcast([128, NH])),
                         (A_bc.t[:], a_log[l:l + 1, :].to_broadcast([128, NH])),
                         (D_bc.t[:], d_skip[l:l + 1, :].to_broadcast([128, NH])),
                         (fg_bc.t[:], final_g.rearrange("(o d) -> o d", o=1).to_broadcast([128, D]))], [],
                  [dtb_bc.b, A_bc.b, D_bc.b, fg_bc.b])
            P.act(A_bc.t[:], A_bc.t[:], AF.Exp, [A_bc.b], [A_bc.b])
            P.ts('dve', A_bc.t[:], A_bc.t[:], -1.0, None, ALU.mult, None, [A_bc.b], [A_bc.b])
            modT = C.sb(ph, [128, 48, 5], F32, "modT")
            grows = C.sb(ph, [5, 2 * D], F32, "grows")
            wt = [C.sb(ph, [128, 8, 512], F32, "wada%d" % i) for i in range(2)]
            pvm, pbm = C.psum(1)
            for n in range(12):
                w = wt[n % 2]
                P.dma('sp', [(w.t[:, 0:4, :], w_ada[l, 0:512, n * 512:(n + 1) * 512].rearrange("(a p) n -> p a n", p=128)),
                             (w.t[:, 4:8, :], w_ada[l, 512:1024, n * 512:(n + 1) * 512].rearrange("(a p) n -> p a n", p=128))],
                      [], [w.b])
                for fc in range(4):
                    f = n * 4 + fc
                    P.mm([(pvm[:, 0, f * 5:(f + 1) * 5], w.t[:, kc, fc * 128:(fc + 1) * 128], cT.t[:, kc, :],
                           kc == 0, kc == 7) for kc in range(8)], [w.b, cT.b], pbm)
            P.tt('dve', modT.t[:], pvm[:, 0, 0:240].rearrange("p (a b) -> p a b", b=5),
                 colp.t[:, 16:64].unsqueeze(2).to_broadcast([128, 48, 5]), ALU.add, pbm + [colp.b], [modT.b])
            for (A_, B_, goff, sc, sh) in ((A1, B1, 0, 8, 0), (A2, B2, 8, 32, 24)):
                P.ts('dve', A_.t[:], modT.t[:, sc:sc + 8, :], 1.0, None, ALU.add, None, [modT.b], [A_.b])
                P.tt('dve', A_.t[:], A_.t[:], colp.t[:, goff:goff + 8].unsqueeze(2).to_broadcast([128, 8, 5]),
                     ALU.mult, [A_.b, colp.b], [A_.b])
                P.copy('dve', B_.t[:], modT.t[:, sh:sh + 8, :], [modT.b], [B_.b])
            for half, base in ((0, 16), (1, 40)):
                pv, pb = C.psum(2)
                P.tr([(pv[0:5, kc // 4, (kc % 4) * 128:(kc % 4 + 1) * 128], modT.t[:, base + kc, :], ident.t[:])
                      for kc in range(8)], [modT.b, ident.b], pb)
                P.copy('dve', grows.t[:, half * D:(half + 1) * D].rearrange("p (a b) -> p a b", b=512),
                       pv[0:5, :, :], pb, [grows.b])
            P.dma('sp', [(grows_d[l], grows.t[:])], [grows.b], [C.dbuf(('grows', l))])
            e0 = C.sb(ph, [5, 128], F32, "e0")
            P.dma('sp', [(e0.t[:], c_e5[0])], [], [e0.b])
            for half, gb in ((0, g1bc), (1, g2bc)):
                pv, pb = C.psum(2)
                for j in range(2):
                    P.mm([(pv[:, j, :], e0.t[:], grows.t[:, half * D + j * 512: half * D + (j + 1) * 512], True, True)],
                         [e0.b, grows.b], pb[j:j + 1])
                P.copy('dve', gb.t[:].rearrange("p (a b) -> p a b", b=512), pv, pb, [gb.b])
        P.barrier()

    def rms_rstd(ph_tiles, xt, nblk, junk, ss, rstd):
        for j in range(nblk):
            P.act(junk.t[:], xt.t[:, j, :], AF.Square, [xt.b], [junk.b, ss.b], accum_out=ss.t[:, j:j + 1])
        P.act(rstd.t[:, 0:nblk], ss.t[:, 0:nblk], AF.Sqrt, [ss.b, epsT.b], [rstd.b], scale=1.0 / D, bias=epsT.t[:, 0:1])
        P.op('dve', lambda e: e.reciprocal(out=rstd.t[:, 0:nblk], in_=rstd.t[:, 0:nblk]), [rstd.b], [rstd.b])

    def norm_to_hT(xt, j, rstd, A_, B_, hT, col0, flip):
        P.act(xt.t[:, j, :], xt.t[:, j, :], AF.Copy, [xt.b, rstd.b], [xt.b], scale=rstd.t[:, j:j + 1])
        pv, pb = C.psum(2)
        P.tr([(pv[:, kc // 4, (kc % 4) * 128:(kc % 4 + 1) * 128], xt.t[:, j, kc * 128:(kc + 1) * 128], ident.t[:])
              for kc in range(8)], [xt.b, ident.b], pb)
        for kc in range(8):
            src = pv[:, kc // 4, (kc % 4) * 128:(kc % 4 + 1) * 128]
            dst = hT.t[:, kc, col0:col0 + 128]
            if (kc + flip) % 2 == 0:
                P.act(dst, src, AF.Identity, pb + [A_.b, B_.b], [hT.b], scale=A_.t[:, kc, 0:1], bias=B_.t[:, kc, 0:1])
            else:
                P.ts('dve', dst, src, A_.t[:, kc, 0:1], B_.t[:, kc, 0:1], ALU.mult, ALU.add, pb + [A_.b, B_.b], [hT.b])

    def phase_norm1(l):
        xsrc = x_in if l == 0 else xm_d
        with ExitStack() as ph:
            xts = [C.sb(ph, [128, 4, D], F32, "xt%d" % i) for i in range(2)]
            hTs = [C.sb(ph, [128, 8, 512], BF, "hTo%d" % i) for i in range(2)]
            junk = C.sb(ph, [128, D], F32, "junk")
            ss = C.sb(ph, [128, 4], F32, "ss")
            rstd = C.sb(ph, [128, 4], F32, "rstd")
            for tt in range(L // 512):
                xt, hT = xts[tt % 2], hTs[tt % 2]
                rd = [C.dbuf(('xm', tt * 4 + j)) for j in range(4)] if l > 0 else []
                P.dma('sp', [(xt.t[:], xsrc[tt * 512:(tt + 1) * 512, :].rearrange("(j p) d -> p j d", p=128))], rd, [xt.b])
                rms_rstd(ph, xt, 4, junk, ss, rstd)
                for j in range(4):
                    norm_to_hT(xt, j, rstd, A1, B1, hT, j * 128, j)
                P.dma('pool', [(hT_d[:, :, tt * 512:(tt + 1) * 512].rearrange("k p t -> p k t"), hT.t[:])],
                      [hT.b], [C.dbuf(('hT', tt))])
        P.barrier()

    def load_w_bf16(ph, wdram_rows, ncols, wb, col_off, stg, cnt):
        pass

    def cast_load(dst_ap, dst_buf, src_ap, stg_list, state, ncols):
        i = state[0]
        state[0] += 1
        stg = stg_list[i % len(stg_list)]
        P.dma('sp', [(stg.t[:, 0:ncols], src_ap)], [], [stg.b])
        eng = ('dve', 'act', 'pool')[i % 3] if False else ('dve', 'act')[i % 2]
        P.copy(eng, dst_ap, stg.t[:, 0:ncols], [stg.b], [dst_buf])

    def phase_inproj_T(l):
        groups = [
            (C_Q, 1536, [('q', i) for i in range(6)] + [('k', i) for i in range(6)]),
            (C_X, 2048, [('x', i) for i in range(16)]),
            (C_X + 2048, 2048, [('x', 16 + i) for i in range(16)]),
            (C_G, 2048, [('g', i) for i in range(16)]),
        ]
        with ExitStack() as ph:
            wbs = [C.sb(ph, [128, 8, 2048], BF, "wT%d" % i) for i in range(2)]
            stg = [C.sb(ph, [128, 2048], F32, "stg%d" % i) for i in range(2)]
            hts = [C.sb(ph, [128, 8, 512], BF, "hTi%d" % i) for i in range(2)]
            obs = [C.sb(ph, [128, 512], BF, "ob%d" % i) for i in range(4)]
            xps = [C.sb(ph, [128, 515], BF, "xp%d" % i) for i in range(3)]
            accs = [C.sb(ph, [128, 512], F32, "acc%d" % i) for i in range(2)]
            c3s = [C.sb(ph, [128, 3], F32, "c3%d" % i) for i in range(2)]
            halo = C.sb(ph, [128, 32, 3], BF, "halo")
            dg = C.sb(ph, [128, 16, 4, 128], BF, "dg")
            P.memset('dve', halo.t[:], 0.0, [halo.b])
            st = [0]
            dcnt = [0]
            cnt = {'ob': 0, 'xp': 0, 'acc': 0, 'ht': 0}
            for gi, (c0, ncols, chunks) in enumerate(groups):
                wb = wbs[gi % 2]
                for kc in range(8):
                    cast_load(wb.t[:, kc, 0:ncols], wb.b, w_in[l, kc * 128:(kc + 1) * 128, c0:c0 + ncols], stg, st, ncols)
                if DECODE:
                    dec_proj(wb, 0, ncols, c0, accs, dcnt)
                if chunks[0][0] == 'x':
                    for ci, (kind, idx) in enumerate(chunks):
                        for j in range(4):
                            P.ts(('dve', 'pool')[(ci * 4 + j) % 2], dg.t[:, ci, j, :], ident.t[:], cwT.t[:, j, idx:idx + 1], None,
                                 ALU.mult, None, [ident.b, cwT.b], [dg.b])
                for tt in range(L // 512):
                    ht = hts[cnt['ht'] % 2]
                    cnt['ht'] += 1
                    P.dma('sp', [(ht.t[:], hT_d[:, :, tt * 512:(tt + 1) * 512].rearrange("k p t -> p k t"))],
                          [C.dbuf(('hT', tt))], [ht.b])
                    for ci, (kind, idx) in enumerate(chunks):
                        pv, pb = C.psum(1)
                        P.mm([(pv[:, 0, :], wb.t[:, kc, ci * 128:(ci + 1) * 128], ht.t[:, kc, :], kc == 0, kc == 7)
                              for kc in range(8)], [wb.b, ht.b], pb)
                        ob = obs[cnt['ob'] % 4]
                        cnt['ob'] += 1
                        if kind in ('q', 'k'):
                            P.copy('act', ob.t[:], pv[:, 0, :], pb, [ob.b])
                            dst = (qT_d if kind == 'q' else kT_d)[idx * 128:(idx + 1) * 128, tt * 512:(tt + 1) * 512]
                            P.dma('act', [(dst, ob.t[:])], [ob.b], [C.dbuf((kind + 'T', idx, tt))])
                        elif kind == 'g':
                            P.act(ob.t[:], pv[:, 0, :], AF.Sigmoid, pb, [ob.b])
                            P.dma('act', [(gT_d[idx * 128:(idx + 1) * 128, tt * 512:(tt + 1) * 512], ob.t[:])],
                                  [ob.b], [C.dbuf(('gT', idx, tt))])
                        else:
                            xp = xps[cnt['xp'] % 3]
                            cnt['xp'] += 1
                            P.copy('act', xp.t[:, 3:515], pv[:, 0, :], pb, [xp.b])
                            if tt == L // 512 - 1:
                                c3 = c3s[idx % 2]
                                P.copy('act', c3.t[:], pv[:, 0, 509:512], pb, [c3.b])
                                P.dma('pool', [(convp[l, :, idx * 128:(idx + 1) * 128].rearrange("t p -> p t"), c3.t[:])],
                                      [c3.b], [C.dbuf(('convp', l, idx))], slow=True)
                            P.copy('pool', xp.t[:, 0:3], halo.t[:, idx, :], [halo.b], [xp.b])
                            P.copy('pool', halo.t[:, idx, :], xp.t[:, 512:515], [xp.b], [halo.b])
                            pc, pcb = C.psum(1)
                            P.mm([(pc[:, 0, :], dg.t[:, ci, j, :], xp.t[:, j:j + 512], j == 0, j == 3) for j in range(4)],
                                 [dg.b, xp.b], pcb)
                            P.act(ob.t[:], pc[:, 0, :], AF.Silu, pcb + [colp.b], [ob.b], bias=colp.t[:, 64 + idx:65 + idx])
                            P.dma('act', [(xbcT_d[idx * 128:(idx + 1) * 128, tt * 512:(tt + 1) * 512], ob.t[:])],
                                  [ob.b], [C.dbuf(('xbcT', idx, tt))])
        P.barrier()


    def phase_inproj_tok(l):
        NW = 3616
        with ExitStack() as ph:
            wb = C.sb(ph, [128, 8, NW], BF, "wtok")
            stg = [C.sb(ph, [128, 2048], F32, "stg%d" % i) for i in range(2)]
            hts = [C.sb(ph, [128, 8, 512], BF, "hTk%d" % i) for i in range(2)]
            vts = [C.sb(ph, [128, QKV], BF, "vt%d" % i) for i in range(2)]
            zts = [C.sb(ph, [128, DIN], BF, "zt%d" % i) for i in range(2)]
            dts = [C.sb(ph, [128, 64], F32, "dt%d" % i) for i in range(2)]
            kvf = [C.sb(ph, [128, 2 * QKV], F32, "kvf%d" % i) for i in range(2)]
            st = [0]
            for kc in range(8):
                rows = slice(kc * 128, (kc + 1) * 128)
                cast_load(wb.t[:, kc, 0:1536], wb.b, w_in[l, rows, C_K:C_K + 1536], stg, st, 1536)
                cast_load(wb.t[:, kc, 1536:3584], wb.b, w_in[l, rows, C_Z:C_Z + 2048], stg, st, 2048)
                cast_load(wb.t[:, kc, 3584:3616], wb.b, w_in[l, rows, C_DT:C_DT + 32], stg, st, 32)
            if DECODE:
                dcnt = [0]
                dec_proj(wb, 768, 768, C_V, kvf, dcnt)
                dec_proj(wb, 1536, 2048, C_Z, kvf, dcnt)
                dec_proj(wb, 3584, 32, C_DT, kvf, dcnt)
            for tt in range(L // 512):
                ht = hts[tt % 2]
                P.dma('sp', [(ht.t[:], hT_d[:, :, tt * 512:(tt + 1) * 512].rearrange("k p t -> p k t"))],
                      [C.dbuf(('hT', tt))], [ht.b])
                for j in range(4):
                    tb = tt * 4 + j
                    tok0 = tb * 128
                    vt, zt, dtt = vts[tb % 2], zts[tb % 2], dts[tb % 2]

                    def proj(c0, n, out_ap_fn):
                        pv, pb = C.psum(1)
                        P.mm([(pv[:, 0, 0:n], ht.t[:, kc, j * 128:(j + 1) * 128], wb.t[:, kc, c0:c0 + n], kc == 0, kc == 7)
                              for kc in range(8)], [wb.b, ht.b], pb)
                        return pv[:, 0, 0:n], pb
                    last = tok0 >= L - 2048 and 'kvp' not in SKIP
                    kf = kvf[tb % 2]
                    for (c0, n) in ((768, 512), (1280, 256)):
                        pa, pb = proj(c0, n, None)
                        P.copy('act', vt.t[:, c0 - 768:c0 - 768 + n], pa, pb, [vt.b])
                        if last:
                            P.copy(KVENG, kf.t[:, c0:c0 + n], pa, pb, [kf.b] + (pb if 'pbx' in SKIP else []))
                    P.dma('act', [(v_d[tok0:tok0 + 128, :], vt.t[:])], [vt.b], [C.dbuf(('v', tb))])
                    if last:
                        for (c0, n) in ((0, 512), (512, 256)):
                            pa, pb = proj(c0, n, None)
                            P.copy(KVENG, kf.t[:, c0:c0 + n], pa, pb, [kf.b] + (pb if 'pbx' in SKIP else []))
                        for g in range(3):
                            r0 = tok0 - (L - WINS[g])
                            if r0 < 0:
                                continue
                            if 'kvpdma' in SKIP:
                                continue
                            P.dma('sp', [(kvp[g][l, r0:r0 + 128, 0:256], kf.t[:, g * 256:(g + 1) * 256]),
                                         (kvp[g][l, r0:r0 + 128, 256:512], kf.t[:, 768 + g * 256:768 + (g + 1) * 256])],
                                  [kf.b], [C.dbuf(('kvp', l, tb, g))])
                    for q4 in range(4):
                        pa, pb = proj(1536 + q4 * 512, 512, None)
                        P.act(zt.t[:, q4 * 512:(q4 + 1) * 512], pa, AF.Silu, pb, [zt.b])
                    P.dma('act', [(zs_d[tok0:tok0 + 128, :], zt.t[:])], [zt.b], [C.dbuf(('zs', tb))])
                    if 'dt' in SKIP:
                        continue
                    pa, pb = proj(3584, 32, None)
                    P.tt('dve', dtt.t[:, 0:32], pa, dtb_bc.t[:], ALU.add, pb + [dtb_bc.b], [dtt.b])
                    P.act(dtt.t[:, 0:32], dtt.t[:, 0:32], AF.Exp, [dtt.b], [dtt.b])
                    P.act(dtt.t[:, 0:32], dtt.t[:, 0:32], AF.Ln, [dtt.b], [dtt.b], bias=1.0)
                    P.tt('dve', dtt.t[:, 32:64], dtt.t[:, 0:32], A_bc.t[:], ALU.mult, [dtt.b, A_bc.b], [dtt.b])
                    P.dma('act', [(dta_d[tok0:tok0 + 128, :], dtt.t[:])], [dtt.b], [C.dbuf(('dta', tb))])
        P.barrier()

    def phase_attn(l):
        NSB = L // 2048
        with ExitStack() as ph:
            qA = [[C.sb(ph, [128, 2048], BF, "qA%d%d" % (i, p)) for p in range(2)] for i in range(2)]
            qB = [[C.sb(ph, [128, 2048], BF, "qB%d%d" % (i, p)) for p in range(2)] for i in range(2)]
            kTs = [C.sb(ph, [128, 2, 4096], BF, "kT%d" % i) for i in range(2)]
            vts = [C.sb(ph, [128, 32, 4, 65], BF, "vtl%d" % i) for i in range(2)]
            pTs = [C.sb(ph, [128, 4, 2, 128], BF, "pT%d" % i) for i in range(2)]
            uos = [C.sb(ph, [128, 260], F32, "uo%d" % i) for i in range(4)]
            for i in range(2):
                for p in range(2):
                    P.memset('dve', qA[i][p].t[64:128, :], 0.0, [qA[i][p].b])
                    P.memset('dve', qB[i][p].t[0:64, :], 0.0, [qB[i][p].b])
                P.memset('dve', vts[i].t[:, :, :, 64:65], 1.0, [vts[i].b])
            it = 0
            nq = 0
            for g in range(3):
                dil = DILS[g]
                nj = 16 // dil
                for sb in range(NSB):
                    bi = it % 2
                    it += 1
                    t0 = sb * 2048
                    kT, vt = kTs[bi], vts[bi]
                    for p in range(2):
                        r0 = (g * 2 + p) * 128
                        rdq = [C.dbuf(('qT', g * 2 + p, t)) for t in range(sb * 4, sb * 4 + 4)]
                        P.dma('sp', [(qA[bi][p].t[0:64, :], qT_d[r0:r0 + 64, t0:t0 + 2048])], rdq, [qA[bi][p].b])
                        P.dma('sp', [(qB[bi][p].t[64:128, :], qT_d[r0 + 64:r0 + 128, t0:t0 + 2048])], rdq, [qB[bi][p].b])
                        if sb > 0:
                            rdk = [C.dbuf(('kT', g * 2 + p, t)) for t in range(sb * 4 - 4, sb * 4 + 4)]
                            P.dma('sp', [(kT.t[:, p, :], kT_d[r0:r0 + 128, t0 - 2048:t0 + 2048])], rdk, [kT.b])
                        else:
                            rdk = [C.dbuf(('kT', g * 2 + p, t)) for t in range(0, 4)]
                            P.dma('sp', [(kT.t[:, p, 2048:4096], kT_d[r0:r0 + 128, 0:2048])], rdk, [kT.b])
                    rdv = [C.dbuf(('v', t)) for t in range(max(0, sb * 16 - 16), sb * 16 + 16)]
                    pairs = []
                    for r in range(dil):
                        for kl in range(nj + 1):
                            kbi = sb * nj - 1 + kl
                            if kbi < 0:
                                continue
                            tok = r + dil * kbi * 128
                            src = v_d[tok:tok + dil * 127 + 1:dil, g * 256:(g + 1) * 256].rearrange("t (h d) -> t h d", h=4)
                            pairs.append((vt.t[:, r * (nj + 1) + kl, :, 0:64], src))
                    for i in range(0, len(pairs), 8):
                        P.dma('sp', pairs[i:i + 8], rdv, [vt.b])
                    for r in range(dil):
                        for jl in range(nj):
                            jb = sb * nj + jl
                            has_prev = jb > 0
                            qs = slice(r + dil * jl * 128, r + dil * jl * 128 + dil * 127 + 1, dil)
                            kcur = slice(2048 + qs.start, 2048 + qs.stop, dil)
                            kprv = slice(2048 + qs.start - 128 * dil, 2048 + qs.stop - 128 * dil, dil)
                            pv, pb = C.psum(2)
                            S = pv.rearrange("p a (h q) -> p (a h) q", q=128)
                            items = []
                            for h in range(4):
                                qz = (qA if h % 2 == 0 else qB)[bi][h // 2]
                                for kb in range(2):
                                    if kb == 1 and not has_prev:
                                        continue
                                    ks = kcur if kb == 0 else kprv
                                    items.append((S[:, h * 2 + kb, :], kT.t[:, h // 2, ks], qz.t[:, qs], True, False))
                                    items.append((S[:, h * 2 + kb, :], antib.t[:], hank[g].t[:, h, kb, :], False, True))
                            P.mm(items, [kT.b, qA[bi][0].b, qA[bi][1].b, qB[bi][0].b, qB[bi][1].b, antib.b, hank[g].b], pb)
                            pT = pTs[nq % 2]
                            if has_prev:
                                P.act(pT.t[:].rearrange("p h k q -> p (h k) q"), S, AF.Exp, pb, [pT.b], scale=0.125)
                            else:
                                P.act(pT.t[:, :, 0, :], pv.rearrange("p a (h q) -> p (a h) q", q=256)[:, :, 0:128],
                                      AF.Exp, pb, [pT.b], scale=0.125)
                            pu, pub = C.psum(1)
                            items = []
                            for h in range(4):
                                kbs = (0, 1) if has_prev else (0,)
                                for n_, kb in enumerate(kbs):
                                    vidx = r * (nj + 1) + jl + (1 - kb)
                                    items.append((pu[:, 0, h * 65:(h + 1) * 65], pT.t[:, h, kb, :], vt.t[:, vidx, h, :],
                                                  n_ == 0, n_ == len(kbs) - 1))
                            P.mm(items, [pT.b, vt.b], pub)
                            uo = uos[nq % 4]
                            P.copy('dve', uo.t[:], pu[:, 0, 0:260], pub, [uo.b])
                            tok = t0 + qs.start
                            P.dma('pool', [(U_d[g][tok:tok + dil * 127 + 1:dil, :], uo.t[:])], [uo.b],
                                  [C.dbuf(('U', g, sb, r, jl))])
                            nq += 1
        P.barrier()


    def phase_ssd(l):
        with ExitStack() as ph:
            xin = [C.sb(ph, [128, 32, 128], BF, "xin%d" % i) for i in range(2)]
            dta = [C.sb(ph, [128, 64], F32, "dta%d" % i) for i in range(2)]
            zs = [C.sb(ph, [128, DIN], BF, "zs%d" % i) for i in range(2)]
            xdt = C.sb(ph, [128, DIN], BF, "xdt")
            xtok = C.sb(ph, [128, DIN], BF, "xtok")
            btok = C.sb(ph, [128, 1024], BF, "btok")
            sm = C.sb(ph, [128, 160], F32, "sm")
            rhsA = C.sb(ph, [128, NH, 128], F32, "rhsA")
            eseg = C.sb(ph, [128, NH, 128], F32, "eseg")
            cbm = C.sb(ph, [128, 8, 128], F32, "cbm")
            mT = C.sb(ph, [128, NH, 128], BF, "mT")
            yt = C.sb(ph, [128, DIN], F32, "yt")
            y2 = C.sb(ph, [128, DIN], F32, "y2")
            ynb = C.sb(ph, [128, DIN], BF, "ynb")
            xdtd = C.sb(ph, [128, DIN], BF, "xdtd")
            state = C.sb(ph, [128, DIN], F32, "state")
            stbf = C.sb(ph, [128, DIN], BF, "stbf")
            ssq = C.sb(ph, [128, 8], F32, "ssq")
            junk = C.sb(ph, [128, 256], F32, "junk")
            sTo = [C.sb(ph, [128, 16, 128], BF, "sTo%d" % i) for i in range(2)]
            sng_bc = C.sb(ph, [128, DIN], F32, "sng")
            P.dma('sp', [(sng_bc.t[:], ssd_norm_g[l:l + 1, :].to_broadcast([128, DIN]))], [], [sng_bc.b])
            P.memset('dve', state.t[:], 0.0, [state.b])
            P.memset('dve', stbf.t[:], 0.0, [stbf.b])
            for c in range(NB):
                tok0 = c * 128
                xi, da, z = xin[c % 2], dta[c % 2], zs[c % 2]
                P.dma('sp', [(xi.t[:], xbcT_d[:, tok0:tok0 + 128].rearrange("(a p) t -> p a t", p=128)),
                             (da.t[:], dta_d[tok0:tok0 + 128, :]), (z.t[:], zs_d[tok0:tok0 + 128, :])],
                      [], [xi.b, da.b, z.b])
                dt_ap = da.t[:, 0:32]
                a_ap = da.t[:, 32:64]
                pv, pb = C.psum(2)
                pvb = C.ps[:, C.psb.index(pb[0]):C.psb.index(pb[0]) + 2, :].bitcast(BF)
                P.tr([(pvb[:, cc // 8, (cc % 8) * 128:(cc % 8 + 1) * 128], xi.t[:, cc, :], identb.t[:]) for cc in range(16)],
                     [xi.b, identb.b], pb)
                P.copy('act', xtok.t[:].rearrange("p (a b) -> p a b", b=1024), pvb, pb, [xtok.b])
                P.tt('dve', xdt.t[:].rearrange("p (h d) -> p h d", d=64), xtok.t[:].rearrange("p (h d) -> p h d", d=64),
                     dt_ap.unsqueeze(2).to_broadcast([128, NH, 64]), ALU.mult, [xtok.b, da.b], [xdt.b])
                pv, pb = C.psum(1)
                pvb = C.ps[:, C.psb.index(pb[0]):C.psb.index(pb[0]) + 1, :].bitcast(BF)
                P.tr([(pvb[:, 0, g * 128:(g + 1) * 128], xi.t[:, 16 + g, :], identb.t[:]) for g in range(8)],
                     [xi.b, identb.b], pb)
                P.copy('act', btok.t[:], pvb[:, 0, :], pb, [btok.b])
                pv, pb = C.psum(1)
                P.mm([(pv[:, 0, 0:32], triu.t[:], a_ap, True, True), (pv[:, 0, 32:64], ones.t[:], a_ap, True, True)],
                     [triu.b, ones.b, da.b], pb)
                P.copy('dve', sm.t[:, 0:32], pv[:, 0, 0:32], pb, [sm.b])
                P.copy('dve', sm.t[:, 128:160], pv[:, 0, 32:64], pb, [sm.b])
                P.act(sm.t[:, 32:64], sm.t[:, 0:32], AF.Exp, [sm.b], [sm.b])
                P.tt('dve', sm.t[:, 64:96], sm.t[:, 128:160], sm.t[:, 0:32], ALU.subtract, [sm.b], [sm.b])
                P.act(sm.t[:, 64:96], sm.t[:, 64:96], AF.Exp, [sm.b], [sm.b])
                P.act(sm.t[:, 96:128], sm.t[:, 128:160], AF.Exp, [sm.b], [sm.b])
                P.tt('dve', rhsA.t[:], triu.t[:].unsqueeze(1).to_broadcast([128, NH, 128]),
                     a_ap.unsqueeze(2).to_broadcast([128, NH, 128]), ALU.mult, [triu.b, da.b], [rhsA.b])
                for q4 in range(8):
                    pv, pb = C.psum(1)
                    P.mm([(pv[:, 0, :], su.t[:], rhsA.t[:, q4 * 4:(q4 + 1) * 4, :].rearrange("p h l -> p (h l)"), True, True)],
                         [su.b, rhsA.b], pb)
                    P.act(eseg.t[:, q4 * 4:(q4 + 1) * 4, :].rearrange("p h l -> p (h l)"), pv[:, 0, :], AF.Exp, pb, [eseg.b])
                pv, pb = C.psum(2)
                P.mm([(pv[:, g // 4, (g % 4) * 128:(g % 4 + 1) * 128], xi.t[:, 16 + g, :], xi.t[:, 24 + g, :], True, True)
                      for g in range(8)], [xi.b], pb)
                P.tt('dve', cbm.t[:], pv.rearrange("p a (g l) -> p (a g) l", l=128),
                     triu.t[:].unsqueeze(1).to_broadcast([128, 8, 128]), ALU.mult, pb + [triu.b], [cbm.b])
                P.tt('dve', mT.t[:].rearrange("p (g e) l -> p g e l", e=4), eseg.t[:].rearrange("p (g e) l -> p g e l", e=4),
                     cbm.t[:].unsqueeze(2).to_broadcast([128, 8, 4, 128]), ALU.mult, [eseg.b, cbm.b], [mT.b])
                pvy, pby = C.psum(4)
                P.mm([(pvy[:, h // 8, (h % 8) * 64:(h % 8 + 1) * 64], mT.t[:, h, :], xdt.t[:, h * 64:(h + 1) * 64], True, True)
                      for h in range(NH)], [mT.b, xdt.b], pby)
                pvo, pbo = C.psum(4)
                P.mm([(pvo[:, g // 2, (g % 2) * 256:(g % 2 + 1) * 256], xi.t[:, 24 + g, :], stbf.t[:, g * 256:(g + 1) * 256], True, True)
                      for g in range(8)], [xi.b, stbf.b], pbo)
                P.tt('dve', y2.t[:].rearrange("p (h d) -> p h d", d=64), pvo.rearrange("p a (h d) -> p (a h) d", d=64),
                     sm.t[:, 32:64].unsqueeze(2).to_broadcast([128, NH, 64]), ALU.mult, pbo + [sm.b], [y2.b])
                P.tt('dve', yt.t[:].rearrange("p (a b) -> p a b", b=512), pvy, y2.t[:].rearrange("p (a b) -> p a b", b=512),
                     ALU.add, pby + [y2.b], [yt.b])
                P.tt('dve', y2.t[:].rearrange("p (h d) -> p h d", d=64), xtok.t[:].rearrange("p (h d) -> p h d", d=64),
                     D_bc.t[:].unsqueeze(2).to_broadcast([128, NH, 64]), ALU.mult, [xtok.b, D_bc.b], [y2.b])
                P.tt('dve', yt.t[:], yt.t[:], y2.t[:], ALU.add, [yt.b, y2.b], [yt.b])
                P.tt('dve', yt.t[:], yt.t[:], z.t[:], ALU.mult, [yt.b, z.b], [yt.b])
                for g in range(8):
                    P.act(junk.t[:], yt.t[:, g * 256:(g + 1) * 256], AF.Square, [yt.b], [junk.b, ssq.b],
                          accum_out=ssq.t[:, g:g + 1])
                P.act(ssq.t[:], ssq.t[:], AF.Sqrt, [ssq.b, epsT.b], [ssq.b], scale=1.0 / 256, bias=epsT.t[:, 0:1])
                P.op('dve', lambda e: e.reciprocal(out=ssq.t[:], in_=ssq.t[:]), [ssq.b], [ssq.b])
                P.tt('dve', yt.t[:].rearrange("p (g d) -> p g d", d=256), yt.t[:].rearrange("p (g d) -> p g d", d=256),
                     ssq.t[:].unsqueeze(2).to_broadcast([128, 8, 256]), ALU.mult, [yt.b, ssq.b], [yt.b])
                P.tt('dve', ynb.t[:], yt.t[:], sng_bc.t[:], ALU.mult, [yt.b, sng_bc.b], [ynb.b])
                pv, pb = C.psum(2)
                pvb = C.ps[:, C.psb.index(pb[0]):C.psb.index(pb[0]) + 2, :].bitcast(BF)
                P.tr([(pvb[:, cc // 8, (cc % 8) * 128:(cc % 8 + 1) * 128], ynb.t[:, cc * 128:(cc + 1) * 128], identb.t[:])
                      for cc in range(16)], [ynb.b, identb.b], pb)
                so = sTo[c % 2]
                P.copy('act', so.t[:].rearrange("p (a b) l -> p a (b l)", a=2), pvb, pb, [so.b])
                P.dma('pool', [(sT_d[:, tok0:tok0 + 128].rearrange("(a p) t -> p a t", p=128), so.t[:])], [so.b],
                      [C.dbuf(('sT', c))])
                P.tt('dve', xdtd.t[:].rearrange("p (h d) -> p h d", d=64), xdt.t[:].rearrange("p (h d) -> p h d", d=64),
                     sm.t[:, 64:96].unsqueeze(2).to_broadcast([128, NH, 64]), ALU.mult, [xdt.b, sm.b], [xdtd.b])
                pvs, pbs = C.psum(4)
                P.mm([(pvs[:, g // 2, (g % 2) * 256:(g % 2 + 1) * 256], btok.t[:, g * 128:(g + 1) * 128],
                       xdtd.t[:, g * 256:(g + 1) * 256], True, True) for g in range(8)], [btok.b, xdtd.b], pbs)
                P.tt('dve', state.t[:].rearrange("p (h d) -> p h d", d=64), state.t[:].rearrange("p (h d) -> p h d", d=64),
                     sm.t[:, 96:128].unsqueeze(2).to_broadcast([128, NH, 64]), ALU.mult, [state.b, sm.b], [state.b])
                P.tt('dve', state.t[:].rearrange("p (a b) -> p a b", b=512), pvs, state.t[:].rearrange("p (a b) -> p a b", b=512),
                     ALU.add, pbs + [state.b], [state.b])
                P.copy('act', stbf.t[:], state.t[:], [state.b], [stbf.b])
            for cc in range(16):
                pv, pb = C.psum(1)
                P.tr([(pv[:, 0, 0:128], state.t[:, cc * 128:(cc + 1) * 128], ident.t[:])], [state.b, ident.b], pb)
                o_ = sTo[cc % 2]
                of = yt
                P.copy('dve', yt.t[:, cc * 128:(cc + 1) * 128], pv[:, 0, 0:128], pb, [yt.b])
            P.dma('pool', [(ssmp[l].rearrange("(a p) n -> p a n", p=128), yt.t[:].rearrange("p (a n) -> p a n", n=128))],
                  [yt.b], [C.dbuf(('ssmp', l))])
        P.barrier()


    def phase_out(l):
        with ExitStack() as ph:
            woa = C.sb(ph, [128, 2, D], BF, "woa")
            wos = C.sb(ph, [128, 16, D], BF, "wos")
            wo = C.sb(ph, [128, 8, D], BF, "wo")
            stg = [C.sb(ph, [128, 2048], F32, "stg%d" % i) for i in range(2)]
            st = [0]
            for kc in range(2):
                cast_load(woa.t[:, kc, :], woa.b, w_o_attn[l, kc * 128:(kc + 1) * 128, :], stg, st, D)
            for kc in range(16):
                cast_load(wos.t[:, kc, :], wos.b, w_o_ssd[l, kc * 128:(kc + 1) * 128, :], stg, st, D)
            for kc in range(8):
                cast_load(wo.t[:, kc, :], wo.b, w_out[l, kc * 128:(kc + 1) * 128, :], stg, st, D)
            us = [C.sb(ph, [128, 3, 260], F32, "us%d" % i) for i in range(2)]
            rz = C.sb(ph, [128, 4], F32, "rz")
            att = C.sb(ph, [128, 256], F32, "att")
            aT = [C.sb(ph, [128, 2, 512], BF, "aT%d" % i) for i in range(2)]
            sT = [C.sb(ph, [128, 16, 512], BF, "sT%d" % i) for i in range(1)]
            gT = [C.sb(ph, [128, 16, 512], BF, "gT%d" % i) for i in range(1)]
            mg = C.sb(ph, [128, 8, 512], BF, "mg")
            t1 = [C.sb(ph, [128, 512], F32, "t1%d" % i) for i in range(2)]
            t2 = [C.sb(ph, [128, 512], F32, "t2%d" % i) for i in range(2)]
            xts = [C.sb(ph, [128, 4, D], F32, "xo%d" % i) for i in range(2)]
            xsrc = x_in if l == 0 else xm_d
            if DECODE:
                dec_out(ph, l, woa, wos, wo, xts[1], t1)
            for tt in range(L // 512):
                a_, s_, g_, xt = aT[tt % 2], sT[0], gT[0], xts[tt % 2]
                ts_ = slice(tt * 512, (tt + 1) * 512)
                P.dma('sp', [(s_.t[:], sT_d[:, ts_].rearrange("(a p) t -> p a t", p=128)),
                             (g_.t[:], gT_d[:, ts_].rearrange("(a p) t -> p a t", p=128)),
                             (xt.t[:], xsrc[ts_, :].rearrange("(j p) d -> p j d", p=128))], [], [s_.b, g_.b, xt.b])
                for j in range(4):
                    tok0 = tt * 512 + j * 128
                    u = us[j % 2]
                    P.dma('sp', [(u.t[:, g, :], U_d[g][tok0:tok0 + 128, :]) for g in range(3)], [], [u.b])
                    P.tt('dve', u.t[:, 0, :], u.t[:, 0, :], u.t[:, 1, :], ALU.add, [u.b], [u.b])
                    P.tt('dve', u.t[:, 0, :], u.t[:, 0, :], u.t[:, 2, :], ALU.add, [u.b], [u.b])
                    uv = u.t[:, 0, :].rearrange("p (h d) -> p h d", d=65)
                    P.op('dve', lambda e, uv=uv: e.reciprocal(out=rz.t[:].unsqueeze(2), in_=uv[:, :, 64:65]), [u.b], [rz.b])
                    P.tt('dve', att.t[:].rearrange("p (h d) -> p h d", d=64), uv[:, :, 0:64],
                         rz.t[:].unsqueeze(2).to_broadcast([128, 4, 64]), ALU.mult, [u.b, rz.b], [att.b])
                    pv, pb = C.psum(1)
                    P.tr([(pv[:, 0, cc * 128:(cc + 1) * 128], att.t[:, cc * 128:(cc + 1) * 128], ident.t[:]) for cc in range(2)],
                         [att.b, ident.b], pb)
                    P.copy('act', a_.t[:, :, j * 128:(j + 1) * 128], pv[:, 0, 0:256].rearrange("p (c t) -> p c t", t=128),
                           pb, [a_.b])
                for fc in range(8):
                    fs = slice(fc * 128, (fc + 1) * 128)
                    pa, pab = C.psum(1)
                    P.mm([(pa[:, 0, :], woa.t[:, kc, fs], a_.t[:, kc, :], kc == 0, kc == 1) for kc in range(2)],
                         [woa.b, a_.b], pab)
                    pbs_, pbb = C.psum(1)
                    P.mm([(pbs_[:, 0, :], wos.t[:, kc, fs], s_.t[:, kc, :], kc == 0, kc == 15) for kc in range(16)],
                         [wos.b, s_.b], pbb)
                    P.tt('dve', t1[fc % 2].t[:], pa[:, 0, :], g_.t[:, fc, :], ALU.mult, pab + [g_.b], [t1[fc % 2].b])
                    P.tt('dve', t2[fc % 2].t[:], pbs_[:, 0, :], g_.t[:, 8 + fc, :], ALU.mult, pbb + [g_.b], [t2[fc % 2].b])
                    P.tt('dve', mg.t[:, fc, :], t1[fc % 2].t[:], t2[fc % 2].t[:], ALU.add, [t1[fc % 2].b, t2[fc % 2].b], [mg.b])
                for j in range(4):
                    for n in range(2):
                        ns = slice(n * 512, (n + 1) * 512)
                        pv, pb = C.psum(1)
                        P.mm([(pv[:, 0, :], mg.t[:, kc, j * 128:(j + 1) * 128], wo.t[:, kc, ns], kc == 0, kc == 7)
                              for kc in range(8)], [mg.b, wo.b], pb)
                        P.tt('dve', t1[n].t[:], pv[:, 0, :], g1bc.t[:, ns], ALU.mult, pb + [g1bc.b], [t1[n].b])
                        P.tt('dve', xt.t[:, j, ns], xt.t[:, j, ns], t1[n].t[:], ALU.add, [xt.b, t1[n].b], [xt.b])
                P.dma('pool', [(x1_d[ts_, :].rearrange("(j p) d -> p j d", p=128), xt.t[:])], [xt.b], [C.dbuf(('x1', tt))])
        P.barrier()

    def phase_mlp(l):
        last = (l == DEPTH - 1)
        TT = 256
        with ExitStack() as ph:
            wu = C.sb(ph, [128, 8, DFF], BF, "wu")
            wd = C.sb(ph, [128, 32, D], BF, "wd")
            stg = [C.sb(ph, [128, 512], F32, "stg%d" % i) for i in range(2)]
            st = [0]
            for kc in range(8):
                for hf in range(8):
                    cast_load(wu.t[:, kc, hf * 512:(hf + 1) * 512], wu.b,
                              w_up[l, kc * 128:(kc + 1) * 128, hf * 512:(hf + 1) * 512], stg, st, 512)
            for kc in range(32):
                for hf in range(2):
                    cast_load(wd.t[:, kc, hf * 512:(hf + 1) * 512], wd.b,
                              w_down[l, kc * 128:(kc + 1) * 128, hf * 512:(hf + 1) * 512], stg, st, 512)
            xts = [C.sb(ph, [128, 2, D], F32, "xm%d" % i) for i in range(1)]
            xns = [C.sb(ph, [128, 2, D], F32, "xn%d" % i) for i in range(1)]
            h2 = [C.sb(ph, [128, 8, TT], BF, "h2%d" % i) for i in range(1)]
            hid = C.sb(ph, [128, 32, TT], BF, "hid")
            rl = [C.sb(ph, [128, TT], F32, "rl%d" % i) for i in range(2)]
            t1 = [C.sb(ph, [128, 512], F32, "tm%d" % i) for i in range(2)]
            junk = C.sb(ph, [128, D], BF, "junk")
            ss = C.sb(ph, [128, 4], F32, "ss")
            rstd = C.sb(ph, [128, 4], F32, "rstd")
            if DECODE:
                dec_mlp(ph, l, wu, wd, xns[0], t1, last)
            for tt in range(L // TT):
                xt, xn, h2t = xts[0], xns[0], h2[0]
                ts_ = slice(tt * TT, (tt + 1) * TT)
                P.dma('sp', [(xt.t[:], x1_d[ts_, :].rearrange("(j p) d -> p j d", p=128))], [], [xt.b])
                P.copy('pool', xn.t[:], xt.t[:], [xt.b], [xn.b])
                rms_rstd(ph, xn, 2, junk, ss, rstd)
                for j in range(2):
                    norm_to_hT(xn, j, rstd, A2, B2, h2t, j * 128, j)
                for hc in range(32):
                    pv, pb = C.psum(1)
                    P.mm([(pv[:, 0, 0:TT], wu.t[:, kc, hc * 128:(hc + 1) * 128], h2t.t[:, kc, :], kc == 0, kc == 7)
                          for kc in range(8)], [wu.b, h2t.b], pb)
                    r_ = rl[hc % 2]
                    P.act(r_.t[:], pv[:, 0, 0:TT], AF.Relu, pb, [r_.b])
                    P.tt('dve', hid.t[:, hc, :], r_.t[:], pv[:, 0, 0:TT], ALU.mult, pb + [r_.b], [hid.b])
                for j in range(2):
                    for n in range(2):
                        ns = slice(n * 512, (n + 1) * 512)
                        pv, pb = C.psum(1)
                        P.mm([(pv[:, 0, :], hid.t[:, hc, j * 128:(j + 1) * 128], wd.t[:, hc, ns], hc == 0, hc == 31)
                              for hc in range(32)], [hid.b, wd.b], pb)
                        P.tt('dve', t1[n].t[:], pv[:, 0, :], g2bc.t[:, ns], ALU.mult, pb + [g2bc.b], [t1[n].b])
                        P.tt('dve', xt.t[:, j, ns], xt.t[:, j, ns], t1[n].t[:], ALU.add, [xt.b, t1[n].b], [xt.b])
                if not last:
                    P.dma('pool', [(xm_d[ts_, :].rearrange("(j p) d -> p j d", p=128), xt.t[:])], [xt.b],
                          [C.dbuf(('xmw', tt))])
                else:
                    rms_rstd(ph, xt, 2, junk, ss, rstd)
                    for j in range(2):
                        P.ts('dve', xt.t[:, j, :], xt.t[:, j, :], rstd.t[:, j:j + 1], None, ALU.mult, None,
                             [xt.b, rstd.b], [xt.b])
                        P.tt('dve', xt.t[:, j, :], xt.t[:, j, :], fg_bc.t[:], ALU.mult, [xt.b, fg_bc.b], [xt.b])
                    P.dma('pool', [(y_out[ts_, :].rearrange("(j p) d -> p j d", p=128), xt.t[:])], [xt.b],
                          [C.dbuf(('y', tt))])
        P.barrier()


    def dec_norm_T(ph, xsrc_ap, xsrc_buf, A_, B_, outT, scr):
        ss = C.sb(ph, [NS, 2], F32, "dss")
        tmpT = C.sb(ph, [128, 8, NS], F32, "dtmpT")
        xn = scr.t[0:NS, 0:D] if len(scr.t.shape) == 2 else scr.t[0:NS, 0, :]
        P.act(xn, xsrc_ap, AF.Square, [xsrc_buf], [scr.b, ss.b], accum_out=ss.t[:, 0:1])
        P.act(ss.t[:, 1:2], ss.t[:, 0:1], AF.Sqrt, [ss.b, epsT.b], [ss.b], scale=1.0 / D, bias=epsT.t[0:NS, 0:1])
        P.op('dve', lambda e: e.reciprocal(out=ss.t[:, 1:2], in_=ss.t[:, 1:2]), [ss.b], [ss.b])
        P.act(xn, xsrc_ap, AF.Copy, [xsrc_buf, ss.b], [scr.b], scale=ss.t[:, 1:2])
        pv, pb = C.psum(1)
        P.tr([(pv[:, 0, kc * NS:(kc + 1) * NS], xn[:, kc * 128:(kc + 1) * 128], ident.t[0:NS, 0:NS]) for kc in range(8)],
             [scr.b, ident.b], pb)
        P.tt('dve', tmpT.t[:], pv[:, 0, 0:8 * NS].rearrange("p (k b) -> p k b", b=NS), A_.t[:, :, 1:1 + NS], ALU.mult,
             pb + [A_.b], [tmpT.b])
        P.tt('dve', outT.t[:], tmpT.t[:], B_.t[:, :, 1:1 + NS], ALU.add, [tmpT.b, B_.b], [outT.b])
        return ss

    def dec_proj(wb, wcol, ncols, dcol, tmps, cnt):
        for n0 in range(0, ncols, 512):
            n = min(512, ncols - n0)
            pv, pb = C.psum(1)
            P.mm([(pv[0:NS, 0, 0:n], hdT.t[:, kc, :], wb.t[:, kc, wcol + n0:wcol + n0 + n], kc == 0, kc == 7)
                  for kc in range(8)], [hdT.b, wb.b], pb)
            t = tmps[cnt[0] % len(tmps)]
            cnt[0] += 1
            P.copy('act', t.t[0:NS, 0:n], pv[0:NS, 0, 0:n], pb, [t.b])
            P.dma('pool', [(dproj_d[:, dcol + n0:dcol + n0 + n], t.t[0:NS, 0:n])], [t.b], [Buf()])

    def phase_dec_norm1(l):
        with ExitStack() as ph:
            scr = C.sb(ph, [NS, D], F32, "dscr")
            dec_norm_T(ph, xd.t[:], xd.b, A1, B1, hdT, scr)
        P.barrier()

    def phase_dec_mixers(l):
        with ExitStack() as ph:
            qkvd = C.sb(ph, [NS, 2304], F32, "qkvd")
            xnew = C.sb(ph, [NS, CONV], F32, "xnew")
            xc = C.sb(ph, [NS, CONV], F32, "xc")
            zd = C.sb(ph, [NS, DIN], F32, "zd")
            dtd = C.sb(ph, [NS, 64], F32, "dtd")
            P.dma('sp', [(qkvd.t[:], dproj_d[:, 0:2304]), (xnew.t[:], dproj_d[:, C_X:C_X + CONV]),
                         (zd.t[:], dproj_d[:, C_Z:C_Z + DIN]), (dtd.t[:, 0:32], dproj_d[:, C_DT:C_DT + 32])], [],
                  [qkvd.b, xnew.b, zd.b, dtd.b])
            P.dma('pool', [(convs[l, :, 0:2, :], conv_in[l, :, 1:3, :]), (convs[l, :, 2, :], xnew.t[:])], [xnew.b], [Buf()])
            cst = C.sb(ph, [NS, 3, 1024], F32, "cst")
            cw4 = C.sb(ph, [NS, 4, 1024], F32, "cw4")
            cb4 = C.sb(ph, [NS, 1024], F32, "cb4")
            ctmp = C.sb(ph, [NS, 1024], F32, "ctmp")
            for q4 in range(4):
                cs_ = slice(q4 * 1024, (q4 + 1) * 1024)
                P.dma('sp', [(cst.t[:], conv_in[l, :, :, cs_]),
                             (cw4.t[:], conv_w[l:l + 1, :, cs_].to_broadcast([NS, 4, 1024])),
                             (cb4.t[:], conv_b[l:l + 1, cs_].to_broadcast([NS, 1024]))], [], [cst.b, cw4.b, cb4.b])
                P.tt('dve', xc.t[:, cs_], xnew.t[:, cs_], cw4.t[:, 3, :], ALU.mult, [xnew.b, cw4.b], [xc.b])
                P.tt('dve', xc.t[:, cs_], xc.t[:, cs_], cb4.t[:], ALU.add, [xc.b, cb4.b], [xc.b])
                for j in range(3):
                    P.tt('dve', ctmp.t[:], cst.t[:, j, :], cw4.t[:, j, :], ALU.mult, [cst.b, cw4.b], [ctmp.b])
                    P.tt('dve', xc.t[:, cs_], xc.t[:, cs_], ctmp.t[:], ALU.add, [xc.b, ctmp.b], [xc.b])
            P.act(xc.t[:], xc.t[:], AF.Silu, [xc.b], [xc.b])
            P.tt('dve', dtd.t[:, 0:32], dtd.t[:, 0:32], dtb_bc.t[0:NS, :], ALU.add, [dtd.b, dtb_bc.b], [dtd.b])
            P.act(dtd.t[:, 0:32], dtd.t[:, 0:32], AF.Exp, [dtd.b], [dtd.b])
            P.act(dtd.t[:, 0:32], dtd.t[:, 0:32], AF.Ln, [dtd.b], [dtd.b], bias=1.0)
            P.tt('dve', dtd.t[:, 32:64], dtd.t[:, 0:32], A_bc.t[0:NS, :], ALU.mult, [dtd.b, A_bc.b], [dtd.b])
            P.act(dtd.t[:, 32:64], dtd.t[:, 32:64], AF.Exp, [dtd.b], [dtd.b])
            rows = C.sb(ph, [NS, 2, DIN], F32, "drows")
            P.tt('dve', rows.t[:, 0, :].rearrange("p (h d) -> p h d", d=64), xc.t[:, 0:DIN].rearrange("p (h d) -> p h d", d=64),
                 dtd.t[:, 0:32].unsqueeze(2).to_broadcast([NS, NH, 64]), ALU.mult, [xc.b, dtd.b], [rows.b])
            P.copy('dve', rows.t[:, 1, :].rearrange("p (h d) -> p h d", d=64),
                   dtd.t[:, 32:64].unsqueeze(2).to_broadcast([NS, NH, 64]), [dtd.b], [rows.b])
            colD = C.sb(ph, [128, 2, 16, NS], F32, "colD")
            pv, pb = C.psum(1)
            P.tr([(pv[:, 0, (w * 16 + cc) * NS:(w * 16 + cc + 1) * NS], rows.t[:, w, cc * 128:(cc + 1) * 128], ident.t[0:NS, 0:NS])
                  for w in range(2) for cc in range(16)], [rows.b, ident.b], pb)
            P.copy('act', colD.t[:].rearrange("p w c b -> p (w c b)"), pv[:, 0, 0:32 * NS], pb, [colD.b])
            stt_ = [C.sb(ph, [128, 16, 128], F32, "dst%d" % i) for i in range(2)]
            t1 = C.sb(ph, [128, 16, 128], F32, "dt1")
            ycol = C.sb(ph, [128, 16, NS], F32, "ycol")
            for b in range(NS):
                st_ = stt_[b % 2]
                P.dma('sp', [(st_.t[:], ssm_in[l, b].rearrange("(a p) n -> p a n", p=128))], [], [st_.b])
                pv, pb = C.psum(4)
                for j in range(4):
                    P.mm([(pv[:, j, :], sel4.t[:, b, :], xc.t[:, DIN + j * 512:DIN + (j + 1) * 512], True, True)],
                         [sel4.b, xc.b], pb[j:j + 1])
                Bv = pv[:, 0:2, :].rearrange("p a (g n) -> p (a g) n", n=128)
                Cv = pv[:, 2:4, :].rearrange("p a (g n) -> p (a g) n", n=128)
                P.tt('dve', t1.t[:].rearrange("p (g e) n -> p g e n", e=2), Bv.unsqueeze(2).to_broadcast([128, 8, 2, 128]),
                     colD.t[:, 0, :, b].rearrange("p (g e) -> p g e", e=2).unsqueeze(3).to_broadcast([128, 8, 2, 128]),
                     ALU.mult, pb[0:2] + [colD.b], [t1.b])
                P.tt('dve', st_.t[:], st_.t[:], colD.t[:, 1, :, b].unsqueeze(2).to_broadcast([128, 16, 128]), ALU.mult,
                     [st_.b, colD.b], [st_.b])
                P.tt('dve', st_.t[:], st_.t[:], t1.t[:], ALU.add, [st_.b, t1.b], [st_.b])
                P.dma('pool', [(ssms[l, b].rearrange("(a p) n -> p a n", p=128), st_.t[:])], [st_.b], [Buf()])
                P.tt('dve', t1.t[:].rearrange("p (g e) n -> p g e n", e=2), st_.t[:].rearrange("p (g e) n -> p g e n", e=2),
                     Cv.unsqueeze(2).to_broadcast([128, 8, 2, 128]), ALU.mult, pb[2:4] + [st_.b], [t1.b])
                P.op('dve', lambda e, b=b: e.tensor_reduce(out=ycol.t[:, :, b], in_=t1.t[:], axis=AX.X, op=ALU.add),
                     [t1.b], [ycol.b])
            pv, pb = C.psum(4)
            P.tr([(pv[0:NS, cc // 4, (cc % 4) * 128:(cc % 4 + 1) * 128], ycol.t[:, cc, :], ident.t[:]) for cc in range(16)],
                 [ycol.b, ident.b], pb)
            yd = C.sb(ph, [NS, DIN], F32, "yd")
            P.tt('dve', rows.t[:, 0, :].rearrange("p (h d) -> p h d", d=64), xc.t[:, 0:DIN].rearrange("p (h d) -> p h d", d=64),
                 D_bc.t[0:NS, :].unsqueeze(2).to_broadcast([NS, NH, 64]), ALU.mult, [xc.b, D_bc.b], [rows.b])
            P.tt('dve', yd.t[:].rearrange("p (a b) -> p a b", b=512), pv[0:NS, :, :], rows.t[:, 0, :].rearrange("p (a b) -> p a b", b=512),
                 ALU.add, pb + [rows.b], [yd.b])
            P.act(zd.t[:], zd.t[:], AF.Silu, [zd.b], [zd.b])
            P.tt('dve', yd.t[:], yd.t[:], zd.t[:], ALU.mult, [yd.b, zd.b], [yd.b])
            ssq = C.sb(ph, [NS, 8], F32, "dssq")
            for g in range(8):
                P.act(rows.t[:, 1, 0:256], yd.t[:, g * 256:(g + 1) * 256], AF.Square, [yd.b], [rows.b, ssq.b],
                      accum_out=ssq.t[:, g:g + 1])
            P.act(ssq.t[:], ssq.t[:], AF.Sqrt, [ssq.b, epsT.b], [ssq.b], scale=1.0 / 256, bias=epsT.t[0:NS, 0:1])
            P.op('dve', lambda e: e.reciprocal(out=ssq.t[:], in_=ssq.t[:]), [ssq.b], [ssq.b])
            P.tt('dve', yd.t[:].rearrange("p (g d) -> p g d", d=256), yd.t[:].rearrange("p (g d) -> p g d", d=256),
                 ssq.t[:].unsqueeze(2).to_broadcast([NS, 8, 256]), ALU.mult, [yd.b, ssq.b], [yd.b])
            P.dma('sp', [(rows.t[:, 1, :], ssd_norm_g[l:l + 1, :].to_broadcast([NS, DIN]))], [], [rows.b])
            P.tt('dve', yd.t[:], yd.t[:], rows.t[:, 1, :], ALU.mult, [yd.b, rows.b], [yd.b])
            pv, pb = C.psum(1)
            P.tr([(pv[:, 0, cc * NS:(cc + 1) * NS], yd.t[:, cc * 128:(cc + 1) * 128], ident.t[0:NS, 0:NS]) for cc in range(16)],
                 [yd.b, ident.b], pb)
            P.copy('act', sdT.t[:].rearrange("p c b -> p (c b)"), pv[:, 0, 0:16 * NS], pb, [sdT.b])
            for g in range(3):
                w = WINS[g]
                P.dma('pool', [(kvs[g][l, :, 0:w - 1, :], kvc[g][l, :, 1:w, :]),
                               (kvs[g][l, :, w - 1, 0:256], qkvd.t[:, 768 + g * 256:768 + (g + 1) * 256]),
                               (kvs[g][l, :, w - 1, 256:512], qkvd.t[:, 1536 + g * 256:1536 + (g + 1) * 256])],
                      [qkvd.b], [Buf()])
            prod0 = C.sb(ph, [NS, 768], F32, "prod0")
            e0 = C.sb(ph, [NS, 24], F32, "e0")
            P.dma('sp', [(e0.t[:, 12:24], rel_bias[0:1, :].to_broadcast([NS, 12]))], [], [e0.b])
            P.tt('dve', prod0.t[:], qkvd.t[:, 0:768], qkvd.t[:, 768:1536], ALU.mult, [qkvd.b], [prod0.b])
            P.op('dve', lambda e: e.tensor_reduce(out=e0.t[:, 0:12], in_=prod0.t[:].rearrange("p (h d) -> p h d", d=64),
                                                  axis=AX.X, op=ALU.add), [prod0.b], [e0.b])
            P.stt(e0.t[:, 0:12], e0.t[:, 0:12], 0.125, e0.t[:, 12:24], ALU.mult, ALU.add, [e0.b], [e0.b])
            P.act(e0.t[:, 0:12], e0.t[:, 0:12], AF.Exp, [e0.b], [e0.b])
            P.tt('dve', prod0.t[:].rearrange("p (h d) -> p h d", d=64), qkvd.t[:, 1536:2304].rearrange("p (h d) -> p h d", d=64),
                 e0.t[:, 0:12].unsqueeze(2).to_broadcast([NS, 12, 64]), ALU.mult, [qkvd.b, e0.b], [prod0.b])
            u0 = C.sb(ph, [NS, 260], F32, "u0")
            P.tt('dve', u0.t[:, 0:256], prod0.t[:, 0:256], prod0.t[:, 256:512], ALU.add, [prod0.b], [u0.b])
            P.tt('dve', u0.t[:, 0:256], u0.t[:, 0:256], prod0.t[:, 512:768], ALU.add, [prod0.b, u0.b], [u0.b])
            P.tt('dve', u0.t[:, 256:260], e0.t[:, 0:4], e0.t[:, 4:8], ALU.add, [e0.b], [u0.b])
            P.tt('dve', u0.t[:, 256:260], u0.t[:, 256:260], e0.t[:, 8:12], ALU.add, [e0.b, u0.b], [u0.b])
            kvt = [C.sb(ph, [128, 516], F32, "kvt%d" % i) for i in range(2)]
            for i in range(2):
                P.memset('dve', kvt[i].t[:, 512:516], 1.0, [kvt[i].b])
            prod = C.sb(ph, [128, 256], F32, "dprod")
            sc = C.sb(ph, [128, 8], F32, "dsc")
            mk = C.sb(ph, [4, NS, 260], F32, "mk")
            nk = 0
            for b in range(NS):
                pq, pqb = C.psum(2)
                P.mm([(pq[:, 0, :], sel4.t[:, b, :], qkvd.t[:, 0:512], True, True)], [sel4.b, qkvd.b], pqb[0:1])
                P.mm([(pq[:, 1, 0:256], sel4.t[:, b, :], qkvd.t[:, 512:768], True, True)], [sel4.b, qkvd.b], pqb[1:2])
                pu, pub = C.psum(1)
                for g in range(3):
                    kt = kvt[nk % 2]
                    nk += 1
                    P.dma('sp', [(kt.t[:, 0:512], kvc[g][l, b, 0:WINS[g]:DILS[g], :])], [], [kt.b])
                    qv = pq[:, g // 2, (g % 2) * 256:(g % 2) * 256 + 256]
                    P.tt('dve', prod.t[:], kt.t[:, 0:256], qv, ALU.mult, [kt.b] + pqb[g // 2:g // 2 + 1], [prod.b])
                    P.op('dve', lambda e: e.tensor_reduce(out=sc.t[:, 0:4], in_=prod.t[:].rearrange("p (h d) -> p h d", d=64),
                                                          axis=AX.X, op=ALU.add), [prod.b], [sc.b])
                    P.stt(sc.t[:, 0:4], sc.t[:, 0:4], 0.125, biasd.t[:, g * 4:(g + 1) * 4], ALU.mult, ALU.add,
                          [sc.b, biasd.b], [sc.b])
                    P.act(sc.t[:, 4:8], sc.t[:, 0:4], AF.Exp, [sc.b], [sc.b])
                    P.mm([(pu[0:4, 0, 0:260], sc.t[:, 4:8], kt.t[:, 256:516], g == 0, g == 2)], [sc.b, kt.b], pub)
                P.tt('dve', mk.t[:, b, :], pu[0:4, 0, 0:260], dmask.t[:], ALU.mult, pub + [dmask.b], [mk.b])
            po, pob = C.psum(1)
            P.mm([(po[0:NS, 0, 0:260], colsel.t[:, b, :], mk.t[:, b, :], b == 0, b == NS - 1) for b in range(NS)],
                 [colsel.b, mk.b], pob)
            P.tt('dve', u0.t[:], u0.t[:], po[0:NS, 0, 0:260], ALU.add, [u0.b] + pob, [u0.b])
            P.op('dve', lambda e: e.reciprocal(out=u0.t[:, 256:260], in_=u0.t[:, 256:260]), [u0.b], [u0.b])
            att = C.sb(ph, [NS, 256], F32, "datt")
            P.tt('dve', att.t[:].rearrange("p (h d) -> p h d", d=64), u0.t[:, 0:256].rearrange("p (h d) -> p h d", d=64),
                 u0.t[:, 256:260].unsqueeze(2).to_broadcast([NS, 4, 64]), ALU.mult, [u0.b], [att.b])
            pv, pb = C.psum(1)
            P.tr([(pv[:, 0, cc * NS:(cc + 1) * NS], att.t[:, cc * 128:(cc + 1) * 128], ident.t[0:NS, 0:NS]) for cc in range(2)],
                 [att.b, ident.b], pb)
            P.copy('act', adT.t[:].rearrange("p c b -> p (c b)"), pv[:, 0, 0:2 * NS], pb, [adT.b])
        P.barrier()

    def dec_out(ph, l, woa, wos, wo, scr, t1s):
        gd = scr.t[0:NS, 0:2, :].rearrange("p a d -> p (a d)")
        g1d = scr.t[0:NS, 2, :]
        md = scr.t[0:NS, 3, :]
        P.dma('sp', [(scr.t[0:NS, 0:2, :], dproj_d[:, C_G:C_G + DIN].rearrange("p (a d) -> p a d", a=2)),
                     (g1d, grows_d[l, 1:1 + NS, 0:D])], [], [scr.b])
        P.act(gd, gd, AF.Sigmoid, [scr.b], [scr.b])
        for n in range(2):
            ns = slice(n * 512, (n + 1) * 512)
            pa, pab = C.psum(1)
            P.mm([(pa[0:NS, 0, :], adT.t[:, kc, :], woa.t[:, kc, ns], kc == 0, kc == 1) for kc in range(2)], [adT.b, woa.b], pab)
            P.tt('dve', t1s[0].t[0:NS, :], pa[0:NS, 0, :], gd[:, n * 512:(n + 1) * 512], ALU.mult, pab + [scr.b], [t1s[0].b])
            ps_, psb_ = C.psum(1)
            P.mm([(ps_[0:NS, 0, :], sdT.t[:, kc, :], wos.t[:, kc, ns], kc == 0, kc == 15) for kc in range(16)], [sdT.b, wos.b], psb_)
            P.tt('dve', t1s[1].t[0:NS, :], ps_[0:NS, 0, :], gd[:, D + n * 512:D + (n + 1) * 512], ALU.mult, psb_ + [scr.b],
                 [t1s[1].b])
            P.tt('dve', md[:, ns], t1s[0].t[0:NS, :], t1s[1].t[0:NS, :], ALU.add, [t1s[0].b, t1s[1].b], [scr.b])
        mT = C.sb(ph, [128, 8, NS], BF, "dmT")
        pv, pb = C.psum(1)
        P.tr([(pv[:, 0, kc * NS:(kc + 1) * NS], md[:, kc * 128:(kc + 1) * 128], ident.t[0:NS, 0:NS]) for kc in range(8)],
             [scr.b, ident.b], pb)
        P.copy('act', mT.t[:].rearrange("p k b -> p (k b)"), pv[:, 0, 0:8 * NS], pb, [mT.b])
        for n in range(2):
            ns = slice(n * 512, (n + 1) * 512)
            pv, pb = C.psum(1)
            P.mm([(pv[0:NS, 0, :], mT.t[:, kc, :], wo.t[:, kc, ns], kc == 0, kc == 7) for kc in range(8)], [mT.b, wo.b], pb)
            P.tt('dve', t1s[0].t[0:NS, :], pv[0:NS, 0, :], g1d[:, ns], ALU.mult, pb + [scr.b], [t1s[0].b])
            P.tt('dve', xd.t[:, ns], xd.t[:, ns], t1s[0].t[0:NS, :], ALU.add, [xd.b, t1s[0].b], [xd.b])

    def dec_mlp(ph, l, wu, wd, scr, t1s, last):
        h2dT = C.sb(ph, [128, 8, NS], BF, "h2dT")
        hidT = C.sb(ph, [128, 32, NS], BF, "hidT")
        dec_norm_T(ph, xd.t[:], xd.b, A2, B2, h2dT, scr)
        g2d = scr.t[0:NS, 1, :]
        P.dma('sp', [(g2d, grows_d[l, 1:1 + NS, D:2 * D])], [], [scr.b])
        for n in range(8):
            pv, pb = C.psum(1)
            P.mm([(pv[0:NS, 0, :], h2dT.t[:, kc, :], wu.t[:, kc, n * 512:(n + 1) * 512], kc == 0, kc == 7) for kc in range(8)],
                 [h2dT.b, wu.b], pb)
            P.act(t1s[0].t[0:NS, :], pv[0:NS, 0, :], AF.Relu, pb, [t1s[0].b])
            P.tt('dve', t1s[1].t[0:NS, :], t1s[0].t[0:NS, :], pv[0:NS, 0, :], ALU.mult, pb + [t1s[0].b], [t1s[1].b])
            pt, ptb = C.psum(1)
            P.tr([(pt[:, 0, i * NS:(i + 1) * NS], t1s[1].t[0:NS, i * 128:(i + 1) * 128], ident.t[0:NS, 0:NS]) for i in range(4)],
                 [t1s[1].b, ident.b], ptb)
            P.copy('act', hidT.t[:, n * 4:(n + 1) * 4, :].rearrange("p c b -> p (c b)"), pt[:, 0, 0:4 * NS], ptb, [hidT.b])
        for n in range(2):
            ns = slice(n * 512, (n + 1) * 512)
            pv, pb = C.psum(1)
            P.mm([(pv[0:NS, 0, :], hidT.t[:, hc, :], wd.t[:, hc, ns], hc == 0, hc == 31) for hc in range(32)], [hidT.b, wd.b], pb)
            P.tt('dve', t1s[0].t[0:NS, :], pv[0:NS, 0, :], g2d[:, ns], ALU.mult, pb + [scr.b], [t1s[0].b])
            P.tt('dve', xd.t[:, ns], xd.t[:, ns], t1s[0].t[0:NS, :], ALU.add, [xd.b, t1s[0].b], [xd.b])
        if last:
            ss = C.sb(ph, [NS, 2], F32, "fss")
            yo = scr.t[0:NS, 0, :]
            P.act(yo, xd.t[:], AF.Square, [xd.b], [scr.b, ss.b], accum_out=ss.t[:, 0:1])
            P.act(ss.t[:, 1:2], ss.t[:, 0:1], AF.Sqrt, [ss.b, epsT.b], [ss.b], scale=1.0 / D, bias=epsT.t[0:NS, 0:1])
            P.op('dve', lambda e: e.reciprocal(out=ss.t[:, 1:2], in_=ss.t[:, 1:2]), [ss.b], [ss.b])
            P.ts('dve', yo, xd.t[:], ss.t[:, 1:2], None, ALU.mult, None, [xd.b, ss.b], [scr.b])
            P.tt('dve', yo, yo, fg_bc.t[0:NS, :], ALU.mult, [scr.b, fg_bc.b], [scr.b])
            P.dma('pool', [(ys_out, yo)], [scr.b], [Buf()])

    for l in range(DEPTH):
        phase_params(l)
        if 'params_only' in stages:
            break
        if DECODE:
            phase_dec_norm1(l)
        phase_norm1(l)
        if 'norm1_only' in stages:
            break
        phase_inproj_T(l)
        if 'inT_only' in stages:
            break
        phase_inproj_tok(l)
        if 'intok_only' in stages:
            break
        phase_attn(l)
        if 'attn_only' in stages:
            break
        phase_ssd(l)
        if 'ssd_only' in stages:
            break
        if DECODE:
            phase_dec_mixers(l)
        phase_out(l)
        if 'out_only' in stages:
            break
        phase_mlp(l)
        if 'l0_only' in stages:
            break

    if 'params_only' in stages:
        for nm, tb, shp in (("dbg_A1", A1, [128, 40]), ("dbg_B1", B1, [128, 40]), ("dbg_A2", A2, [128, 40]),
                            ("dbg_B2", B2, [128, 40])):
            o_ = dout(nm, shp)
            P.dma('sp', [(o_, tb.t[:].rearrange("p a b -> p (a b)"))], [tb.b], [C.dbuf(nm)])
        for nm, tb, shp in (("dbg_g1bc", g1bc, [128, D]), ("dbg_g2bc", g2bc, [128, D]), ("dbg_cwT", cwT, [128, 128]),
                            ("dbg_colp", colp, [128, 96]), ("dbg_Abc", A_bc, [128, NH])):
            o_ = dout(nm, shp)
            src_ = tb.t[:].rearrange("p a b -> p (a b)") if len(tb.t.shape) == 3 else tb.t[:]
            P.dma('sp', [(o_, src_)], [tb.b], [C.dbuf(nm)])

    P.finalize(nc, es)


def prep_inputs(inputs):
    f = lambda a: np.ascontiguousarray(np.asarray(a, dtype=np.float32))
    consts = make_consts()
    shared = {}
    for k in ("rel_bias", "w_ada", "b_ada", "norm1_g", "norm2_g", "w_in", "conv_w", "conv_b", "dt_bias",
              "a_log", "d_skip", "ssd_norm_g", "w_o_attn", "w_o_ssd", "w_out", "w_up", "w_down", "final_g"):
        shared[k] = f(inputs[k])
    shared.update(consts)
    maps = []
    for core in range(8):
        b = core // 4
        s0 = core * NS
        m = dict(shared)
        m["x"] = f(inputs["x_prompt"][b][:L])
        m["c5"] = f(np.concatenate([inputs["c_prompt"][b:b + 1], inputs["c_sample"][s0:s0 + NS]], axis=0))
        m["xs"] = f(inputs["x_sample"][s0:s0 + NS, 0])
        caches = (inputs["cache_kv_g0"], inputs["cache_kv_g1"], inputs["cache_kv_g2"])
        for g in range(3):
            kv = np.asarray(caches[g])[:, s0:s0 + NS]
            m["kvc%d" % g] = f(kv.reshape(DEPTH, NS, WINS[g], 512))
        m["ssm_in"] = f(np.asarray(inputs["state_ssm"])[:, s0:s0 + NS].reshape(DEPTH, NS, NH * 64, 128))
        m["conv_in"] = f(np.asarray(inputs["state_conv"])[:, s0:s0 + NS])
        maps.append(m)
    return maps


_NC_CACHE = {}


def kernel(**inputs):
    maps = prep_inputs(inputs)
    if "nc" not in _NC_CACHE:
        _NC_CACHE["nc"] = build()
    nc = _NC_CACHE["nc"]
    res = run_bass_kernel_spmd(nc, maps, core_ids=list(range(8)))
    R = res.results
    B = 2
    y_prompt = np.stack([R[0]["y"], R[4]["y"]], axis=0)
    y_sample = np.concatenate([R[c]["ys"] for c in range(8)], axis=0).reshape(32, 1, D)
    outs = [y_prompt.astype(np.float32), y_sample.astype(np.float32)]
    for g in range(3):
        kvg = np.stack([R[0]["kvp%d" % g], R[4]["kvp%d" % g]], axis=1)
        outs.append(kvg.reshape(DEPTH, B, WINS[g], 2, 4, 64).astype(np.float32))
    outs.append(np.stack([R[0]["ssmp"], R[4]["ssmp"]], axis=1).reshape(DEPTH, B, NH, 64, 128).astype(np.float32))
    outs.append(np.stack([R[0]["convp"], R[4]["convp"]], axis=1).reshape(DEPTH, B, 3, CONV).astype(np.float32))
    for g in range(3):
        kvg = np.concatenate([R[c]["kvs%d" % g] for c in range(8)], axis=1)
        outs.append(kvg.reshape(DEPTH, 32, WINS[g], 2, 4, 64).astype(np.float32))
    outs.append(np.concatenate([R[c]["ssms"] for c in range(8)], axis=1).reshape(DEPTH, 32, NH, 64, 128)
                .astype(np.float32))
    outs.append(np.concatenate([R[c]["convs"] for c in range(8)], axis=1).reshape(DEPTH, 32, 3, CONV)
                .astype(np.float32))
    return tuple(outs)
```

```python
import math
import os as _os_l
from contextlib import ExitStack

import numpy as np
import concourse.bass as bass
import concourse.mybir as mybir
from concourse.bass_utils import run_bass_kernel_spmd

F32 = mybir.dt.float32
BF = mybir.dt.bfloat16
AF = mybir.ActivationFunctionType
ALU = mybir.AluOpType
AX = mybir.AxisListType

D = 1024
L = int(_os_l.environ.get("KL", "8192"))
NB = L // 128
DEPTH = 2
NS = 4
NIN = 10528
QKV = 768
DIN = 2048
CONV = 4096
NH = 32
DFF = 4096
EPS = 1e-6
DILS = (1, 4, 16)
WINS = (128, 512, 2048)
C_Q, C_K, C_V, C_Z, C_X, C_DT, C_G = 0, 768, 1536, 2304, 4352, 8448, 8480
NEG8 = -240000.0

COMPUTE = {'pe': 0, 'act': 1, 'dve': 2, 'pool': 3}
DMA_ENGS = ['sp', 'pool', 'act']
NSLOT = 8
NSTREAM = 4 + len(DMA_ENGS) * NSLOT
SAME_ENGINE_SYNC = True
import os as _os
KVENG = _os.environ.get('KVENG', 'dve')
DVE_COPY_TS = _os.environ.get('DVE_COPY_TS', '0') == '1'
SAME_ENGINE_SYNC = _os.environ.get('SES', '1') == '1'
SKIP = set(_os.environ.get('KSKIP', '').split(','))


class Buf:
    __slots__ = ('lw', 'rd', 'name', 'excl')

    def __init__(self, name='', excl=False):
        self.lw = None
        self.rd = {}
        self.name = name
        self.excl = excl


class Op:
    __slots__ = ('eng', 'fn', 'sid', 'ord', 'ndma', 'ms', 'cnt', 'waits', 'deps', 'gi')


class Prog:
    def __init__(self):
        self.ops = []
        self.stream_ops = [[] for _ in range(NSTREAM)]
        self.eng_ops = {e: [] for e in ['pe', 'act', 'dve', 'pool', 'sp']}
        self.dma_count = {e: 0 for e in DMA_ENGS}
        self.barrier_deps = []

    def op(self, eng, fn, reads=(), writes=(), ndma=0):
        o = Op()
        o.eng, o.fn, o.ndma, o.ms, o.cnt, o.waits = eng, fn, ndma, False, 0, None
        deps = {}

        def add(d):
            if d is None:
                return
            cur = deps.get(d.sid)
            if cur is None or cur.ord < d.ord:
                deps[d.sid] = d

        if any(b.excl for b in reads):
            writes = list(writes) + [b for b in reads if b.excl and b not in writes]
            reads = [b for b in reads if not b.excl]
        for b in reads:
            add(b.lw)
        for b in writes:
            add(b.lw)
            for r in b.rd.values():
                add(r)
        for d in self.barrier_deps:
            add(d)
        if ndma:
            j = self.dma_count[eng]
            self.dma_count[eng] = j + 1
            o.sid = 4 + DMA_ENGS.index(eng) * NSLOT + (j % NSLOT)
            prev = self.stream_ops[o.sid]
            if prev:
                add(prev[-1])
        elif eng in COMPUTE:
            o.sid = COMPUTE[eng]
        else:
            o.sid = -1
        if o.sid >= 0:
            so = self.stream_ops[o.sid]
            o.ord = len(so) + 1
            so.append(o)
        else:
            o.ord = 0
        o.deps = list(deps.values())
        if o.sid >= 0:
            for b in reads:
                b.rd[o.sid] = o
            for b in writes:
                b.lw = o
                b.rd = {}
        o.gi = len(self.ops)
        self.ops.append(o)
        self.eng_ops[eng].append(o)
        return o

    def barrier(self):
        self.barrier_deps = [s[-1] for s in self.stream_ops if s]

    def finalize(self, nc, es):
        self.barrier()
        self.op('sp', None)
        n = len(self.ops)
        clocks = np.zeros((n, NSTREAM), np.int32)
        engclk = {e: np.zeros(NSTREAM, np.int32) for e in self.eng_ops}
        for o in self.ops:
            clk = engclk[o.eng]
            waits = []
            own = COMPUTE.get(o.eng, -2)
            for d in o.deps:
                if clk[d.sid] >= d.ord:
                    continue
                if d.sid == own and (o.eng == 'pe' or not SAME_ENGINE_SYNC):
                    continue
                waits.append(d)
                d.ms = True
                np.maximum(clk, clocks[d.gi], out=clk)
                if clk[d.sid] < d.ord:
                    clk[d.sid] = d.ord
            clocks[o.gi] = clk
            o.waits = waits
        for sid, so in enumerate(self.stream_ops):
            c = 0
            for o in so:
                if sid < 4:
                    if o.ms:
                        c += 1
                else:
                    c += 16 * o.ndma
                o.cnt = c
        sems = [es.enter_context(nc.semaphore("s%d" % i)) for i in range(NSTREAM)]
        eng_ops = self.eng_ops

        def emit(name, h):
            for o in eng_ops[name]:
                for d in o.waits:
                    h.wait_ge(sems[d.sid], d.cnt)
                if o.fn is None:
                    continue
                r = o.fn(h)
                if o.ndma:
                    rl = r if isinstance(r, (list, tuple)) else [r]
                    assert len(rl) == o.ndma, (len(rl), o.ndma)
                    for ins in rl:
                        ins.then_inc(sems[o.sid], 16)
                elif o.ms:
                    r.then_inc(sems[o.sid], 1)

        with nc.Block() as block:
            @block.tensor
            def _(e):
                emit('pe', e)

            @block.scalar
            def _(e):
                emit('act', e)

            @block.vector
            def _(e):
                emit('dve', e)

            @block.gpsimd
            def _(e):
                emit('pool', e)

            @block.sync
            def _(e):
                emit('sp', e)

    def mm(self, items, reads, writes):
        items = list(items)

        def fn(e):
            ins = None
            for (o, l, r, st, sp) in items:
                ins = e.matmul(o, l, r, start=st, stop=sp)
            return ins
        return self.op('pe', fn, reads, writes)

    def tr(self, items, reads, writes):
        items = list(items)

        def fn(e):
            ins = None
            for (o, i, idn) in items:
                ins = e.transpose(o, i, idn)
            return ins
        return self.op('pe', fn, reads, writes)

    def act(self, out, in_, func, reads, writes, **kw):
        return self.op('act', lambda e: e.activation(out=out, in_=in_, func=func, **kw), reads, writes)

    def tt(self, eng, out, in0, in1, op, reads, writes):
        return self.op(eng, lambda e: e.tensor_tensor(out=out, in0=in0, in1=in1, op=op), reads, writes)

    def ts(self, eng, out, in0, s1, s2, op0, op1, reads, writes):
        if s2 is None:
            return self.op(eng, lambda e: e.tensor_scalar(out=out, in0=in0, scalar1=s1, scalar2=None, op0=op0),
                           reads, writes)
        return self.op(eng, lambda e: e.tensor_scalar(out=out, in0=in0, scalar1=s1, scalar2=s2, op0=op0, op1=op1),
                       reads, writes)

    def stt(self, out, in0, scalar, in1, op0, op1, reads, writes):
        return self.op('dve', lambda e: e.scalar_tensor_tensor(out=out, in0=in0, scalar=scalar, in1=in1,
                                                               op0=op0, op1=op1), reads, writes)

    def copy(self, eng, out, in_, reads, writes):
        if eng == 'act':
            return self.op('act', lambda e: e.activation(out=out, in_=in_, func=AF.Copy), reads, writes)
        if eng == 'dve' and DVE_COPY_TS:
            return self.op(eng, lambda e: e.tensor_scalar(out=out, in0=in_, scalar1=1.0, scalar2=None, op0=ALU.mult),
                           reads, writes)
        return self.op(eng, lambda e: e.tensor_copy(out=out, in_=in_), reads, writes)

    def memset(self, eng, ap, val, writes):
        return self.op(eng, lambda e: e.memset(ap, val), (), writes)

    def dma(self, eng, pairs, reads, writes, slow=False):
        pairs = list(pairs)

        def fn(e):
            if slow:
                return [e.dma_start(out=o, in_=i, allow_slow_non_contiguous=True) for (o, i) in pairs]
            return [e.dma_start(out=o, in_=i) for (o, i) in pairs]
        return self.op(eng, fn, reads, writes, ndma=len(pairs))


class TB:
    __slots__ = ('t', 'b')

    def __init__(self, t, name=''):
        self.t = t
        self.b = Buf(name)


class Ctx:
    def __init__(self, nc, es):
        self.nc = nc
        self.es = es
        self.P = Prog()
        self.n = 0
        ps = es.enter_context(nc.psum_tensor("psum_all", [128, 8, 512], F32))
        self.ps = ps
        self.psb = [Buf("psb%d" % i, excl=True) for i in range(8)]
        self.psp = 0
        self.dbufs = {}

    def sb(self, stack, shape, dtype, name=None):
        self.n += 1
        nm = "%s_%d" % (name or "t", self.n)
        t = stack.enter_context(self.nc.sbuf_tensor(nm, list(shape), dtype))
        return TB(t, nm)

    def psum(self, nbanks):
        p = self.psp
        if p % nbanks:
            p += nbanks - (p % nbanks)
        if p + nbanks > 8:
            p = 0
        self.psp = (p + nbanks) % 8
        return self.ps[:, p:p + nbanks, :], self.psb[p:p + nbanks]

    def dbuf(self, key):
        b = self.dbufs.get(key)
        if b is None:
            b = Buf(str(key))
            self.dbufs[key] = b
        return b


def t5_bucket_np(dist):
    dist = np.asarray(dist, np.int32)
    max_exact = 16
    df = np.maximum(dist, 1).astype(np.float32)
    large = max_exact + (np.log(df / np.float32(max_exact)) / np.float32(math.log(2048 / max_exact))
                         * np.float32(32 - max_exact)).astype(np.int32)
    large = np.minimum(large, 31)
    return np.where(dist < max_exact, dist, large)


def make_consts():
    c = {}
    c['c_ident'] = np.eye(128, dtype=np.float32)
    c['c_anti'] = np.ascontiguousarray(np.eye(128, dtype=np.float32)[::-1])
    k = np.arange(128)
    c['c_triu'] = (k[:, None] <= k[None, :]).astype(np.float32)
    c['c_su'] = (k[:, None] > k[None, :]).astype(np.float32)
    sel = np.zeros((3, 33, 384), np.float32)
    for g, dil in enumerate(DILS):
        bk = t5_bucket_np(np.arange(129) * dil)
        sel[g, 32, :] = NEG8
        for j in range(129):
            sel[g, bk[j], j + 127] = 8.0
            sel[g, 32, j + 127] = 0.0
    c['c_sel'] = sel
    seld = np.zeros((3, 33, 128), np.float32)
    for g, dil in enumerate(DILS):
        bk = t5_bucket_np(np.arange(129) * dil)
        for p in range(128):
            seld[g, bk[128 - p], p] = 1.0
    c['c_seld'] = seld
    sel4 = np.zeros((NS, NS, 128), np.float32)
    colsel = np.zeros((NS, 4, NS), np.float32)
    for b in range(NS):
        sel4[b, b, :] = 1.0
        colsel[b, :, b] = 1.0
    c['c_sel4'] = sel4
    c['c_colsel'] = colsel
    dmask = np.zeros((4, 260), np.float32)
    for h in range(4):
        dmask[h, h * 64:(h + 1) * 64] = 1.0
        dmask[h, 256 + h] = 1.0
    c['c_dmask'] = dmask
    e5 = np.zeros((5, 5, 128), np.float32)
    for b in range(5):
        e5[b, b, :] = 1.0
    c['c_e5'] = e5
    return c


def build(stages=('all',), dbg=()):
    nc = bass.Bass("TRN2", target_bir_lowering=False)
    es = ExitStack()
    with es:
        _build(nc, es, stages, dbg)
    return nc


def _build(nc, es, stages, dbg):
    C = Ctx(nc, es)
    P = C.P
    allst = 'all' in stages
    DECODE = 'nodec' not in stages

    def din(name, shape, dt=F32):
        return nc.dram_tensor(name, list(shape), dt, kind="ExternalInput").ap()

    def dout(name, shape, dt=F32):
        return nc.dram_tensor(name, list(shape), dt, kind="ExternalOutput").ap()

    def dscr(name, shape, dt):
        kind = "ExternalOutput" if name in dbg else "Internal"
        return nc.dram_tensor(name, list(shape), dt, kind=kind).ap()

    x_in = din("x", [L, D])
    c5_in = din("c5", [5, D])
    xs_in = din("xs", [NS, D])
    kvc = [din("kvc%d" % g, [DEPTH, NS, WINS[g], 512]) for g in range(3)]
    ssm_in = din("ssm_in", [DEPTH, NS, NH * 64, 128])
    conv_in = din("conv_in", [DEPTH, NS, 3, CONV])
    rel_bias = din("rel_bias", [32, 12])
    w_ada = din("w_ada", [DEPTH, D, 6 * D])
    b_ada = din("b_ada", [DEPTH, 6 * D])
    norm1_g = din("norm1_g", [DEPTH, D])
    norm2_g = din("norm2_g", [DEPTH, D])
    w_in = din("w_in", [DEPTH, D, NIN])
    conv_w = din("conv_w", [DEPTH, 4, CONV])
    conv_b = din("conv_b", [DEPTH, CONV])
    dt_bias = din("dt_bias", [DEPTH, NH])
    a_log = din("a_log", [DEPTH, NH])
    d_skip = din("d_skip", [DEPTH, NH])
    ssd_norm_g = din("ssd_norm_g", [DEPTH, DIN])
    w_o_attn = din("w_o_attn", [DEPTH, 256, D])
    w_o_ssd = din("w_o_ssd", [DEPTH, DIN, D])
    w_out = din("w_out", [DEPTH, D, D])
    w_up = din("w_up", [DEPTH, D, DFF])
    w_down = din("w_down", [DEPTH, DFF, D])
    final_g = din("final_g", [D])
    c_ident = din("c_ident", [128, 128])
    c_anti = din("c_anti", [128, 128])
    c_triu = din("c_triu", [128, 128])
    c_su = din("c_su", [128, 128])
    c_sel = din("c_sel", [3, 33, 384])
    c_seld = din("c_seld", [3, 33, 128])
    c_e5 = din("c_e5", [5, 5, 128])
    c_sel4 = din("c_sel4", [NS, NS, 128])
    c_colsel = din("c_colsel", [NS, 4, NS])
    c_dmask = din("c_dmask", [4, 260])

    y_out = dout("y", [L, D])
    ys_out = dout("ys", [NS, D])
    kvp = [dout("kvp%d" % g, [DEPTH, WINS[g], 512]) for g in range(3)]
    ssmp = dout("ssmp", [DEPTH, NH * 64, 128])
    convp = dout("convp", [DEPTH, 3, CONV])
    kvs = [dout("kvs%d" % g, [DEPTH, NS, WINS[g], 512]) for g in range(3)]
    ssms = dout("ssms", [DEPTH, NS, NH * 64, 128])
    convs = dout("convs", [DEPTH, NS, 3, CONV])

    hT_d = dscr("hT_d", [8, 128, L], BF)
    qT_d = dscr("qT_d", [QKV, L], BF)
    kT_d = dscr("kT_d", [QKV, L], BF)
    xbcT_d = dscr("xbcT_d", [CONV, L], BF)
    gT_d = dscr("gT_d", [DIN, L], BF)
    v_d = dscr("v_d", [L, QKV], BF)
    zs_d = dscr("zs_d", [L, DIN], BF)
    dta_d = dscr("dta_d", [L, 64], F32)
    U_d = [dscr("U_d%d" % g, [L, 260], F32) for g in range(3)]
    sT_d = dscr("sT_d", [DIN, L], BF)
    x1_d = dscr("x1_d", [L, D], F32)
    xm_d = dscr("xm_d", [L, D], F32)
    vec_d = dscr("vec_d", [12, 384], F32)
    grows_d = dscr("grows_d", [DEPTH, 5, 2 * D], F32)
    dproj_d = dscr("dproj_d", [NS, NIN], F32)
    hidT_d = dscr("hidT_d", [DFF, L], BF)

    ident = C.sb(es, [128, 128], F32, "ident")
    identb = C.sb(es, [128, 128], BF, "identb")
    antib = C.sb(es, [128, 128], BF, "antib")
    triu = C.sb(es, [128, 128], F32, "triu")
    su = C.sb(es, [128, 128], F32, "su")
    ones = C.sb(es, [128, 128], F32, "ones")
    epsT = C.sb(es, [128, 1], F32, "eps")
    cT = C.sb(es, [128, 8, 5], F32, "cT")
    hank = [C.sb(es, [128, 4, 2, 128], BF, "hank%d" % g) for g in range(3)]
    A1 = C.sb(es, [128, 8, 5], F32, "A1")
    B1 = C.sb(es, [128, 8, 5], F32, "B1")
    A2 = C.sb(es, [128, 8, 5], F32, "A2")
    B2 = C.sb(es, [128, 8, 5], F32, "B2")
    g1bc = C.sb(es, [128, D], F32, "g1bc")
    g2bc = C.sb(es, [128, D], F32, "g2bc")
    cwT = C.sb(es, [128, 4, 32], F32, "cwT")
    colp = C.sb(es, [128, 96], F32, "colp")
    dtb_bc = C.sb(es, [128, NH], F32, "dtb")
    A_bc = C.sb(es, [128, NH], F32, "Abc")
    D_bc = C.sb(es, [128, NH], F32, "Dbc")
    fg_bc = C.sb(es, [128, D], F32, "fg")
    xd = C.sb(es, [NS, D], F32, "xd")
    hdT = C.sb(es, [128, 8, NS], BF, "hdT")
    adT = C.sb(es, [128, 2, NS], BF, "adT")
    sdT = C.sb(es, [128, 16, NS], BF, "sdT")
    biasd = C.sb(es, [128, 12], F32, "biasd")
    sel4 = C.sb(es, [NS, NS, 128], F32, "sel4")
    colsel = C.sb(es, [4, NS, NS], F32, "colsel")
    dmask = C.sb(es, [4, 260], F32, "dmask")

    P.dma('sp', [(ident.t[:], c_ident), (triu.t[:], c_triu), (su.t[:], c_su)], [], [ident.b, triu.b, su.b])
    P.memset('dve', ones.t[:], 1.0, [ones.b])
    P.memset('dve', epsT.t[:], EPS, [epsT.b])
    P.copy('dve', identb.t[:], ident.t[:], [ident.b], [identb.b])
    with ExitStack() as ph:
        tmp = C.sb(ph, [128, 128], F32, "tmp")
        P.dma('sp', [(tmp.t[:], c_anti)], [], [tmp.b])
        P.copy('dve', antib.t[:], tmp.t[:], [tmp.b], [antib.b])
        c5 = C.sb(ph, [5, D], F32, "c5")
        P.dma('sp', [(c5.t[:], c5_in)], [], [c5.b])
        P.act(c5.t[:], c5.t[:], AF.Silu, [c5.b], [c5.b])
        pv, pb = C.psum(1)
        P.tr([(pv[:, 0, kc * 5:(kc + 1) * 5], c5.t[:, kc * 128:(kc + 1) * 128], ident.t[0:5, 0:5]) for kc in range(8)],
             [c5.b, ident.b], pb)
        P.copy('dve', cT.t[:].rearrange("p a b -> p (a b)"), pv[:, 0, 0:40], pb, [cT.b])
        rb33 = C.sb(ph, [33, 12], F32, "rb33")
        selt = C.sb(ph, [33, 3, 384], F32, "selt")
        P.memset('dve', rb33.t[32:33, :], 1.0, [rb33.b])
        P.dma('sp', [(rb33.t[0:32, :], rel_bias), (selt.t[:], c_sel.rearrange("g r m -> r g m"))], [],
              [rb33.b, selt.b])
        seldt = C.sb(ph, [32, 3, 128], F32, "seldt")
        P.dma('sp', [(seldt.t[:], c_seld[:, 0:32, :].rearrange("g r m -> r g m")),
                     (sel4.t[:], c_sel4.rearrange("b k m -> k b m")),
                     (colsel.t[:], c_colsel.rearrange("b k m -> k b m")),
                     (dmask.t[:], c_dmask), (xd.t[:], xs_in)], [], [seldt.b, sel4.b, colsel.b, dmask.b, xd.b])
        pv, pb = C.psum(1)
        P.mm([(pv[:, 0, g * 4:(g + 1) * 4], seldt.t[:, g, :], rb33.t[0:32, g * 4:(g + 1) * 4], True, True) for g in range(3)],
             [seldt.b, rb33.b], pb)
        P.copy('act', biasd.t[:], pv[:, 0, 0:12], pb, [biasd.b])
        vecs = C.sb(ph, [4, 3, 384], F32, "vecs")
        for g in range(3):
            pv, pb = C.psum(1)
            P.mm([(pv[0:4, 0, 0:384], rb33.t[:, g * 4:(g + 1) * 4], selt.t[:, g, :], True, True)],
                 [rb33.b, selt.b], pb)
            P.copy('dve', vecs.t[:, g, :], pv[0:4, 0, 0:384], pb, [vecs.b])
        vb = C.dbuf('vec')
        P.dma('sp', [(vec_d.rearrange("(g h) m -> h g m", g=3), vecs.t[:])], [vecs.b], [vb])
        hk = C.sb(ph, [128, 24, 128], F32, "hk")
        pairs = []
        for g in range(3):
            for h in range(4):
                for kb in range(2):
                    src = bass.AP(vec_d.tensor, (g * 4 + h) * 384 + kb * 128, [[1, 128], [1, 128]])
                    pairs.append((hk.t[:, (g * 4 + h) * 2 + kb, :], src))
        for i in range(0, 24, 8):
            P.dma('sp', pairs[i:i + 8], [vb], [hk.b])
        for g in range(3):
            P.copy('dve', hank[g].t[:].rearrange("p h k q -> p (h k) q"), hk.t[:, g * 8:(g + 1) * 8, :],
                   [hk.b], [hank[g].b])
    P.barrier()

    if 'setup_only' in stages:
        dbg_o = dout("dbg_cT", [128, 40])
        P.dma('sp', [(dbg_o, cT.t[:].rearrange("p a b -> p (a b)"))], [cT.b], [C.dbuf('dbg')])
        dbg_h = dout("dbg_hank", [128, 3, 1024], BF)
        for g in range(3):
            P.dma('sp', [(dbg_h[:, g, :], hank[g].t[:].rearrange("p h k q -> p (h k q)"))], [hank[g].b],
                  [C.dbuf('dbg')])
        P.finalize(nc, es)
        return


    wbytes = lambda: None

    def phase_params(l):
        with ExitStack() as ph:
            rows = C.sb(ph, [128, 128], F32, "rows")
            rows2 = C.sb(ph, [128, 128], F32, "rows2")
            P.memset('dve', rows.t[:], 0.0, [rows.b])
            P.dma('sp', [(rows.t[0:8, :], norm1_g[l].rearrange("(a p) -> a p", p=128)),
                         (rows.t[8:16, :], norm2_g[l].rearrange("(a p) -> a p", p=128)),
                         (rows.t[16:64, :], b_ada[l].rearrange("(a p) -> a p", p=128)),
                         (rows.t[64:96, :], conv_b[l].rearrange("(a p) -> a p", p=128)),
                         (rows2.t[:], conv_w[l].rearrange("j (a p) -> (j a) p", p=128))], [], [rows.b, rows2.b])
            pv, pb = C.psum(1)
            P.tr([(pv[:, 0, 0:128], rows.t[:], ident.t[:]), (pv[:, 0, 128:256], rows2.t[:], ident.t[:])],
                 [rows.b, rows2.b, ident.b], pb)
            P.copy('dve', colp.t[:], pv[:, 0, 0:96], pb, [colp.b])
            P.copy('dve', cwT.t[:].rearrange("p j a -> p (j a)"), pv[:, 0, 128:256], pb, [cwT.b])
            P.dma('sp', [(dtb_bc.t[:], dt_bias[l:l + 1, :].to_broadcast([128, NH])),
                         (A_bc.t[:], a_log[l:l + 1, :].to_broadcast([128, NH])),
                         (D_bc.t[:], d_skip[l:l + 1, :].to_broadcast([128, NH])),
                         (fg_bc.t[:], final_g.rearrange("(o d) -> o d", o=1).to_broadcast([128, D]))], [],
                  [dtb_bc.b, A_bc.b, D_bc.b, fg_bc.b])
            P.act(A_bc.t[:], A_bc.t[:], AF.Exp, [A_bc.b], [A_bc.b])
            P.ts('dve', A_bc.t[:], A_bc.t[:], -1.0, None, ALU.mult, None, [A_bc.b], [A_bc.b])
            modT = C.sb(ph, [128, 48, 5], F32, "modT")
            grows = C.sb(ph, [5, 2 * D], F32, "grows")
            wt = [C.sb(ph, [128, 8, 512], F32, "wada%d" % i) for i in range(2)]
            pvm, pbm = C.psum(1)
            for n in range(12):
                w = wt[n % 2]
                P.dma('sp', [(w.t[:, 0:4, :], w_ada[l, 0:512, n * 512:(n + 1) * 512].rearrange("(a p) n -> p a n", p=128)),
                             (w.t[:, 4:8, :], w_ada[l, 512:1024, n * 512:(n + 1) * 512].rearrange("(a p) n -> p a n", p=128))],
                      [], [w.b])
                for fc in range(4):
                    f = n * 4 + fc
                    P.mm([(pvm[:, 0, f * 5:(f + 1) * 5], w.t[:, kc, fc * 128:(fc + 1) * 128], cT.t[:, kc, :],
                           kc == 0, kc == 7) for kc in range(8)], [w.b, cT.b], pbm)
            P.tt('dve', modT.t[:], pvm[:, 0, 0:240].rearrange("p (a b) -> p a b", b=5),
                 colp.t[:, 16:64].unsqueeze(2).to_broadcast([128, 48, 5]), ALU.add, pbm + [colp.b], [modT.b])
            for (A_, B_, goff, sc, sh) in ((A1, B1, 0, 8, 0), (A2, B2, 8, 32, 24)):
                P.ts('dve', A_.t[:], modT.t[:, sc:sc + 8, :], 1.0, None, ALU.add, None, [modT.b], [A_.b])
                P.tt('dve', A_.t[:], A_.t[:], colp.t[:, goff:goff + 8].unsqueeze(2).to_broadcast([128, 8, 5]),
                     ALU.mult, [A_.b, colp.b], [A_.b])
                P.copy('dve', B_.t[:], modT.t[:, sh:sh + 8, :], [modT.b], [B_.b])
            for half, base in ((0, 16), (1, 40)):
                pv, pb = C.psum(2)
                P.tr([(pv[0:5, kc // 4, (kc % 4) * 128:(kc % 4 + 1) * 128], modT.t[:, base + kc, :], ident.t[:])
                      for kc in range(8)], [modT.b, ident.b], pb)
                P.copy('dve', grows.t[:, half * D:(half + 1) * D].rearrange("p (a b) -> p a b", b=512),
                       pv[0:5, :, :], pb, [grows.b])
            P.dma('sp', [(grows_d[l], grows.t[:])], [grows.b], [C.dbuf(('grows', l))])
            e0 = C.sb(ph, [5, 128], F32, "e0")
            P.dma('sp', [(e0.t[:], c_e5[0])], [], [e0.b])
            for half, gb in ((0, g1bc), (1, g2bc)):
                pv, pb = C.psum(2)
                for j in range(2):
                    P.mm([(pv[:, j, :], e0.t[:], grows.t[:, half * D + j * 512: half * D + (j + 1) * 512], True, True)],
                         [e0.b, grows.b], pb[j:j + 1])
                P.copy('dve', gb.t[:].rearrange("p (a b) -> p a b", b=512), pv, pb, [gb.b])
        P.barrier()

    def rms_rstd(ph_tiles, xt, nblk, junk, ss, rstd):
        for j in range(nblk):
            P.act(junk.t[:], xt.t[:, j, :], AF.Square, [xt.b], [junk.b, ss.b], accum_out=ss.t[:, j:j + 1])
        P.act(rstd.t[:, 0:nblk], ss.t[:, 0:nblk], AF.Sqrt, [ss.b, epsT.b], [rstd.b], scale=1.0 / D, bias=epsT.t[:, 0:1])
        P.op('dve', lambda e: e.reciprocal(out=rstd.t[:, 0:nblk], in_=rstd.t[:, 0:nblk]), [rstd.b], [rstd.b])

    def norm_to_hT(xt, j, rstd, A_, B_, hT, col0, flip):
        P.act(xt.t[:, j, :], xt.t[:, j, :], AF.Copy, [xt.b, rstd.b], [xt.b], scale=rstd.t[:, j:j + 1])
        pv, pb = C.psum(2)
        P.tr([(pv[:, kc // 4, (kc % 4) * 128:(kc % 4 + 1) * 128], xt.t[:, j, kc * 128:(kc + 1) * 128], ident.t[:])
              for kc in range(8)], [xt.b, ident.b], pb)
        for kc in range(8):
            src = pv[:, kc // 4, (kc % 4) * 128:(kc % 4 + 1) * 128]
            dst = hT.t[:, kc, col0:col0 + 128]
            if (kc + flip) % 2 == 0:
                P.act(dst, src, AF.Identity, pb + [A_.b, B_.b], [hT.b], scale=A_.t[:, kc, 0:1], bias=B_.t[:, kc, 0:1])
            else:
                P.ts('dve', dst, src, A_.t[:, kc, 0:1], B_.t[:, kc, 0:1], ALU.mult, ALU.add, pb + [A_.b, B_.b], [hT.b])

    def phase_norm1(l):
        xsrc = x_in if l == 0 else xm_d
        with ExitStack() as ph:
            xts = [C.sb(ph, [128, 4, D], F32, "xt%d" % i) for i in range(2)]
            hTs = [C.sb(ph, [128, 8, 512], BF, "hTo%d" % i) for i in range(2)]
            junk = C.sb(ph, [128, D], F32, "junk")
            ss = C.sb(ph, [128, 4], F32, "ss")
            rstd = C.sb(ph, [128, 4], F32, "rstd")
            for tt in range(L // 512):
                xt, hT = xts[tt % 2], hTs[tt % 2]
                rd = [C.dbuf(('xm', tt * 4 + j)) for j in range(4)] if l > 0 else []
                P.dma('sp', [(xt.t[:], xsrc[tt * 512:(tt + 1) * 512, :].rearrange("(j p) d -> p j d", p=128))], rd, [xt.b])
                rms_rstd(ph, xt, 4, junk, ss, rstd)
                for j in range(4):
                    norm_to_hT(xt, j, rstd, A1, B1, hT, j * 128, j)
                P.dma('pool', [(hT_d[:, :, tt * 512:(tt + 1) * 512].rearrange("k p t -> p k t"), hT.t[:])],
                      [hT.b], [C.dbuf(('hT', tt))])
        P.barrier()

    def load_w_bf16(ph, wdram_rows, ncols, wb, col_off, stg, cnt):
        pass

    def cast_load(dst_ap, dst_buf, src_ap, stg_list, state, ncols):
        i = state[0]
        state[0] += 1
        stg = stg_list[i % len(stg_list)]
        P.dma('sp', [(stg.t[:, 0:ncols], src_ap)], [], [stg.b])
        eng = ('dve', 'act', 'pool')[i % 3] if False else ('dve', 'act')[i % 2]
        P.copy(eng, dst_ap, stg.t[:, 0:ncols], [stg.b], [dst_buf])

    def phase_inproj_T(l):
        groups = [
            (C_Q, 1536, [('q', i) for i in range(6)] + [('k', i) for i in range(6)]),
            (C_X, 2048, [('x', i) for i in range(16)]),
            (C_X + 2048, 2048, [('x', 16 + i) for i in range(16)]),
            (C_G, 2048, [('g', i) for i in range(16)]),
        ]
        with ExitStack() as ph:
            wbs = [C.sb(ph, [128, 8, 2048], BF, "wT%d" % i) for i in range(2)]
            stg = [C.sb(ph, [128, 2048], F32, "stg%d" % i) for i in range(2)]
            hts = [C.sb(ph, [128, 8, 512], BF, "hTi%d" % i) for i in range(2)]
            obs = [C.sb(ph, [128, 512], BF, "ob%d" % i) for i in range(4)]
            xps = [C.sb(ph, [128, 515], BF, "xp%d" % i) for i in range(3)]
            accs = [C.sb(ph, [128, 512], F32, "acc%d" % i) for i in range(2)]
            c3s = [C.sb(ph, [128, 3], F32, "c3%d" % i) for i in range(2)]
            halo = C.sb(ph, [128, 32, 3], BF, "halo")
            dg = C.sb(ph, [128, 16, 4, 128], BF, "dg")
            P.memset('dve', halo.t[:], 0.0, [halo.b])
            st = [0]
            dcnt = [0]
            pend = [None]
            cnt = {'ob': 0, 'xp': 0, 'acc': 0, 'ht': 0}
            for gi, (c0, ncols, chunks) in enumerate(groups):
                wb = wbs[gi % 2]
                for kc in range(8):
                    cast_load(wb.t[:, kc, 0:ncols], wb.b, w_in[l, kc * 128:(kc + 1) * 128, c0:c0 + ncols], stg, st, ncols)
                if DECODE:
                    dec_proj(wb, 0, ncols, c0, accs, dcnt)
                if chunks[0][0] == 'x':
                    for ci, (kind, idx) in enumerate(chunks):
                        for j in range(4):
                            P.ts(('dve', 'pool')[(ci * 4 + j) % 2], dg.t[:, ci, j, :], ident.t[:], cwT.t[:, j, idx:idx + 1], None,
                                 ALU.mult, None, [ident.b, cwT.b], [dg.b])
                for tt in range(L // 512):
                    ht = hts[cnt['ht'] % 2]
                    cnt['ht'] += 1
                    P.dma('sp', [(ht.t[:], hT_d[:, :, tt * 512:(tt + 1) * 512].rearrange("k p t -> p k t"))],
                          [C.dbuf(('hT', tt))], [ht.b])
                    for ci, (kind, idx) in enumerate(chunks):
                        pv, pb = C.psum(1)
                        P.mm([(pv[:, 0, :], wb.t[:, kc, ci * 128:(ci + 1) * 128], ht.t[:, kc, :], kc == 0, kc == 7)
                              for kc in range(8)], [wb.b, ht.b], pb)
                        ob = obs[cnt['ob'] % 4]
                        cnt['ob'] += 1
                        if kind in ('q', 'k'):
                            P.copy('act', ob.t[:], pv[:, 0, :], pb, [ob.b])
                            dst = (qT_d if kind == 'q' else kT_d)[idx * 128:(idx + 1) * 128, tt * 512:(tt + 1) * 512]
                            P.dma('act', [(dst, ob.t[:])], [ob.b], [C.dbuf((kind + 'T', idx, tt))])
                        elif kind == 'g':
                            P.act(ob.t[:], pv[:, 0, :], AF.Sigmoid, pb, [ob.b])
                            P.dma('act', [(gT_d[idx * 128:(idx + 1) * 128, tt * 512:(tt + 1) * 512], ob.t[:])],
                                  [ob.b], [C.dbuf(('gT', idx, tt))])
                        else:
                            xp = xps[cnt['xp'] % 3]
                            cnt['xp'] += 1
                            P.copy('act', xp.t[:, 3:515], pv[:, 0, :], pb, [xp.b])
                            if tt == L // 512 - 1:
                                c3 = c3s[idx % 2]
                                P.copy('act', c3.t[:], pv[:, 0, 509:512], pb, [c3.b])
                                P.dma('pool', [(convp[l, :, idx * 128:(idx + 1) * 128].rearrange("t p -> p t"), c3.t[:])],
                                      [c3.b], [C.dbuf(('convp', l, idx))], slow=True)
                            P.copy('pool', xp.t[:, 0:3], halo.t[:, idx, :], [halo.b], [xp.b])
                            P.copy('pool', halo.t[:, idx, :], xp.t[:, 512:515], [xp.b], [halo.b])
                            if pend[0] is not None:
                                pend[0]()

                            def stage2(xp=xp, ob=ob, ci=ci, idx=idx, tt=tt):
                                pc, pcb = C.psum(1)
                                P.mm([(pc[:, 0, :], dg.t[:, ci, j, :], xp.t[:, j:j + 512], j == 0, j == 3) for j in range(4)],
                                     [dg.b, xp.b], pcb)
                                P.act(ob.t[:], pc[:, 0, :], AF.Silu, pcb + [colp.b], [ob.b], bias=colp.t[:, 64 + idx:65 + idx])
                                P.dma('act', [(xbcT_d[idx * 128:(idx + 1) * 128, tt * 512:(tt + 1) * 512], ob.t[:])],
                                      [ob.b], [C.dbuf(('xbcT', idx, tt))])
                            pend[0] = stage2
                if pend[0] is not None:
                    pend[0]()
                    pend[0] = None
        P.barrier()


    def phase_inproj_tok(l):
        NW = 3616
        with ExitStack() as ph:
            wb = C.sb(ph, [128, 8, NW], BF, "wtok")
            stg = [C.sb(ph, [128, 2048], F32, "stg%d" % i) for i in range(2)]
            hts = [C.sb(ph, [128, 8, 512], BF, "hTk%d" % i) for i in range(2)]
            vts = [C.sb(ph, [128, QKV], BF, "vt%d" % i) for i in range(2)]
            zts = [C.sb(ph, [128, DIN], BF, "zt%d" % i) for i in range(2)]
            dts = [C.sb(ph, [128, 64], F32, "dt%d" % i) for i in range(2)]
            kvf = [C.sb(ph, [128, 2 * QKV], F32, "kvf%d" % i) for i in range(2)]
            st = [0]
            for kc in range(8):
                rows = slice(kc * 128, (kc + 1) * 128)
                cast_load(wb.t[:, kc, 0:1536], wb.b, w_in[l, rows, C_K:C_K + 1536], stg, st, 1536)
                cast_load(wb.t[:, kc, 1536:3584], wb.b, w_in[l, rows, C_Z:C_Z + 2048], stg, st, 2048)
                cast_load(wb.t[:, kc, 3584:3616], wb.b, w_in[l, rows, C_DT:C_DT + 32], stg, st, 32)
            if DECODE:
                dcnt = [0]
                dec_proj(wb, 768, 768, C_V, kvf, dcnt)
                dec_proj(wb, 1536, 2048, C_Z, kvf, dcnt)
                dec_proj(wb, 3584, 32, C_DT, kvf, dcnt)
            for tt in range(L // 512):
                ht = hts[tt % 2]
                P.dma('sp', [(ht.t[:], hT_d[:, :, tt * 512:(tt + 1) * 512].rearrange("k p t -> p k t"))],
                      [C.dbuf(('hT', tt))], [ht.b])
                for j in range(4):
                    tb = tt * 4 + j
                    tok0 = tb * 128
                    vt, zt, dtt = vts[tb % 2], zts[tb % 2], dts[tb % 2]

                    def proj(c0, n, out_ap_fn):
                        pv, pb = C.psum(1)
                        P.mm([(pv[:, 0, 0:n], ht.t[:, kc, j * 128:(j + 1) * 128], wb.t[:, kc, c0:c0 + n], kc == 0, kc == 7)
                              for kc in range(8)], [wb.b, ht.b], pb)
                        return pv[:, 0, 0:n], pb
                    last = tok0 >= L - 2048 and 'kvp' not in SKIP
                    kf = kvf[tb % 2]
                    for (c0, n) in ((768, 512), (1280, 256)):
                        pa, pb = proj(c0, n, None)
                        P.copy('act', vt.t[:, c0 - 768:c0 - 768 + n], pa, pb, [vt.b])
                        if last:
                            P.copy(KVENG, kf.t[:, c0:c0 + n], pa, pb, [kf.b] + (pb if 'pbx' in SKIP else []))
                    P.dma('act', [(v_d[tok0:tok0 + 128, :], vt.t[:])], [vt.b], [C.dbuf(('v', tb))])
                    if last:
                        for (c0, n) in ((0, 512), (512, 256)):
                            pa, pb = proj(c0, n, None)
                            P.copy(KVENG, kf.t[:, c0:c0 + n], pa, pb, [kf.b] + (pb if 'pbx' in SKIP else []))
                        for g in range(3):
                            r0 = tok0 - (L - WINS[g])
                            if r0 < 0:
                                continue
                            if 'kvpdma' in SKIP:
                                continue
                            P.dma('sp', [(kvp[g][l, r0:r0 + 128, 0:256], kf.t[:, g * 256:(g + 1) * 256]),
                                         (kvp[g][l, r0:r0 + 128, 256:512], kf.t[:, 768 + g * 256:768 + (g + 1) * 256])],
                                  [kf.b], [C.dbuf(('kvp', l, tb, g))])
                    for q4 in range(4):
                        pa, pb = proj(1536 + q4 * 512, 512, None)
                        P.act(zt.t[:, q4 * 512:(q4 + 1) * 512], pa, AF.Silu, pb, [zt.b])
                    P.dma('act', [(zs_d[tok0:tok0 + 128, :], zt.t[:])], [zt.b], [C.dbuf(('zs', tb))])
                    if 'dt' in SKIP:
                        continue
                    pa, pb = proj(3584, 32, None)
                    P.tt('dve', dtt.t[:, 0:32], pa, dtb_bc.t[:], ALU.add, pb + [dtb_bc.b], [dtt.b])
                    P.act(dtt.t[:, 0:32], dtt.t[:, 0:32], AF.Exp, [dtt.b], [dtt.b])
                    P.act(dtt.t[:, 0:32], dtt.t[:, 0:32], AF.Ln, [dtt.b], [dtt.b], bias=1.0)
                    P.tt('dve', dtt.t[:, 32:64], dtt.t[:, 0:32], A_bc.t[:], ALU.mult, [dtt.b, A_bc.b], [dtt.b])
                    P.dma('act', [(dta_d[tok0:tok0 + 128, :], dtt.t[:])], [dtt.b], [C.dbuf(('dta', tb))])
        P.barrier()

    def phase_attn(l):
        NSB = L // 2048
        with ExitStack() as ph:
            qA = [[C.sb(ph, [128, 2048], BF, "qA%d%d" % (i, p)) for p in range(2)] for i in range(2)]
            qB = [[C.sb(ph, [128, 2048], BF, "qB%d%d" % (i, p)) for p in range(2)] for i in range(2)]
            kTs = [C.sb(ph, [128, 2, 4096], BF, "kT%d" % i) for i in range(2)]
            vts = [C.sb(ph, [128, 32, 4, 65], BF, "vtl%d" % i) for i in range(2)]
            pTs = [C.sb(ph, [128, 4, 2, 128], BF, "pT%d" % i) for i in range(2)]
            uos = [C.sb(ph, [128, 260], F32, "uo%d" % i) for i in range(4)]
            for i in range(2):
                for p in range(2):
                    P.memset('dve', qA[i][p].t[64:128, :], 0.0, [qA[i][p].b])
                    P.memset('dve', qB[i][p].t[0:64, :], 0.0, [qB[i][p].b])
                P.memset('dve', vts[i].t[:, :, :, 64:65], 1.0, [vts[i].b])
            it = 0
            nq = 0
            for g in range(3):
                dil = DILS[g]
                nj = 16 // dil
                for sb in range(NSB):
                    bi = it % 2
                    it += 1
                    t0 = sb * 2048
                    kT, vt = kTs[bi], vts[bi]
                    for p in range(2):
                        r0 = (g * 2 + p) * 128
                        rdq = [C.dbuf(('qT', g * 2 + p, t)) for t in range(sb * 4, sb * 4 + 4)]
                        P.dma('sp', [(qA[bi][p].t[0:64, :], qT_d[r0:r0 + 64, t0:t0 + 2048])], rdq, [qA[bi][p].b])
                        P.dma('sp', [(qB[bi][p].t[64:128, :], qT_d[r0 + 64:r0 + 128, t0:t0 + 2048])], rdq, [qB[bi][p].b])
                        if sb > 0:
                            rdk = [C.dbuf(('kT', g * 2 + p, t)) for t in range(sb * 4 - 4, sb * 4 + 4)]
                            P.dma('sp', [(kT.t[:, p, :], kT_d[r0:r0 + 128, t0 - 2048:t0 + 2048])], rdk, [kT.b])
                        else:
                            rdk = [C.dbuf(('kT', g * 2 + p, t)) for t in range(0, 4)]
                            P.dma('sp', [(kT.t[:, p, 2048:4096], kT_d[r0:r0 + 128, 0:2048])], rdk, [kT.b])
                    rdv = [C.dbuf(('v', t)) for t in range(max(0, sb * 16 - 16), sb * 16 + 16)]
                    pairs = []
                    for r in range(dil):
                        for kl in range(nj + 1):
                            kbi = sb * nj - 1 + kl
                            if kbi < 0:
                                continue
                            tok = r + dil * kbi * 128
                            src = v_d[tok:tok + dil * 127 + 1:dil, g * 256:(g + 1) * 256].rearrange("t (h d) -> t h d", h=4)
                            pairs.append((vt.t[:, r * (nj + 1) + kl, :, 0:64], src))
                    for i in range(0, len(pairs), 8):
                        P.dma('sp', pairs[i:i + 8], rdv, [vt.b])
                    for r in range(dil):
                        for jl in range(nj):
                            jb = sb * nj + jl
                            has_prev = jb > 0
                            qs = slice(r + dil * jl * 128, r + dil * jl * 128 + dil * 127 + 1, dil)
                            kcur = slice(2048 + qs.start, 2048 + qs.stop, dil)
                            kprv = slice(2048 + qs.start - 128 * dil, 2048 + qs.stop - 128 * dil, dil)
                            pv, pb = C.psum(2)
                            S = pv.rearrange("p a (h q) -> p (a h) q", q=128)
                            items = []
                            for h in range(4):
                                qz = (qA if h % 2 == 0 else qB)[bi][h // 2]
                                for kb in range(2):
                                    if kb == 1 and not has_prev:
                                        continue
                                    ks = kcur if kb == 0 else kprv
                                    items.append((S[:, h * 2 + kb, :], kT.t[:, h // 2, ks], qz.t[:, qs], True, False))
                                    items.append((S[:, h * 2 + kb, :], antib.t[:], hank[g].t[:, h, kb, :], False, True))
                            P.mm(items, [kT.b, qA[bi][0].b, qA[bi][1].b, qB[bi][0].b, qB[bi][1].b, antib.b, hank[g].b], pb)
                            pT = pTs[nq % 2]
                            if has_prev:
                                P.act(pT.t[:].rearrange("p h k q -> p (h k) q"), S, AF.Exp, pb, [pT.b], scale=0.125)
                            else:
                                P.act(pT.t[:, :, 0, :], pv.rearrange("p a (h q) -> p (a h) q", q=256)[:, :, 0:128],
                                      AF.Exp, pb, [pT.b], scale=0.125)
                            pu, pub = C.psum(1)
                            items = []
                            for h in range(4):
                                kbs = (0, 1) if has_prev else (0,)
                                for n_, kb in enumerate(kbs):
                                    vidx = r * (nj + 1) + jl + (1 - kb)
                                    items.append((pu[:, 0, h * 65:(h + 1) * 65], pT.t[:, h, kb, :], vt.t[:, vidx, h, :],
                                                  n_ == 0, n_ == len(kbs) - 1))
                            P.mm(items, [pT.b, vt.b], pub)
                            uo = uos[nq % 4]
                            P.copy('dve', uo.t[:], pu[:, 0, 0:260], pub, [uo.b])
                            tok = t0 + qs.start
                            P.dma('pool', [(U_d[g][tok:tok + dil * 127 + 1:dil, :], uo.t[:])], [uo.b],
                                  [C.dbuf(('U', g, sb, r, jl))])
                            nq += 1
        P.barrier()


    def phase_ssd(l):
        with ExitStack() as ph:
            xin = [C.sb(ph, [128, 32, 128], BF, "xin%d" % i) for i in range(2)]
            dta = [C.sb(ph, [128, 64], F32, "dta%d" % i) for i in range(2)]
            zs = [C.sb(ph, [128, DIN], BF, "zs%d" % i) for i in range(2)]
            xdt = C.sb(ph, [128, DIN], BF, "xdt")
            xtok = C.sb(ph, [128, DIN], BF, "xtok")
            btok = C.sb(ph, [128, 1024], BF, "btok")
            sm = C.sb(ph, [128, 160], F32, "sm")
            rhsA = C.sb(ph, [128, NH, 128], F32, "rhsA")
            eseg = C.sb(ph, [128, NH, 128], F32, "eseg")
            cbm = C.sb(ph, [128, 8, 128], F32, "cbm")
            mT = C.sb(ph, [128, NH, 128], BF, "mT")
            yt = C.sb(ph, [128, DIN], F32, "yt")
            y2 = C.sb(ph, [128, DIN], F32, "y2")
            ynb = C.sb(ph, [128, DIN], BF, "ynb")
            xdtd = C.sb(ph, [128, DIN], BF, "xdtd")
            state = C.sb(ph, [128, DIN], F32, "state")
            stbf = C.sb(ph, [128, DIN], BF, "stbf")
            ssq = C.sb(ph, [128, 8], F32, "ssq")
            junk = C.sb(ph, [128, 256], F32, "junk")
            sTo = [C.sb(ph, [128, 16, 128], BF, "sTo%d" % i) for i in range(2)]
            sng_bc = C.sb(ph, [128, DIN], F32, "sng")
            P.dma('sp', [(sng_bc.t[:], ssd_norm_g[l:l + 1, :].to_broadcast([128, DIN]))], [], [sng_bc.b])
            P.memset('dve', state.t[:], 0.0, [state.b])
            P.memset('dve', stbf.t[:], 0.0, [stbf.b])
            for c in range(NB):
                tok0 = c * 128
                xi, da, z = xin[c % 2], dta[c % 2], zs[c % 2]
                P.dma('sp', [(xi.t[:], xbcT_d[:, tok0:tok0 + 128].rearrange("(a p) t -> p a t", p=128)),
                             (da.t[:], dta_d[tok0:tok0 + 128, :]), (z.t[:], zs_d[tok0:tok0 + 128, :])],
                      [], [xi.b, da.b, z.b])
                dt_ap = da.t[:, 0:32]
                a_ap = da.t[:, 32:64]
                pv, pb = C.psum(2)
                pvb = C.ps[:, C.psb.index(pb[0]):C.psb.index(pb[0]) + 2, :].bitcast(BF)
                P.tr([(pvb[:, cc // 8, (cc % 8) * 128:(cc % 8 + 1) * 128], xi.t[:, cc, :], identb.t[:]) for cc in range(16)],
                     [xi.b, identb.b], pb)
                P.copy('act', xtok.t[:].rearrange("p (a b) -> p a b", b=1024), pvb, pb, [xtok.b])
                P.tt('dve', xdt.t[:].rearrange("p (h d) -> p h d", d=64), xtok.t[:].rearrange("p (h d) -> p h d", d=64),
                     dt_ap.unsqueeze(2).to_broadcast([128, NH, 64]), ALU.mult, [xtok.b, da.b], [xdt.b])
                pv, pb = C.psum(1)
                pvb = C.ps[:, C.psb.index(pb[0]):C.psb.index(pb[0]) + 1, :].bitcast(BF)
                P.tr([(pvb[:, 0, g * 128:(g + 1) * 128], xi.t[:, 16 + g, :], identb.t[:]) for g in range(8)],
                     [xi.b, identb.b], pb)
                P.copy('act', btok.t[:], pvb[:, 0, :], pb, [btok.b])
                pv, pb = C.psum(1)
                P.mm([(pv[:, 0, 0:32], triu.t[:], a_ap, True, True), (pv[:, 0, 32:64], ones.t[:], a_ap, True, True)],
                     [triu.b, ones.b, da.b], pb)
                P.copy('dve', sm.t[:, 0:32], pv[:, 0, 0:32], pb, [sm.b])
                P.copy('dve', sm.t[:, 128:160], pv[:, 0, 32:64], pb, [sm.b])
                P.act(sm.t[:, 32:64], sm.t[:, 0:32], AF.Exp, [sm.b], [sm.b])
                P.tt('dve', sm.t[:, 64:96], sm.t[:, 128:160], sm.t[:, 0:32], ALU.subtract, [sm.b], [sm.b])
                P.act(sm.t[:, 64:96], sm.t[:, 64:96], AF.Exp, [sm.b], [sm.b])
                P.act(sm.t[:, 96:128], sm.t[:, 128:160], AF.Exp, [sm.b], [sm.b])
                P.tt('dve', rhsA.t[:], triu.t[:].unsqueeze(1).to_broadcast([128, NH, 128]),
                     a_ap.unsqueeze(2).to_broadcast([128, NH, 128]), ALU.mult, [triu.b, da.b], [rhsA.b])
                for q4 in range(8):
                    pv, pb = C.psum(1)
                    P.mm([(pv[:, 0, :], su.t[:], rhsA.t[:, q4 * 4:(q4 + 1) * 4, :].rearrange("p h l -> p (h l)"), True, True)],
                         [su.b, rhsA.b], pb)
                    P.act(eseg.t[:, q4 * 4:(q4 + 1) * 4, :].rearrange("p h l -> p (h l)"), pv[:, 0, :], AF.Exp, pb, [eseg.b])
                pv, pb = C.psum(2)
                P.mm([(pv[:, g // 4, (g % 4) * 128:(g % 4 + 1) * 128], xi.t[:, 16 + g, :], xi.t[:, 24 + g, :], True, True)
                      for g in range(8)], [xi.b], pb)
                P.tt('dve', cbm.t[:], pv.rearrange("p a (g l) -> p (a g) l", l=128),
                     triu.t[:].unsqueeze(1).to_broadcast([128, 8, 128]), ALU.mult, pb + [triu.b], [cbm.b])
                P.tt('dve', mT.t[:].rearrange("p (g e) l -> p g e l", e=4), eseg.t[:].rearrange("p (g e) l -> p g e l", e=4),
                     cbm.t[:].unsqueeze(2).to_broadcast([128, 8, 4, 128]), ALU.mult, [eseg.b, cbm.b], [mT.b])
                pvy, pby = C.psum(4)
                P.mm([(pvy[:, h // 8, (h % 8) * 64:(h % 8 + 1) * 64], mT.t[:, h, :], xdt.t[:, h * 64:(h + 1) * 64], True, True)
                      for h in range(NH)], [mT.b, xdt.b], pby)
                pvo, pbo = C.psum(4)
                P.mm([(pvo[:, g // 2, (g % 2) * 256:(g % 2 + 1) * 256], xi.t[:, 24 + g, :], stbf.t[:, g * 256:(g + 1) * 256], True, True)
                      for g in range(8)], [xi.b, stbf.b], pbo)
                P.tt('dve', y2.t[:].rearrange("p (h d) -> p h d", d=64), pvo.rearrange("p a (h d) -> p (a h) d", d=64),
                     sm.t[:, 32:64].unsqueeze(2).to_broadcast([128, NH, 64]), ALU.mult, pbo + [sm.b], [y2.b])
                P.tt('dve', yt.t[:].rearrange("p (a b) -> p a b", b=512), pvy, y2.t[:].rearrange("p (a b) -> p a b", b=512),
                     ALU.add, pby + [y2.b], [yt.b])
                P.tt('dve', y2.t[:].rearrange("p (h d) -> p h d", d=64), xtok.t[:].rearrange("p (h d) -> p h d", d=64),
                     D_bc.t[:].unsqueeze(2).to_broadcast([128, NH, 64]), ALU.mult, [xtok.b, D_bc.b], [y2.b])
                P.tt('dve', yt.t[:], yt.t[:], y2.t[:], ALU.add, [yt.b, y2.b], [yt.b])
                P.tt('dve', yt.t[:], yt.t[:], z.t[:], ALU.mult, [yt.b, z.b], [yt.b])
                for g in range(8):
                    P.act(junk.t[:], yt.t[:, g * 256:(g + 1) * 256], AF.Square, [yt.b], [junk.b, ssq.b],
                          accum_out=ssq.t[:, g:g + 1])
                P.act(ssq.t[:], ssq.t[:], AF.Sqrt, [ssq.b, epsT.b], [ssq.b], scale=1.0 / 256, bias=epsT.t[:, 0:1])
                P.op('dve', lambda e: e.reciprocal(out=ssq.t[:], in_=ssq.t[:]), [ssq.b], [ssq.b])
                P.tt('dve', yt.t[:].rearrange("p (g d) -> p g d", d=256), yt.t[:].rearrange("p (g d) -> p g d", d=256),
                     ssq.t[:].unsqueeze(2).to_broadcast([128, 8, 256]), ALU.mult, [yt.b, ssq.b], [yt.b])
                P.tt('dve', ynb.t[:], yt.t[:], sng_bc.t[:], ALU.mult, [yt.b, sng_bc.b], [ynb.b])
                pv, pb = C.psum(2)
                pvb = C.ps[:, C.psb.index(pb[0]):C.psb.index(pb[0]) + 2, :].bitcast(BF)
                P.tr([(pvb[:, cc // 8, (cc % 8) * 128:(cc % 8 + 1) * 128], ynb.t[:, cc * 128:(cc + 1) * 128], identb.t[:])
                      for cc in range(16)], [ynb.b, identb.b], pb)
                so = sTo[c % 2]
                P.copy('act', so.t[:].rearrange("p (a b) l -> p a (b l)", a=2), pvb, pb, [so.b])
                P.dma('pool', [(sT_d[:, tok0:tok0 + 128].rearrange("(a p) t -> p a t", p=128), so.t[:])], [so.b],
                      [C.dbuf(('sT', c))])
                P.tt('dve', xdtd.t[:].rearrange("p (h d) -> p h d", d=64), xdt.t[:].rearrange("p (h d) -> p h d", d=64),
                     sm.t[:, 64:96].unsqueeze(2).to_broadcast([128, NH, 64]), ALU.mult, [xdt.b, sm.b], [xdtd.b])
                pvs, pbs = C.psum(4)
                P.mm([(pvs[:, g // 2, (g % 2) * 256:(g % 2 + 1) * 256], btok.t[:, g * 128:(g + 1) * 128],
                       xdtd.t[:, g * 256:(g + 1) * 256], True, True) for g in range(8)], [btok.b, xdtd.b], pbs)
                P.tt('dve', state.t[:].rearrange("p (h d) -> p h d", d=64), state.t[:].rearrange("p (h d) -> p h d", d=64),
                     sm.t[:, 96:128].unsqueeze(2).to_broadcast([128, NH, 64]), ALU.mult, [state.b, sm.b], [state.b])
                P.tt('dve', state.t[:].rearrange("p (a b) -> p a b", b=512), pvs, state.t[:].rearrange("p (a b) -> p a b", b=512),
                     ALU.add, pbs + [state.b], [state.b])
                P.copy('act', stbf.t[:], state.t[:], [state.b], [stbf.b])
            for cc in range(16):
                pv, pb = C.psum(1)
                P.tr([(pv[:, 0, 0:128], state.t[:, cc * 128:(cc + 1) * 128], ident.t[:])], [state.b, ident.b], pb)
                o_ = sTo[cc % 2]
                of = yt
                P.copy('dve', yt.t[:, cc * 128:(cc + 1) * 128], pv[:, 0, 0:128], pb, [yt.b])
            P.dma('pool', [(ssmp[l].rearrange("(a p) n -> p a n", p=128), yt.t[:].rearrange("p (a n) -> p a n", n=128))],
                  [yt.b], [C.dbuf(('ssmp', l))])
        P.barrier()


    def phase_out(l):
        with ExitStack() as ph:
            woa = C.sb(ph, [128, 2, D], BF, "woa")
            wos = C.sb(ph, [128, 16, D], BF, "wos")
            wo = C.sb(ph, [128, 8, D], BF, "wo")
            stg = [C.sb(ph, [128, 2048], F32, "stg%d" % i) for i in range(2)]
            st = [0]
            for kc in range(2):
                cast_load(woa.t[:, kc, :], woa.b, w_o_attn[l, kc * 128:(kc + 1) * 128, :], stg, st, D)
            for kc in range(16):
                cast_load(wos.t[:, kc, :], wos.b, w_o_ssd[l, kc * 128:(kc + 1) * 128, :], stg, st, D)
            for kc in range(8):
                cast_load(wo.t[:, kc, :], wo.b, w_out[l, kc * 128:(kc + 1) * 128, :], stg, st, D)
            us = [C.sb(ph, [128, 3, 260], F32, "us%d" % i) for i in range(2)]
            rz = C.sb(ph, [128, 4], F32, "rz")
            att = C.sb(ph, [128, 256], F32, "att")
            aT = [C.sb(ph, [128, 2, 512], BF, "aT%d" % i) for i in range(2)]
            sT = [C.sb(ph, [128, 16, 512], BF, "sT%d" % i) for i in range(1)]
            gT = [C.sb(ph, [128, 16, 512], BF, "gT%d" % i) for i in range(1)]
            mg = C.sb(ph, [128, 8, 512], BF, "mg")
            t1 = [C.sb(ph, [128, 512], F32, "t1%d" % i) for i in range(2)]
            t2 = [C.sb(ph, [128, 512], F32, "t2%d" % i) for i in range(2)]
            xts = [C.sb(ph, [128, 4, D], F32, "xo%d" % i) for i in range(2)]
            xsrc = x_in if l == 0 else xm_d
            if DECODE:
                dec_out(ph, l, woa, wos, wo, xts[1], t1)
            for tt in range(L // 512):
                a_, s_, g_, xt = aT[tt % 2], sT[0], gT[0], xts[tt % 2]
                ts_ = slice(tt * 512, (tt + 1) * 512)
                P.dma('sp', [(s_.t[:], sT_d[:, ts_].rearrange("(a p) t -> p a t", p=128)),
                             (g_.t[:], gT_d[:, ts_].rearrange("(a p) t -> p a t", p=128)),
                             (xt.t[:], xsrc[ts_, :].rearrange("(j p) d -> p j d", p=128))], [], [s_.b, g_.b, xt.b])
                for j in range(4):
                    tok0 = tt * 512 + j * 128
                    u = us[j % 2]
                    P.dma('sp', [(u.t[:, g, :], U_d[g][tok0:tok0 + 128, :]) for g in range(3)], [], [u.b])
                    P.tt('dve', u.t[:, 0, :], u.t[:, 0, :], u.t[:, 1, :], ALU.add, [u.b], [u.b])
                    P.tt('dve', u.t[:, 0, :], u.t[:, 0, :], u.t[:, 2, :], ALU.add, [u.b], [u.b])
                    uv = u.t[:, 0, :].rearrange("p (h d) -> p h d", d=65)
                    P.op('dve', lambda e, uv=uv: e.reciprocal(out=rz.t[:].unsqueeze(2), in_=uv[:, :, 64:65]), [u.b], [rz.b])
                    P.tt('dve', att.t[:].rearrange("p (h d) -> p h d", d=64), uv[:, :, 0:64],
                         rz.t[:].unsqueeze(2).to_broadcast([128, 4, 64]), ALU.mult, [u.b, rz.b], [att.b])
                    pv, pb = C.psum(1)
                    P.tr([(pv[:, 0, cc * 128:(cc + 1) * 128], att.t[:, cc * 128:(cc + 1) * 128], ident.t[:]) for cc in range(2)],
                         [att.b, ident.b], pb)
                    P.copy('act', a_.t[:, :, j * 128:(j + 1) * 128], pv[:, 0, 0:256].rearrange("p (c t) -> p c t", t=128),
                           pb, [a_.b])
                for fc in range(8):
                    fs = slice(fc * 128, (fc + 1) * 128)
                    pa, pab = C.psum(1)
                    P.mm([(pa[:, 0, :], woa.t[:, kc, fs], a_.t[:, kc, :], kc == 0, kc == 1) for kc in range(2)],
                         [woa.b, a_.b], pab)
                    pbs_, pbb = C.psum(1)
                    P.mm([(pbs_[:, 0, :], wos.t[:, kc, fs], s_.t[:, kc, :], kc == 0, kc == 15) for kc in range(16)],
                         [wos.b, s_.b], pbb)
                    P.tt('dve', t1[fc % 2].t[:], pa[:, 0, :], g_.t[:, fc, :], ALU.mult, pab + [g_.b], [t1[fc % 2].b])
                    P.tt('dve', t2[fc % 2].t[:], pbs_[:, 0, :], g_.t[:, 8 + fc, :], ALU.mult, pbb + [g_.b], [t2[fc % 2].b])
                    P.tt('dve', mg.t[:, fc, :], t1[fc % 2].t[:], t2[fc % 2].t[:], ALU.add, [t1[fc % 2].b, t2[fc % 2].b], [mg.b])
                for j in range(4):
                    for n in range(2):
                        ns = slice(n * 512, (n + 1) * 512)
                        pv, pb = C.psum(1)
                        P.mm([(pv[:, 0, :], mg.t[:, kc, j * 128:(j + 1) * 128], wo.t[:, kc, ns], kc == 0, kc == 7)
                              for kc in range(8)], [mg.b, wo.b], pb)
                        P.tt('dve', t1[n].t[:], pv[:, 0, :], g1bc.t[:, ns], ALU.mult, pb + [g1bc.b], [t1[n].b])
                        P.tt('dve', xt.t[:, j, ns], xt.t[:, j, ns], t1[n].t[:], ALU.add, [xt.b, t1[n].b], [xt.b])
                P.dma('pool', [(x1_d[ts_, :].rearrange("(j p) d -> p j d", p=128), xt.t[:])], [xt.b], [C.dbuf(('x1', tt))])
        P.barrier()

    def phase_mlp(l):
        last = (l == DEPTH - 1)
        hidTd = C.sb(es_layer[0], [128, 32, NS], BF, "hidTd") if DECODE else None
        with ExitStack() as ph:
            wu = C.sb(ph, [128, 8, DFF], BF, "wu")
            stg = [C.sb(ph, [128, 1024], F32, "stg%d" % i) for i in range(2)]
            st = [0]
            for kc in range(8):
                for hf in range(4):
                    cast_load(wu.t[:, kc, hf * 1024:(hf + 1) * 1024], wu.b,
                              w_up[l, kc * 128:(kc + 1) * 128, hf * 1024:(hf + 1) * 1024], stg, st, 1024)
            xts = [C.sb(ph, [128, 4, D], F32, "xm%d" % i) for i in range(2)]
            h2 = [C.sb(ph, [128, 8, 512], BF, "h2%d" % i) for i in range(2)]
            rl = [C.sb(ph, [128, 512], F32, "rl%d" % i) for i in range(3)]
            ho = [C.sb(ph, [128, 512], BF, "ho%d" % i) for i in range(4)]
            junk = C.sb(ph, [128, D], BF, "junk")
            sss = [C.sb(ph, [128, 4], F32, "ss%d" % i) for i in range(2)]
            rstds = [C.sb(ph, [128, 4], F32, "rstd%d" % i) for i in range(2)]
            if DECODE:
                h2dT = C.sb(ph, [128, 8, NS], BF, "h2dT")
                dec_norm_T(ph, xd.t[:], xd.b, A2, B2, h2dT, xts[1])
                for n in range(8):
                    pv, pb = C.psum(1)
                    P.mm([(pv[0:NS, 0, :], h2dT.t[:, kc, :], wu.t[:, kc, n * 512:(n + 1) * 512], kc == 0, kc == 7)
                          for kc in range(8)], [h2dT.b, wu.b], pb)
                    P.act(rl[0].t[0:NS, :], pv[0:NS, 0, :], AF.Relu, pb, [rl[0].b])
                    P.tt('dve', rl[1].t[0:NS, :], rl[0].t[0:NS, :], pv[0:NS, 0, :], ALU.mult, pb + [rl[0].b], [rl[1].b])
                    pt, ptb = C.psum(1)
                    P.tr([(pt[:, 0, i * NS:(i + 1) * NS], rl[1].t[0:NS, i * 128:(i + 1) * 128], ident.t[0:NS, 0:NS])
                          for i in range(4)], [rl[1].b, ident.b], ptb)
                    P.copy('act', hidTd.t[:, n * 4:(n + 1) * 4, :].rearrange("p c b -> p (c b)"), pt[:, 0, 0:4 * NS], ptb,
                           [hidTd.b])
            nr = 0
            for tt in range(L // 512):
                xt, h2t, ss, rstd = xts[tt % 2], h2[tt % 2], sss[tt % 2], rstds[tt % 2]
                ts_ = slice(tt * 512, (tt + 1) * 512)
                P.dma('sp', [(xt.t[:], x1_d[ts_, :].rearrange("(j p) d -> p j d", p=128))], [], [xt.b])
                rms_rstd(ph, xt, 4, junk, ss, rstd)
                for j in range(4):
                    norm_to_hT(xt, j, rstd, A2, B2, h2t, j * 128, j)
                for hc in range(32):
                    pv, pb = C.psum(1)
                    P.mm([(pv[:, 0, :], wu.t[:, kc, hc * 128:(hc + 1) * 128], h2t.t[:, kc, :], kc == 0, kc == 7)
                          for kc in range(8)], [wu.b, h2t.b], pb)
                    r_ = rl[nr % 3]
                    o_ = ho[nr % 4]
                    nr += 1
                    P.act(r_.t[:], pv[:, 0, :], AF.Relu, pb, [r_.b])
                    P.tt('dve', o_.t[:], r_.t[:], pv[:, 0, :], ALU.mult, pb + [r_.b], [o_.b])
                    P.dma('sp', [(hidT_d[hc * 128:(hc + 1) * 128, ts_], o_.t[:])], [o_.b], [Buf()])
        P.barrier()
        with ExitStack() as ph:
            wd = C.sb(ph, [128, 32, D], BF, "wd")
            stg = [C.sb(ph, [128, 512], F32, "stg%d" % i) for i in range(2)]
            st = [0]
            for kc in range(32):
                for hf in range(2):
                    cast_load(wd.t[:, kc, hf * 512:(hf + 1) * 512], wd.b,
                              w_down[l, kc * 128:(kc + 1) * 128, hf * 512:(hf + 1) * 512], stg, st, 512)
            xts = [C.sb(ph, [128, 4, D], F32, "xm%d" % i) for i in range(2)]
            hts = [C.sb(ph, [128, 32, 512], BF, "hid%d" % i) for i in range(2)]
            t1 = [C.sb(ph, [128, 512], F32, "tm%d" % i) for i in range(3)]
            junk = C.sb(ph, [128, D], BF, "junk")
            ss = C.sb(ph, [128, 4], F32, "ss")
            rstd = C.sb(ph, [128, 4], F32, "rstd")
            if DECODE:
                g2d = TB(xts[1].t[0:NS, 0, :])
                g2d.b = xts[1].b
                P.dma('sp', [(g2d.t[:], grows_d[l, 1:1 + NS, D:2 * D])], [], [g2d.b])
                for n in range(2):
                    ns = slice(n * 512, (n + 1) * 512)
                    pv, pb = C.psum(1)
                    P.mm([(pv[0:NS, 0, :], hidTd.t[:, hc, :], wd.t[:, hc, ns], hc == 0, hc == 31) for hc in range(32)],
                         [hidTd.b, wd.b], pb)
                    P.tt('dve', t1[0].t[0:NS, :], pv[0:NS, 0, :], g2d.t[:, ns], ALU.mult, pb + [g2d.b], [t1[0].b])
                    P.tt('dve', xd.t[:, ns], xd.t[:, ns], t1[0].t[0:NS, :], ALU.add, [xd.b, t1[0].b], [xd.b])
                if last:
                    fss = C.sb(ph, [NS, 2], F32, "fss")
                    yo = g2d.t[:]
                    P.act(yo, xd.t[:], AF.Square, [xd.b], [g2d.b, fss.b], accum_out=fss.t[:, 0:1])
                    P.act(fss.t[:, 1:2], fss.t[:, 0:1], AF.Sqrt, [fss.b, epsT.b], [fss.b], scale=1.0 / D, bias=epsT.t[0:NS, 0:1])
                    P.op('dve', lambda e: e.reciprocal(out=fss.t[:, 1:2], in_=fss.t[:, 1:2]), [fss.b], [fss.b])
                    P.ts('dve', yo, xd.t[:], fss.t[:, 1:2], None, ALU.mult, None, [xd.b, fss.b], [g2d.b])
                    P.tt('dve', yo, yo, fg_bc.t[0:NS, :], ALU.mult, [g2d.b, fg_bc.b], [g2d.b])
                    P.dma('pool', [(ys_out, yo)], [g2d.b], [Buf()])
            nt = 0
            for tt in range(L // 512):
                xt, hid = xts[tt % 2], hts[tt % 2]
                ts_ = slice(tt * 512, (tt + 1) * 512)
                P.dma('sp', [(xt.t[:], x1_d[ts_, :].rearrange("(j p) d -> p j d", p=128)),
                             (hid.t[:], hidT_d[:, ts_].rearrange("(a p) t -> p a t", p=128))], [], [xt.b, hid.b])
                for j in range(4):
                    for n in range(2):
                        ns = slice(n * 512, (n + 1) * 512)
                        pv, pb = C.psum(1)
                        P.mm([(pv[:, 0, :], hid.t[:, hc, j * 128:(j + 1) * 128], wd.t[:, hc, ns], hc == 0, hc == 31)
                              for hc in range(32)], [hid.b, wd.b], pb)
                        tm = t1[nt % 3]
                        nt += 1
                        P.tt('dve', tm.t[:], pv[:, 0, :], g2bc.t[:, ns], ALU.mult, pb + [g2bc.b], [tm.b])
                        P.tt('dve', xt.t[:, j, ns], xt.t[:, j, ns], tm.t[:], ALU.add, [xt.b, tm.b], [xt.b])
                if not last:
                    P.dma('act', [(xm_d[ts_, :].rearrange("(j p) d -> p j d", p=128), xt.t[:])], [xt.b], [Buf()])
                else:
                    rms_rstd(ph, xt, 4, junk, ss, rstd)
                    for j in range(4):
                        P.ts('dve', xt.t[:, j, :], xt.t[:, j, :], rstd.t[:, j:j + 1], None, ALU.mult, None,
                             [xt.b, rstd.b], [xt.b])
                        P.tt('dve', xt.t[:, j, :], xt.t[:, j, :], fg_bc.t[:], ALU.mult, [xt.b, fg_bc.b], [xt.b])
                    P.dma('act', [(y_out[ts_, :].rearrange("(j p) d -> p j d", p=128), xt.t[:])], [xt.b], [Buf()])
        P.barrier()

    def dec_norm_T(ph, xsrc_ap, xsrc_buf, A_, B_, outT, scr):
        ss = C.sb(ph, [NS, 2], F32, "dss")
        tmpT = C.sb(ph, [128, 8, NS], F32, "dtmpT")
        xn = scr.t[0:NS, 0:D] if len(scr.t.shape) == 2 else scr.t[0:NS, 0, :]
        P.act(xn, xsrc_ap, AF.Square, [xsrc_buf], [scr.b, ss.b], accum_out=ss.t[:, 0:1])
        P.act(ss.t[:, 1:2], ss.t[:, 0:1], AF.Sqrt, [ss.b, epsT.b], [ss.b], scale=1.0 / D, bias=epsT.t[0:NS, 0:1])
        P.op('dve', lambda e: e.reciprocal(out=ss.t[:, 1:2], in_=ss.t[:, 1:2]), [ss.b], [ss.b])
        P.act(xn, xsrc_ap, AF.Copy, [xsrc_buf, ss.b], [scr.b], scale=ss.t[:, 1:2])
        pv, pb = C.psum(1)
        P.tr([(pv[:, 0, kc * NS:(kc + 1) * NS], xn[:, kc * 128:(kc + 1) * 128], ident.t[0:NS, 0:NS]) for kc in range(8)],
             [scr.b, ident.b], pb)
        P.tt('dve', tmpT.t[:], pv[:, 0, 0:8 * NS].rearrange("p (k b) -> p k b", b=NS), A_.t[:, :, 1:1 + NS], ALU.mult,
             pb + [A_.b], [tmpT.b])
        P.tt('dve', outT.t[:], tmpT.t[:], B_.t[:, :, 1:1 + NS], ALU.add, [tmpT.b, B_.b], [outT.b])
        return ss

    def dec_proj(wb, wcol, ncols, dcol, tmps, cnt):
        for n0 in range(0, ncols, 512):
            n = min(512, ncols - n0)
            pv, pb = C.psum(1)
            P.mm([(pv[0:NS, 0, 0:n], hdT.t[:, kc, :], wb.t[:, kc, wcol + n0:wcol + n0 + n], kc == 0, kc == 7)
                  for kc in range(8)], [hdT.b, wb.b], pb)
            t = tmps[cnt[0] % len(tmps)]
            cnt[0] += 1
            P.copy('act', t.t[0:NS, 0:n], pv[0:NS, 0, 0:n], pb, [t.b])
            P.dma('pool', [(dproj_d[:, dcol + n0:dcol + n0 + n], t.t[0:NS, 0:n])], [t.b], [Buf()])

    def phase_dec_norm1(l):
        with ExitStack() as ph:
            scr = C.sb(ph, [NS, D], F32, "dscr")
            dec_norm_T(ph, xd.t[:], xd.b, A1, B1, hdT, scr)
        P.barrier()

    def phase_dec_mixers(l):
        with ExitStack() as ph:
            qkvd = C.sb(ph, [NS, 2304], F32, "qkvd")
            xnew = C.sb(ph, [NS, CONV], F32, "xnew")
            xc = C.sb(ph, [NS, CONV], F32, "xc")
            zd = C.sb(ph, [NS, DIN], F32, "zd")
            dtd = C.sb(ph, [NS, 64], F32, "dtd")
            P.dma('sp', [(qkvd.t[:], dproj_d[:, 0:2304]), (xnew.t[:], dproj_d[:, C_X:C_X + CONV]),
                         (zd.t[:], dproj_d[:, C_Z:C_Z + DIN]), (dtd.t[:, 0:32], dproj_d[:, C_DT:C_DT + 32])], [],
                  [qkvd.b, xnew.b, zd.b, dtd.b])
            P.dma('pool', [(convs[l, :, 0:2, :], conv_in[l, :, 1:3, :]), (convs[l, :, 2, :], xnew.t[:])], [xnew.b], [Buf()])
            cst = C.sb(ph, [NS, 3, 1024], F32, "cst")
            cw4 = C.sb(ph, [NS, 4, 1024], F32, "cw4")
            cb4 = C.sb(ph, [NS, 1024], F32, "cb4")
            ctmp = C.sb(ph, [NS, 1024], F32, "ctmp")
            for q4 in range(4):
                cs_ = slice(q4 * 1024, (q4 + 1) * 1024)
                P.dma('sp', [(cst.t[:], conv_in[l, :, :, cs_]),
                             (cw4.t[:], conv_w[l:l + 1, :, cs_].to_broadcast([NS, 4, 1024])),
                             (cb4.t[:], conv_b[l:l + 1, cs_].to_broadcast([NS, 1024]))], [], [cst.b, cw4.b, cb4.b])
                P.tt('dve', xc.t[:, cs_], xnew.t[:, cs_], cw4.t[:, 3, :], ALU.mult, [xnew.b, cw4.b], [xc.b])
                P.tt('dve', xc.t[:, cs_], xc.t[:, cs_], cb4.t[:], ALU.add, [xc.b, cb4.b], [xc.b])
                for j in range(3):
                    P.tt('dve', ctmp.t[:], cst.t[:, j, :], cw4.t[:, j, :], ALU.mult, [cst.b, cw4.b], [ctmp.b])
                    P.tt('dve', xc.t[:, cs_], xc.t[:, cs_], ctmp.t[:], ALU.add, [xc.b, ctmp.b], [xc.b])
            P.act(xc.t[:], xc.t[:], AF.Silu, [xc.b], [xc.b])
            P.tt('dve', dtd.t[:, 0:32], dtd.t[:, 0:32], dtb_bc.t[0:NS, :], ALU.add, [dtd.b, dtb_bc.b], [dtd.b])
            P.act(dtd.t[:, 0:32], dtd.t[:, 0:32], AF.Exp, [dtd.b], [dtd.b])
            P.act(dtd.t[:, 0:32], dtd.t[:, 0:32], AF.Ln, [dtd.b], [dtd.b], bias=1.0)
            P.tt('dve', dtd.t[:, 32:64], dtd.t[:, 0:32], A_bc.t[0:NS, :], ALU.mult, [dtd.b, A_bc.b], [dtd.b])
            P.act(dtd.t[:, 32:64], dtd.t[:, 32:64], AF.Exp, [dtd.b], [dtd.b])
            rows = C.sb(ph, [NS, 2, DIN], F32, "drows")
            P.tt('dve', rows.t[:, 0, :].rearrange("p (h d) -> p h d", d=64), xc.t[:, 0:DIN].rearrange("p (h d) -> p h d", d=64),
                 dtd.t[:, 0:32].unsqueeze(2).to_broadcast([NS, NH, 64]), ALU.mult, [xc.b, dtd.b], [rows.b])
            P.copy('dve', rows.t[:, 1, :].rearrange("p (h d) -> p h d", d=64),
                   dtd.t[:, 32:64].unsqueeze(2).to_broadcast([NS, NH, 64]), [dtd.b], [rows.b])
            colD = C.sb(ph, [128, 2, 16, NS], F32, "colD")
            pv, pb = C.psum(1)
            P.tr([(pv[:, 0, (w * 16 + cc) * NS:(w * 16 + cc + 1) * NS], rows.t[:, w, cc * 128:(cc + 1) * 128], ident.t[0:NS, 0:NS])
                  for w in range(2) for cc in range(16)], [rows.b, ident.b], pb)
            P.copy('act', colD.t[:].rearrange("p w c b -> p (w c b)"), pv[:, 0, 0:32 * NS], pb, [colD.b])
            stt_ = [C.sb(ph, [128, 16, 128], F32, "dst%d" % i) for i in range(2)]
            t1 = C.sb(ph, [128, 16, 128], F32, "dt1")
            ycol = C.sb(ph, [128, 16, NS], F32, "ycol")
            for b in range(NS):
                st_ = stt_[b % 2]
                P.dma('sp', [(st_.t[:], ssm_in[l, b].rearrange("(a p) n -> p a n", p=128))], [], [st_.b])
                pv, pb = C.psum(4)
                for j in range(4):
                    P.mm([(pv[:, j, :], sel4.t[:, b, :], xc.t[:, DIN + j * 512:DIN + (j + 1) * 512], True, True)],
                         [sel4.b, xc.b], pb[j:j + 1])
                Bv = pv[:, 0:2, :].rearrange("p a (g n) -> p (a g) n", n=128)
                Cv = pv[:, 2:4, :].rearrange("p a (g n) -> p (a g) n", n=128)
                P.tt('dve', t1.t[:].rearrange("p (g e) n -> p g e n", e=2), Bv.unsqueeze(2).to_broadcast([128, 8, 2, 128]),
                     colD.t[:, 0, :, b].rearrange("p (g e) -> p g e", e=2).unsqueeze(3).to_broadcast([128, 8, 2, 128]),
                     ALU.mult, pb[0:2] + [colD.b], [t1.b])
                P.tt('dve', st_.t[:], st_.t[:], colD.t[:, 1, :, b].unsqueeze(2).to_broadcast([128, 16, 128]), ALU.mult,
                     [st_.b, colD.b], [st_.b])
                P.tt('dve', st_.t[:], st_.t[:], t1.t[:], ALU.add, [st_.b, t1.b], [st_.b])
                P.dma('pool', [(ssms[l, b].rearrange("(a p) n -> p a n", p=128), st_.t[:])], [st_.b], [Buf()])
                P.tt('dve', t1.t[:].rearrange("p (g e) n -> p g e n", e=2), st_.t[:].rearrange("p (g e) n -> p g e n", e=2),
                     Cv.unsqueeze(2).to_broadcast([128, 8, 2, 128]), ALU.mult, pb[2:4] + [st_.b], [t1.b])
                P.op('dve', lambda e, b=b: e.tensor_reduce(out=ycol.t[:, :, b], in_=t1.t[:], axis=AX.X, op=ALU.add),
                     [t1.b], [ycol.b])
            pv, pb = C.psum(4)
            P.tr([(pv[0:NS, cc // 4, (cc % 4) * 128:(cc % 4 + 1) * 128], ycol.t[:, cc, :], ident.t[:]) for cc in range(16)],
                 [ycol.b, ident.b], pb)
            yd = C.sb(ph, [NS, DIN], F32, "yd")
            P.tt('dve', rows.t[:, 0, :].rearrange("p (h d) -> p h d", d=64), xc.t[:, 0:DIN].rearrange("p (h d) -> p h d", d=64),
                 D_bc.t[0:NS, :].unsqueeze(2).to_broadcast([NS, NH, 64]), ALU.mult, [xc.b, D_bc.b], [rows.b])
            P.tt('dve', yd.t[:].rearrange("p (a b) -> p a b", b=512), pv[0:NS, :, :], rows.t[:, 0, :].rearrange("p (a b) -> p a b", b=512),
                 ALU.add, pb + [rows.b], [yd.b])
            P.act(zd.t[:], zd.t[:], AF.Silu, [zd.b], [zd.b])
            P.tt('dve', yd.t[:], yd.t[:], zd.t[:], ALU.mult, [yd.b, zd.b], [yd.b])
            ssq = C.sb(ph, [NS, 8], F32, "dssq")
            for g in range(8):
                P.act(rows.t[:, 1, 0:256], yd.t[:, g * 256:(g + 1) * 256], AF.Square, [yd.b], [rows.b, ssq.b],
                      accum_out=ssq.t[:, g:g + 1])
            P.act(ssq.t[:], ssq.t[:], AF.Sqrt, [ssq.b, epsT.b], [ssq.b], scale=1.0 / 256, bias=epsT.t[0:NS, 0:1])
            P.op('dve', lambda e: e.reciprocal(out=ssq.t[:], in_=ssq.t[:]), [ssq.b], [ssq.b])
            P.tt('dve', yd.t[:].rearrange("p (g d) -> p g d", d=256), yd.t[:].rearrange("p (g d) -> p g d", d=256),
                 ssq.t[:].unsqueeze(2).to_broadcast([NS, 8, 256]), ALU.mult, [yd.b, ssq.b], [yd.b])
            P.dma('sp', [(rows.t[:, 1, :], ssd_norm_g[l:l + 1, :].to_broadcast([NS, DIN]))], [], [rows.b])
            P.tt('dve', yd.t[:], yd.t[:], rows.t[:, 1, :], ALU.mult, [yd.b, rows.b], [yd.b])
            pv, pb = C.psum(1)
            P.tr([(pv[:, 0, cc * NS:(cc + 1) * NS], yd.t[:, cc * 128:(cc + 1) * 128], ident.t[0:NS, 0:NS]) for cc in range(16)],
                 [yd.b, ident.b], pb)
            P.copy('act', sdT.t[:].rearrange("p c b -> p (c b)"), pv[:, 0, 0:16 * NS], pb, [sdT.b])
            for g in range(3):
                w = WINS[g]
                P.dma('pool', [(kvs[g][l, :, 0:w - 1, :], kvc[g][l, :, 1:w, :]),
                               (kvs[g][l, :, w - 1, 0:256], qkvd.t[:, 768 + g * 256:768 + (g + 1) * 256]),
                               (kvs[g][l, :, w - 1, 256:512], qkvd.t[:, 1536 + g * 256:1536 + (g + 1) * 256])],
                      [qkvd.b], [Buf()])
            prod0 = C.sb(ph, [NS, 768], F32, "prod0")
            e0 = C.sb(ph, [NS, 24], F32, "e0")
            P.dma('sp', [(e0.t[:, 12:24], rel_bias[0:1, :].to_broadcast([NS, 12]))], [], [e0.b])
            P.tt('dve', prod0.t[:], qkvd.t[:, 0:768], qkvd.t[:, 768:1536], ALU.mult, [qkvd.b], [prod0.b])
            P.op('dve', lambda e: e.tensor_reduce(out=e0.t[:, 0:12], in_=prod0.t[:].rearrange("p (h d) -> p h d", d=64),
                                                  axis=AX.X, op=ALU.add), [prod0.b], [e0.b])
            P.stt(e0.t[:, 0:12], e0.t[:, 0:12], 0.125, e0.t[:, 12:24], ALU.mult, ALU.add, [e0.b], [e0.b])
            P.act(e0.t[:, 0:12], e0.t[:, 0:12], AF.Exp, [e0.b], [e0.b])
            P.tt('dve', prod0.t[:].rearrange("p (h d) -> p h d", d=64), qkvd.t[:, 1536:2304].rearrange("p (h d) -> p h d", d=64),
                 e0.t[:, 0:12].unsqueeze(2).to_broadcast([NS, 12, 64]), ALU.mult, [qkvd.b, e0.b], [prod0.b])
            u0 = C.sb(ph, [NS, 260], F32, "u0")
            P.tt('dve', u0.t[:, 0:256], prod0.t[:, 0:256], prod0.t[:, 256:512], ALU.add, [prod0.b], [u0.b])
            P.tt('dve', u0.t[:, 0:256], u0.t[:, 0:256], prod0.t[:, 512:768], ALU.add, [prod0.b, u0.b], [u0.b])
            P.tt('dve', u0.t[:, 256:260], e0.t[:, 0:4], e0.t[:, 4:8], ALU.add, [e0.b], [u0.b])
            P.tt('dve', u0.t[:, 256:260], u0.t[:, 256:260], e0.t[:, 8:12], ALU.add, [e0.b, u0.b], [u0.b])
            kvt = [C.sb(ph, [128, 516], F32, "kvt%d" % i) for i in range(2)]
            for i in range(2):
                P.memset('dve', kvt[i].t[:, 512:516], 1.0, [kvt[i].b])
            prod = C.sb(ph, [128, 256], F32, "dprod")
            sc = C.sb(ph, [128, 8], F32, "dsc")
            mk = C.sb(ph, [4, NS, 260], F32, "mk")
            nk = 0
            for b in range(NS):
                pq, pqb = C.psum(2)
                P.mm([(pq[:, 0, :], sel4.t[:, b, :], qkvd.t[:, 0:512], True, True)], [sel4.b, qkvd.b], pqb[0:1])
                P.mm([(pq[:, 1, 0:256], sel4.t[:, b, :], qkvd.t[:, 512:768], True, True)], [sel4.b, qkvd.b], pqb[1:2])
                pu, pub = C.psum(1)
                for g in range(3):
                    kt = kvt[nk % 2]
                    nk += 1
                    P.dma('sp', [(kt.t[:, 0:512], kvc[g][l, b, 0:WINS[g]:DILS[g], :])], [], [kt.b])
                    qv = pq[:, g // 2, (g % 2) * 256:(g % 2) * 256 + 256]
                    P.tt('dve', prod.t[:], kt.t[:, 0:256], qv, ALU.mult, [kt.b] + pqb[g // 2:g // 2 + 1], [prod.b])
                    P.op('dve', lambda e: e.tensor_reduce(out=sc.t[:, 0:4], in_=prod.t[:].rearrange("p (h d) -> p h d", d=64),
                                                          axis=AX.X, op=ALU.add), [prod.b], [sc.b])
                    P.stt(sc.t[:, 0:4], sc.t[:, 0:4], 0.125, biasd.t[:, g * 4:(g + 1) * 4], ALU.mult, ALU.add,
                          [sc.b, biasd.b], [sc.b])
                    P.act(sc.t[:, 4:8], sc.t[:, 0:4], AF.Exp, [sc.b], [sc.b])
                    P.mm([(pu[0:4, 0, 0:260], sc.t[:, 4:8], kt.t[:, 256:516], g == 0, g == 2)], [sc.b, kt.b], pub)
                P.tt('dve', mk.t[:, b, :], pu[0:4, 0, 0:260], dmask.t[:], ALU.mult, pub + [dmask.b], [mk.b])
            po, pob = C.psum(1)
            P.mm([(po[0:NS, 0, 0:260], colsel.t[:, b, :], mk.t[:, b, :], b == 0, b == NS - 1) for b in range(NS)],
                 [colsel.b, mk.b], pob)
            P.tt('dve', u0.t[:], u0.t[:], po[0:NS, 0, 0:260], ALU.add, [u0.b] + pob, [u0.b])
            P.op('dve', lambda e: e.reciprocal(out=u0.t[:, 256:260], in_=u0.t[:, 256:260]), [u0.b], [u0.b])
            att = C.sb(ph, [NS, 256], F32, "datt")
            P.tt('dve', att.t[:].rearrange("p (h d) -> p h d", d=64), u0.t[:, 0:256].rearrange("p (h d) -> p h d", d=64),
                 u0.t[:, 256:260].unsqueeze(2).to_broadcast([NS, 4, 64]), ALU.mult, [u0.b], [att.b])
            pv, pb = C.psum(1)
            P.tr([(pv[:, 0, cc * NS:(cc + 1) * NS], att.t[:, cc * 128:(cc + 1) * 128], ident.t[0:NS, 0:NS]) for cc in range(2)],
                 [att.b, ident.b], pb)
            P.copy('act', adT.t[:].rearrange("p c b -> p (c b)"), pv[:, 0, 0:2 * NS], pb, [adT.b])
        P.barrier()

    def dec_out(ph, l, woa, wos, wo, scr, t1s):
        gd = scr.t[0:NS, 0:2, :].rearrange("p a d -> p (a d)")
        g1d = scr.t[0:NS, 2, :]
        md = scr.t[0:NS, 3, :]
        P.dma('sp', [(scr.t[0:NS, 0:2, :], dproj_d[:, C_G:C_G + DIN].rearrange("p (a d) -> p a d", a=2)),
                     (g1d, grows_d[l, 1:1 + NS, 0:D])], [], [scr.b])
        P.act(gd, gd, AF.Sigmoid, [scr.b], [scr.b])
        for n in range(2):
            ns = slice(n * 512, (n + 1) * 512)
            pa, pab = C.psum(1)
            P.mm([(pa[0:NS, 0, :], adT.t[:, kc, :], woa.t[:, kc, ns], kc == 0, kc == 1) for kc in range(2)], [adT.b, woa.b], pab)
            P.tt('dve', t1s[0].t[0:NS, :], pa[0:NS, 0, :], gd[:, n * 512:(n + 1) * 512], ALU.mult, pab + [scr.b], [t1s[0].b])
            ps_, psb_ = C.psum(1)
            P.mm([(ps_[0:NS, 0, :], sdT.t[:, kc, :], wos.t[:, kc, ns], kc == 0, kc == 15) for kc in range(16)], [sdT.b, wos.b], psb_)
            P.tt('dve', t1s[1].t[0:NS, :], ps_[0:NS, 0, :], gd[:, D + n * 512:D + (n + 1) * 512], ALU.mult, psb_ + [scr.b],
                 [t1s[1].b])
            P.tt('dve', md[:, ns], t1s[0].t[0:NS, :], t1s[1].t[0:NS, :], ALU.add, [t1s[0].b, t1s[1].b], [scr.b])
        mT = C.sb(ph, [128, 8, NS], BF, "dmT")
        pv, pb = C.psum(1)
        P.tr([(pv[:, 0, kc * NS:(kc + 1) * NS], md[:, kc * 128:(kc + 1) * 128], ident.t[0:NS, 0:NS]) for kc in range(8)],
             [scr.b, ident.b], pb)
        P.copy('act', mT.t[:].rearrange("p k b -> p (k b)"), pv[:, 0, 0:8 * NS], pb, [mT.b])
        for n in range(2):
            ns = slice(n * 512, (n + 1) * 512)
            pv, pb = C.psum(1)
            P.mm([(pv[0:NS, 0, :], mT.t[:, kc, :], wo.t[:, kc, ns], kc == 0, kc == 7) for kc in range(8)], [mT.b, wo.b], pb)
            P.tt('dve', t1s[0].t[0:NS, :], pv[0:NS, 0, :], g1d[:, ns], ALU.mult, pb + [scr.b], [t1s[0].b])
            P.tt('dve', xd.t[:, ns], xd.t[:, ns], t1s[0].t[0:NS, :], ALU.add, [xd.b, t1s[0].b], [xd.b])

    def dec_mlp(ph, l, wu, wd, scr, t1s, last):
        h2dT = C.sb(ph, [128, 8, NS], BF, "h2dT")
        hidT = C.sb(ph, [128, 32, NS], BF, "hidT")
        dec_norm_T(ph, xd.t[:], xd.b, A2, B2, h2dT, scr)
        g2d = scr.t[0:NS, 1, :]
        P.dma('sp', [(g2d, grows_d[l, 1:1 + NS, D:2 * D])], [], [scr.b])
        for n in range(8):
            pv, pb = C.psum(1)
            P.mm([(pv[0:NS, 0, :], h2dT.t[:, kc, :], wu.t[:, kc, n * 512:(n + 1) * 512], kc == 0, kc == 7) for kc in range(8)],
                 [h2dT.b, wu.b], pb)
            P.act(t1s[0].t[0:NS, :], pv[0:NS, 0, :], AF.Relu, pb, [t1s[0].b])
            P.tt('dve', t1s[1].t[0:NS, :], t1s[0].t[0:NS, :], pv[0:NS, 0, :], ALU.mult, pb + [t1s[0].b], [t1s[1].b])
            pt, ptb = C.psum(1)
            P.tr([(pt[:, 0, i * NS:(i + 1) * NS], t1s[1].t[0:NS, i * 128:(i + 1) * 128], ident.t[0:NS, 0:NS]) for i in range(4)],
                 [t1s[1].b, ident.b], ptb)
            P.copy('act', hidT.t[:, n * 4:(n + 1) * 4, :].rearrange("p c b -> p (c b)"), pt[:, 0, 0:4 * NS], ptb, [hidT.b])
        for n in range(2):
            ns = slice(n * 512, (n + 1) * 512)
            pv, pb = C.psum(1)
            P.mm([(pv[0:NS, 0, :], hidT.t[:, hc, :], wd.t[:, hc, ns], hc == 0, hc == 31) for hc in range(32)], [hidT.b, wd.b], pb)
            P.tt('dve', t1s[0].t[0:NS, :], pv[0:NS, 0, :], g2d[:, ns], ALU.mult, pb + [scr.b], [t1s[0].b])
            P.tt('dve', xd.t[:, ns], xd.t[:, ns], t1s[0].t[0:NS, :], ALU.add, [xd.b, t1s[0].b], [xd.b])
        if last:
            ss = C.sb(ph, [NS, 2], F32, "fss")
            yo = scr.t[0:NS, 0, :]
            P.act(yo, xd.t[:], AF.Square, [xd.b], [scr.b, ss.b], accum_out=ss.t[:, 0:1])
            P.act(ss.t[:, 1:2], ss.t[:, 0:1], AF.Sqrt, [ss.b, epsT.b], [ss.b], scale=1.0 / D, bias=epsT.t[0:NS, 0:1])
            P.op('dve', lambda e: e.reciprocal(out=ss.t[:, 1:2], in_=ss.t[:, 1:2]), [ss.b], [ss.b])
            P.ts('dve', yo, xd.t[:], ss.t[:, 1:2], None, ALU.mult, None, [xd.b, ss.b], [scr.b])
            P.tt('dve', yo, yo, fg_bc.t[0:NS, :], ALU.mult, [scr.b, fg_bc.b], [scr.b])
            P.dma('pool', [(ys_out, yo)], [scr.b], [Buf()])

    es_layer = [es]
    for l in range(DEPTH):
        phase_params(l)
        if 'params_only' in stages:
            break
        if DECODE:
            phase_dec_norm1(l)
        phase_norm1(l)
        if 'norm1_only' in stages:
            break
        phase_inproj_T(l)
        if 'inT_only' in stages:
            break
        phase_inproj_tok(l)
        if 'intok_only' in stages:
            break
        phase_attn(l)
        if 'attn_only' in stages:
            break
        phase_ssd(l)
        if 'ssd_only' in stages:
            break
        if DECODE:
            phase_dec_mixers(l)
        phase_out(l)
        if 'out_only' in stages:
            break
        phase_mlp(l)
        if 'l0_only' in stages:
            break

    if 'params_only' in stages:
        for nm, tb, shp in (("dbg_A1", A1, [128, 40]), ("dbg_B1", B1, [128, 40]), ("dbg_A2", A2, [128, 40]),
                            ("dbg_B2", B2, [128, 40])):
            o_ = dout(nm, shp)
            P.dma('sp', [(o_, tb.t[:].rearrange("p a b -> p (a b)"))], [tb.b], [C.dbuf(nm)])
        for nm, tb, shp in (("dbg_g1bc", g1bc, [128, D]), ("dbg_g2bc", g2bc, [128, D]), ("dbg_cwT", cwT, [128, 128]),
                            ("dbg_colp", colp, [128, 96]), ("dbg_Abc", A_bc, [128, NH])):
            o_ = dout(nm, shp)
            src_ = tb.t[:].rearrange("p a b -> p (a b)") if len(tb.t.shape) == 3 else tb.t[:]
            P.dma('sp', [(o_, src_)], [tb.b], [C.dbuf(nm)])

    P.finalize(nc, es)


def prep_inputs(inputs):
    f = lambda a: np.ascontiguousarray(np.asarray(a, dtype=np.float32))
    consts = make_consts()
    shared = {}
    for k in ("rel_bias", "w_ada", "b_ada", "norm1_g", "norm2_g", "w_in", "conv_w", "conv_b", "dt_bias",
              "a_log", "d_skip", "ssd_norm_g", "w_o_attn", "w_o_ssd", "w_out", "w_up", "w_down", "final_g"):
        shared[k] = f(inputs[k])
    shared.update(consts)
    maps = []
    for core in range(8):
        b = core // 4
        s0 = core * NS
        m = dict(shared)
        m["x"] = f(inputs["x_prompt"][b][:L])
        m["c5"] = f(np.concatenate([inputs["c_prompt"][b:b + 1], inputs["c_sample"][s0:s0 + NS]], axis=0))
        m["xs"] = f(inputs["x_sample"][s0:s0 + NS, 0])
        caches = (inputs["cache_kv_g0"], inputs["cache_kv_g1"], inputs["cache_kv_g2"])
        for g in range(3):
            kv = np.asarray(caches[g])[:, s0:s0 + NS]
            m["kvc%d" % g] = f(kv.reshape(DEPTH, NS, WINS[g], 512))
        m["ssm_in"] = f(np.asarray(inputs["state_ssm"])[:, s0:s0 + NS].reshape(DEPTH, NS, NH * 64, 128))
        m["conv_in"] = f(np.asarray(inputs["state_conv"])[:, s0:s0 + NS])
        maps.append(m)
    return maps


_NC_CACHE = {}


def kernel(**inputs):
    maps = prep_inputs(inputs)
    if "nc" not in _NC_CACHE:
        _NC_CACHE["nc"] = build()
    nc = _NC_CACHE["nc"]
    res = run_bass_kernel_spmd(nc, maps, core_ids=list(range(8)))
    R = res.results
    B = 2
    y_prompt = np.stack([R[0]["y"], R[4]["y"]], axis=0)
    y_sample = np.concatenate([R[c]["ys"] for c in range(8)], axis=0).reshape(32, 1, D)
    outs = [y_prompt.astype(np.float32), y_sample.astype(np.float32)]
    for g in range(3):
        kvg = np.stack([R[0]["kvp%d" % g], R[4]["kvp%d" % g]], axis=1)
        outs.append(kvg.reshape(DEPTH, B, WINS[g], 2, 4, 64).astype(np.float32))
    outs.append(np.stack([R[0]["ssmp"], R[4]["ssmp"]], axis=1).reshape(DEPTH, B, NH, 64, 128).astype(np.float32))
    outs.append(np.stack([R[0]["convp"], R[4]["convp"]], axis=1).reshape(DEPTH, B, 3, CONV).astype(np.float32))
    for g in range(3):
        kvg = np.concatenate([R[c]["kvs%d" % g] for c in range(8)], axis=1)
        outs.append(kvg.reshape(DEPTH, 32, WINS[g], 2, 4, 64).astype(np.float32))
    outs.append(np.concatenate([R[c]["ssms"] for c in range(8)], axis=1).reshape(DEPTH, 32, NH, 64, 128)
                .astype(np.float32))
    outs.append(np.concatenate([R[c]["convs"] for c in range(8)], axis=1).reshape(DEPTH, 32, 3, CONV)
                .astype(np.float32))
    return tuple(outs)
```

```python
import math
import os as _os_l
from contextlib import ExitStack

import numpy as np
import concourse.bass as bass
import concourse.mybir as mybir
from concourse.bass_utils import run_bass_kernel_spmd

F32 = mybir.dt.float32
BF = mybir.dt.bfloat16
AF = mybir.ActivationFunctionType
ALU = mybir.AluOpType
AX = mybir.AxisListType

D = 1024
L = int(_os_l.environ.get("KL", "8192"))
NB = L // 128
DEPTH = 2
NS = 4
NIN = 10528
QKV = 768
DIN = 2048
CONV = 4096
NH = 32
DFF = 4096
EPS = 1e-6
DILS = (1, 4, 16)
WINS = (128, 512, 2048)
C_Q, C_K, C_V, C_Z, C_X, C_DT, C_G = 0, 768, 1536, 2304, 4352, 8448, 8480
NEG8 = -240000.0

COMPUTE = {'pe': 0, 'act': 1, 'dve': 2, 'pool': 3}
DMA_ENGS = ['sp', 'pool', 'act']
NSLOT = 8
NSTREAM = 4 + len(DMA_ENGS) * NSLOT
SAME_ENGINE_SYNC = True
import os as _os
KVENG = _os.environ.get('KVENG', 'dve')
DVE_COPY_TS = _os.environ.get('DVE_COPY_TS', '0') == '1'
SAME_ENGINE_SYNC = _os.environ.get('SES', '1') == '1'
SKIP = set(_os.environ.get('KSKIP', '').split(','))


class Buf:
    __slots__ = ('lw', 'rd', 'name', 'excl')

    def __init__(self, name='', excl=False):
        self.lw = None
        self.rd = {}
        self.name = name
        self.excl = excl


class Op:
    __slots__ = ('eng', 'fn', 'sid', 'ord', 'ndma', 'ms', 'cnt', 'waits', 'deps', 'gi')


class Prog:
    def __init__(self):
        self.ops = []
        self.stream_ops = [[] for _ in range(NSTREAM)]
        self.eng_ops = {e: [] for e in ['pe', 'act', 'dve', 'pool', 'sp']}
        self.dma_count = {e: 0 for e in DMA_ENGS}
        self.barrier_deps = []

    def op(self, eng, fn, reads=(), writes=(), ndma=0):
        o = Op()
        o.eng, o.fn, o.ndma, o.ms, o.cnt, o.waits = eng, fn, ndma, False, 0, None
        deps = {}

        def add(d):
            if d is None:
                return
            cur = deps.get(d.sid)
            if cur is None or cur.ord < d.ord:
                deps[d.sid] = d

        if any(b.excl for b in reads):
            writes = list(writes) + [b for b in reads if b.excl and b not in writes]
            reads = [b for b in reads if not b.excl]
        for b in reads:
            add(b.lw)
        for b in writes:
            add(b.lw)
            for r in b.rd.values():
                add(r)
        for d in self.barrier_deps:
            add(d)
        if ndma:
            j = self.dma_count[eng]
            self.dma_count[eng] = j + 1
            o.sid = 4 + DMA_ENGS.index(eng) * NSLOT + (j % NSLOT)
            prev = self.stream_ops[o.sid]
            if prev:
                add(prev[-1])
        elif eng in COMPUTE:
            o.sid = COMPUTE[eng]
        else:
            o.sid = -1
        if o.sid >= 0:
            so = self.stream_ops[o.sid]
            o.ord = len(so) + 1
            so.append(o)
        else:
            o.ord = 0
        o.deps = list(deps.values())
        if o.sid >= 0:
            for b in reads:
                b.rd[o.sid] = o
            for b in writes:
                b.lw = o
                b.rd = {}
        o.gi = len(self.ops)
        self.ops.append(o)
        self.eng_ops[eng].append(o)
        return o

    def barrier(self):
        self.barrier_deps = [s[-1] for s in self.stream_ops if s]

    def finalize(self, nc, es):
        self.barrier()
        self.op('sp', None)
        n = len(self.ops)
        clocks = np.zeros((n, NSTREAM), np.int32)
        engclk = {e: np.zeros(NSTREAM, np.int32) for e in self.eng_ops}
        for o in self.ops:
            clk = engclk[o.eng]
            waits = []
            own = COMPUTE.get(o.eng, -2)
            for d in o.deps:
                if clk[d.sid] >= d.ord:
                    continue
                if d.sid == own and (o.eng == 'pe' or not SAME_ENGINE_SYNC):
                    continue
                waits.append(d)
                d.ms = True
                np.maximum(clk, clocks[d.gi], out=clk)
                if clk[d.sid] < d.ord:
                    clk[d.sid] = d.ord
            clocks[o.gi] = clk
            o.waits = waits
        for sid, so in enumerate(self.stream_ops):
            c = 0
            for o in so:
                if sid < 4:
                    if o.ms:
                        c += 1
                else:
                    c += 16 * o.ndma
                o.cnt = c
        sems = [es.enter_context(nc.semaphore("s%d" % i)) for i in range(NSTREAM)]
        eng_ops = self.eng_ops

        def emit(name, h):
            for o in eng_ops[name]:
                for d in o.waits:
                    h.wait_ge(sems[d.sid], d.cnt)
                if o.fn is None:
                    continue
                r = o.fn(h)
                if o.ndma:
                    rl = r if isinstance(r, (list, tuple)) else [r]
                    assert len(rl) == o.ndma, (len(rl), o.ndma)
                    for ins in rl:
                        ins.then_inc(sems[o.sid], 16)
                elif o.ms:
                    r.then_inc(sems[o.sid], 1)

        with nc.Block() as block:
            @block.tensor
            def _(e):
                emit('pe', e)

            @block.scalar
            def _(e):
                emit('act', e)

            @block.vector
            def _(e):
                emit('dve', e)

            @block.gpsimd
            def _(e):
                emit('pool', e)

            @block.sync
            def _(e):
                emit('sp', e)

    def mm(self, items, reads, writes):
        items = list(items)

        def fn(e):
            ins = None
            for (o, l, r, st, sp) in items:
                ins = e.matmul(o, l, r, start=st, stop=sp)
            return ins
        return self.op('pe', fn, reads, writes)

    def tr(self, items, reads, writes):
        items = list(items)

        def fn(e):
            ins = None
            for (o, i, idn) in items:
                ins = e.transpose(o, i, idn)
            return ins
        return self.op('pe', fn, reads, writes)

    def act(self, out, in_, func, reads, writes, **kw):
        return self.op('act', lambda e: e.activation(out=out, in_=in_, func=func, **kw), reads, writes)

    def tt(self, eng, out, in0, in1, op, reads, writes):
        return self.op(eng, lambda e: e.tensor_tensor(out=out, in0=in0, in1=in1, op=op), reads, writes)

    def ts(self, eng, out, in0, s1, s2, op0, op1, reads, writes):
        if s2 is None:
            return self.op(eng, lambda e: e.tensor_scalar(out=out, in0=in0, scalar1=s1, scalar2=None, op0=op0),
                           reads, writes)
        return self.op(eng, lambda e: e.tensor_scalar(out=out, in0=in0, scalar1=s1, scalar2=s2, op0=op0, op1=op1),
                       reads, writes)

    def stt(self, out, in0, scalar, in1, op0, op1, reads, writes):
        return self.op('dve', lambda e: e.scalar_tensor_tensor(out=out, in0=in0, scalar=scalar, in1=in1,
                                                               op0=op0, op1=op1), reads, writes)

    def copy(self, eng, out, in_, reads, writes):
        if eng == 'act':
            return self.op('act', lambda e: e.activation(out=out, in_=in_, func=AF.Copy), reads, writes)
        if eng == 'dve' and DVE_COPY_TS:
            return self.op(eng, lambda e: e.tensor_scalar(out=out, in0=in_, scalar1=1.0, scalar2=None, op0=ALU.mult),
                           reads, writes)
        return self.op(eng, lambda e: e.tensor_copy(out=out, in_=in_), reads, writes)

    def memset(self, eng, ap, val, writes):
        return self.op(eng, lambda e: e.memset(ap, val), (), writes)

    def dma(self, eng, pairs, reads, writes, slow=False):
        pairs = list(pairs)

        def fn(e):
            if slow:
                return [e.dma_start(out=o, in_=i, allow_slow_non_contiguous=True) for (o, i) in pairs]
            return [e.dma_start(out=o, in_=i) for (o, i) in pairs]
        return self.op(eng, fn, reads, writes, ndma=len(pairs))


class TB:
    __slots__ = ('t', 'b')

    def __init__(self, t, name=''):
        self.t = t
        self.b = Buf(name)


class Ctx:
    def __init__(self, nc, es):
        self.nc = nc
        self.es = es
        self.P = Prog()
        self.n = 0
        ps = es.enter_context(nc.psum_tensor("psum_all", [128, 8, 512], F32))
        self.ps = ps
        self.psb = [Buf("psb%d" % i, excl=True) for i in range(8)]
        self.psp = 0
        self.dbufs = {}

    def sb(self, stack, shape, dtype, name=None):
        self.n += 1
        nm = "%s_%d" % (name or "t", self.n)
        t = stack.enter_context(self.nc.sbuf_tensor(nm, list(shape), dtype))
        return TB(t, nm)

    def psum(self, nbanks):
        p = self.psp
        if p % nbanks:
            p += nbanks - (p % nbanks)
        if p + nbanks > 8:
            p = 0
        self.psp = (p + nbanks) % 8
        return self.ps[:, p:p + nbanks, :], self.psb[p:p + nbanks]

    def dbuf(self, key):
        b = self.dbufs.get(key)
        if b is None:
            b = Buf(str(key))
            self.dbufs[key] = b
        return b


def t5_bucket_np(dist):
    dist = np.asarray(dist, np.int32)
    max_exact = 16
    df = np.maximum(dist, 1).astype(np.float32)
    large = max_exact + (np.log(df / np.float32(max_exact)) / np.float32(math.log(2048 / max_exact))
                         * np.float32(32 - max_exact)).astype(np.int32)
    large = np.minimum(large, 31)
    return np.where(dist < max_exact, dist, large)


def make_consts():
    c = {}
    c['c_ident'] = np.eye(128, dtype=np.float32)
    c['c_anti'] = np.ascontiguousarray(np.eye(128, dtype=np.float32)[::-1])
    k = np.arange(128)
    c['c_triu'] = (k[:, None] <= k[None, :]).astype(np.float32)
    c['c_su'] = (k[:, None] > k[None, :]).astype(np.float32)
    sel = np.zeros((3, 33, 384), np.float32)
    for g, dil in enumerate(DILS):
        bk = t5_bucket_np(np.arange(129) * dil)
        sel[g, 32, :] = NEG8
        for j in range(129):
            sel[g, bk[j], j + 127] = 8.0
            sel[g, 32, j + 127] = 0.0
    c['c_sel'] = sel
    seld = np.zeros((3, 33, 128), np.float32)
    for g, dil in enumerate(DILS):
        bk = t5_bucket_np(np.arange(129) * dil)
        for p in range(128):
            seld[g, bk[128 - p], p] = 1.0
    c['c_seld'] = seld
    sel4 = np.zeros((NS, NS, 128), np.float32)
    colsel = np.zeros((NS, 4, NS), np.float32)
    for b in range(NS):
        sel4[b, b, :] = 1.0
        colsel[b, :, b] = 1.0
    c['c_sel4'] = sel4
    c['c_colsel'] = colsel
    dmask = np.zeros((4, 260), np.float32)
    for h in range(4):
        dmask[h, h * 64:(h + 1) * 64] = 1.0
        dmask[h, 256 + h] = 1.0
    c['c_dmask'] = dmask
    e5 = np.zeros((5, 5, 128), np.float32)
    for b in range(5):
        e5[b, b, :] = 1.0
    c['c_e5'] = e5
    return c


def build(stages=('all',), dbg=()):
    nc = bass.Bass("TRN2", target_bir_lowering=False)
    es = ExitStack()
    with es:
        _build(nc, es, stages, dbg)
    return nc


def _build(nc, es, stages, dbg):
    C = Ctx(nc, es)
    P = C.P
    allst = 'all' in stages
    DECODE = 'nodec' not in stages

    def din(name, shape, dt=F32):
        return nc.dram_tensor(name, list(shape), dt, kind="ExternalInput").ap()

    def dout(name, shape, dt=F32):
        return nc.dram_tensor(name, list(shape), dt, kind="ExternalOutput").ap()

    def dscr(name, shape, dt):
        kind = "ExternalOutput" if name in dbg else "Internal"
        return nc.dram_tensor(name, list(shape), dt, kind=kind).ap()

    x_in = din("x", [L, D])
    c5_in = din("c5", [5, D])
    xs_in = din("xs", [NS, D])
    kvc = [din("kvc%d" % g, [DEPTH, NS, WINS[g], 512]) for g in range(3)]
    ssm_in = din("ssm_in", [DEPTH, NS, NH * 64, 128])
    conv_in = din("conv_in", [DEPTH, NS, 3, CONV])
    rel_bias = din("rel_bias", [32, 12])
    w_ada = din("w_ada", [DEPTH, D, 6 * D])
    b_ada = din("b_ada", [DEPTH, 6 * D])
    norm1_g = din("norm1_g", [DEPTH, D])
    norm2_g = din("norm2_g", [DEPTH, D])
    w_in = din("w_in", [DEPTH, D, NIN])
    conv_w = din("conv_w", [DEPTH, 4, CONV])
    conv_b = din("conv_b", [DEPTH, CONV])
    dt_bias = din("dt_bias", [DEPTH, NH])
    a_log = din("a_log", [DEPTH, NH])
    d_skip = din("d_skip", [DEPTH, NH])
    ssd_norm_g = din("ssd_norm_g", [DEPTH, DIN])
    w_o_attn = din("w_o_attn", [DEPTH, 256, D])
    w_o_ssd = din("w_o_ssd", [DEPTH, DIN, D])
    w_out = din("w_out", [DEPTH, D, D])
    w_up = din("w_up", [DEPTH, D, DFF])
    w_down = din("w_down", [DEPTH, DFF, D])
    final_g = din("final_g", [D])
    c_ident = din("c_ident", [128, 128])
    c_anti = din("c_anti", [128, 128])
    c_triu = din("c_triu", [128, 128])
    c_su = din("c_su", [128, 128])
    c_sel = din("c_sel", [3, 33, 384])
    c_seld = din("c_seld", [3, 33, 128])
    c_e5 = din("c_e5", [5, 5, 128])
    c_sel4 = din("c_sel4", [NS, NS, 128])
    c_colsel = din("c_colsel", [NS, 4, NS])
    c_dmask = din("c_dmask", [4, 260])

    y_out = dout("y", [L, D])
    ys_out = dout("ys", [NS, D])
    kvp = [dout("kvp%d" % g, [DEPTH, WINS[g], 512]) for g in range(3)]
    ssmp = dout("ssmp", [DEPTH, NH * 64, 128])
    convp = dout("convp", [DEPTH, 3, CONV])
    kvs = [dout("kvs%d" % g, [DEPTH, NS, WINS[g], 512]) for g in range(3)]
    ssms = dout("ssms", [DEPTH, NS, NH * 64, 128])
    convs = dout("convs", [DEPTH, NS, 3, CONV])

    hT_d = dscr("hT_d", [8, 128, L], BF)
    qT_d = dscr("qT_d", [QKV, L], BF)
    kT_d = dscr("kT_d", [QKV, L], BF)
    xbcT_d = dscr("xbcT_d", [CONV, L], BF)
    gT_d = dscr("gT_d", [DIN, L], BF)
    v_d = dscr("v_d", [L, QKV], BF)
    zs_d = dscr("zs_d", [L, DIN], BF)
    dta_d = dscr("dta_d", [L, 64], F32)
    U_d = [dscr("U_d%d" % g, [L, 260], F32) for g in range(3)]
    sT_d = dscr("sT_d", [DIN, L], BF)
    x1_d = dscr("x1_d", [L, D], F32)
    xm_d = dscr("xm_d", [L, D], F32)
    vec_d = dscr("vec_d", [12, 384], F32)
    grows_d = dscr("grows_d", [DEPTH, 5, 2 * D], F32)
    dproj_d = dscr("dproj_d", [NS, NIN], F32)
    hidT_d = dscr("hidT_d", [DFF, L], BF)

    ident = C.sb(es, [128, 128], F32, "ident")
    identb = C.sb(es, [128, 128], BF, "identb")
    antib = C.sb(es, [128, 128], BF, "antib")
    triu = C.sb(es, [128, 128], F32, "triu")
    su = C.sb(es, [128, 128], F32, "su")
    ones = C.sb(es, [128, 128], F32, "ones")
    epsT = C.sb(es, [128, 1], F32, "eps")
    cT = C.sb(es, [128, 8, 5], F32, "cT")
    hank = [C.sb(es, [128, 4, 2, 128], BF, "hank%d" % g) for g in range(3)]
    A1 = C.sb(es, [128, 8, 5], F32, "A1")
    B1 = C.sb(es, [128, 8, 5], F32, "B1")
    A2 = C.sb(es, [128, 8, 5], F32, "A2")
    B2 = C.sb(es, [128, 8, 5], F32, "B2")
    g1bc = C.sb(es, [128, D], F32, "g1bc")
    g2bc = C.sb(es, [128, D], F32, "g2bc")
    cwT = C.sb(es, [128, 4, 32], F32, "cwT")
    colp = C.sb(es, [128, 96], F32, "colp")
    dtb_bc = C.sb(es, [128, NH], F32, "dtb")
    A_bc = C.sb(es, [128, NH], F32, "Abc")
    D_bc = C.sb(es, [128, NH], F32, "Dbc")
    fg_bc = C.sb(es, [128, D], F32, "fg")
    xd = C.sb(es, [NS, D], F32, "xd")
    hdT = C.sb(es, [128, 8, NS], BF, "hdT")
    adT = C.sb(es, [128, 2, NS], BF, "adT")
    sdT = C.sb(es, [128, 16, NS], BF, "sdT")
    biasd = C.sb(es, [128, 12], F32, "biasd")
    sel4 = C.sb(es, [NS, NS, 128], F32, "sel4")
    colsel = C.sb(es, [4, NS, NS], F32, "colsel")
    dmask = C.sb(es, [4, 260], F32, "dmask")

    P.dma('sp', [(ident.t[:], c_ident), (triu.t[:], c_triu), (su.t[:], c_su)], [], [ident.b, triu.b, su.b])
    P.memset('dve', ones.t[:], 1.0, [ones.b])
    P.memset('dve', epsT.t[:], EPS, [epsT.b])
    P.copy('dve', identb.t[:], ident.t[:], [ident.b], [identb.b])
    with ExitStack() as ph:
        tmp = C.sb(ph, [128, 128], F32, "tmp")
        P.dma('sp', [(tmp.t[:], c_anti)], [], [tmp.b])
        P.copy('dve', antib.t[:], tmp.t[:], [tmp.b], [antib.b])
        c5 = C.sb(ph, [5, D], F32, "c5")
        P.dma('sp', [(c5.t[:], c5_in)], [], [c5.b])
        P.act(c5.t[:], c5.t[:], AF.Silu, [c5.b], [c5.b])
        pv, pb = C.psum(1)
        P.tr([(pv[:, 0, kc * 5:(kc + 1) * 5], c5.t[:, kc * 128:(kc + 1) * 128], ident.t[0:5, 0:5]) for kc in range(8)],
             [c5.b, ident.b], pb)
        P.copy('dve', cT.t[:].rearrange("p a b -> p (a b)"), pv[:, 0, 0:40], pb, [cT.b])
        rb33 = C.sb(ph, [33, 12], F32, "rb33")
        selt = C.sb(ph, [33, 3, 384], F32, "selt")
        P.memset('dve', rb33.t[32:33, :], 1.0, [rb33.b])
        P.dma('sp', [(rb33.t[0:32, :], rel_bias), (selt.t[:], c_sel.rearrange("g r m -> r g m"))], [],
              [rb33.b, selt.b])
        seldt = C.sb(ph, [32, 3, 128], F32, "seldt")
        P.dma('sp', [(seldt.t[:], c_seld[:, 0:32, :].rearrange("g r m -> r g m")),
                     (sel4.t[:], c_sel4.rearrange("b k m -> k b m")),
                     (colsel.t[:], c_colsel.rearrange("b k m -> k b m")),
                     (dmask.t[:], c_dmask), (xd.t[:], xs_in)], [], [seldt.b, sel4.b, colsel.b, dmask.b, xd.b])
        pv, pb = C.psum(1)
        P.mm([(pv[:, 0, g * 4:(g + 1) * 4], seldt.t[:, g, :], rb33.t[0:32, g * 4:(g + 1) * 4], True, True) for g in range(3)],
             [seldt.b, rb33.b], pb)
        P.copy('act', biasd.t[:], pv[:, 0, 0:12], pb, [biasd.b])
        vecs = C.sb(ph, [4, 3, 384], F32, "vecs")
        for g in range(3):
            pv, pb = C.psum(1)
            P.mm([(pv[0:4, 0, 0:384], rb33.t[:, g * 4:(g + 1) * 4], selt.t[:, g, :], True, True)],
                 [rb33.b, selt.b], pb)
            P.copy('dve', vecs.t[:, g, :], pv[0:4, 0, 0:384], pb, [vecs.b])
        vb = C.dbuf('vec')
        P.dma('sp', [(vec_d.rearrange("(g h) m -> h g m", g=3), vecs.t[:])], [vecs.b], [vb])
        hk = C.sb(ph, [128, 24, 128], F32, "hk")
        pairs = []
        for g in range(3):
            for h in range(4):
                for kb in range(2):
                    src = bass.AP(vec_d.tensor, (g * 4 + h) * 384 + kb * 128, [[1, 128], [1, 128]])
                    pairs.append((hk.t[:, (g * 4 + h) * 2 + kb, :], src))
        for i in range(0, 24, 8):
            P.dma('sp', pairs[i:i + 8], [vb], [hk.b])
        for g in range(3):
            P.copy('dve', hank[g].t[:].rearrange("p h k q -> p (h k) q"), hk.t[:, g * 8:(g + 1) * 8, :],
                   [hk.b], [hank[g].b])
    P.barrier()

    if 'setup_only' in stages:
        dbg_o = dout("dbg_cT", [128, 40])
        P.dma('sp', [(dbg_o, cT.t[:].rearrange("p a b -> p (a b)"))], [cT.b], [C.dbuf('dbg')])
        dbg_h = dout("dbg_hank", [128, 3, 1024], BF)
        for g in range(3):
            P.dma('sp', [(dbg_h[:, g, :], hank[g].t[:].rearrange("p h k q -> p (h k q)"))], [hank[g].b],
                  [C.dbuf('dbg')])
        P.finalize(nc, es)
        return


    wbytes = lambda: None

    def phase_params(l):
        with ExitStack() as ph:
            rows = C.sb(ph, [128, 128], F32, "rows")
            rows2 = C.sb(ph, [128, 128], F32, "rows2")
            P.memset('dve', rows.t[:], 0.0, [rows.b])
            P.dma('sp', [(rows.t[0:8, :], norm1_g[l].rearrange("(a p) -> a p", p=128)),
                         (rows.t[8:16, :], norm2_g[l].rearrange("(a p) -> a p", p=128)),
                         (rows.t[16:64, :], b_ada[l].rearrange("(a p) -> a p", p=128)),
                         (rows.t[64:96, :], conv_b[l].rearrange("(a p) -> a p", p=128)),
                         (rows2.t[:], conv_w[l].rearrange("j (a p) -> (j a) p", p=128))], [], [rows.b, rows2.b])
            pv, pb = C.psum(1)
            P.tr([(pv[:, 0, 0:128], rows.t[:], ident.t[:]), (pv[:, 0, 128:256], rows2.t[:], ident.t[:])],
                 [rows.b, rows2.b, ident.b], pb)
            P.copy('dve', colp.t[:], pv[:, 0, 0:96], pb, [colp.b])
            P.copy('dve', cwT.t[:].rearrange("p j a -> p (j a)"), pv[:, 0, 128:256], pb, [cwT.b])
            P.dma('sp', [(dtb_bc.t[:], dt_bias[l:l + 1, :].to_broadcast([128, NH])),
                         (A_bc.t[:], a_log[l:l + 1, :].to_broadcast([128, NH])),
                         (D_bc.t[:], d_skip[l:l + 1, :].to_broadcast([128, NH])),
                         (fg_bc.t[:], final_g.rearrange("(o d) -> o d", o=1).to_broadcast([128, D]))], [],
                  [dtb_bc.b, A_bc.b, D_bc.b, fg_bc.b])
            P.act(A_bc.t[:], A_bc.t[:], AF.Exp, [A_bc.b], [A_bc.b])
            P.ts('dve', A_bc.t[:], A_bc.t[:], -1.0, None, ALU.mult, None, [A_bc.b], [A_bc.b])
            modT = C.sb(ph, [128, 48, 5], F32, "modT")
            grows = C.sb(ph, [5, 2 * D], F32, "grows")
            wt = [C.sb(ph, [128, 8, 512], F32, "wada%d" % i) for i in range(2)]
            pvm, pbm = C.psum(1)
            for n in range(12):
                w = wt[n % 2]
                P.dma('sp', [(w.t[:, 0:4, :], w_ada[l, 0:512, n * 512:(n + 1) * 512].rearrange("(a p) n -> p a n", p=128)),
                             (w.t[:, 4:8, :], w_ada[l, 512:1024, n * 512:(n + 1) * 512].rearrange("(a p) n -> p a n", p=128))],
                      [], [w.b])
                for fc in range(4):
                    f = n * 4 + fc
                    P.mm([(pvm[:, 0, f * 5:(f + 1) * 5], w.t[:, kc, fc * 128:(fc + 1) * 128], cT.t[:, kc, :],
                           kc == 0, kc == 7) for kc in range(8)], [w.b, cT.b], pbm)
            P.tt('dve', modT.t[:], pvm[:, 0, 0:240].rearrange("p (a b) -> p a b", b=5),
                 colp.t[:, 16:64].unsqueeze(2).to_broadcast([128, 48, 5]), ALU.add, pbm + [colp.b], [modT.b])
            for (A_, B_, goff, sc, sh) in ((A1, B1, 0, 8, 0), (A2, B2, 8, 32, 24)):
                P.ts('dve', A_.t[:], modT.t[:, sc:sc + 8, :], 1.0, None, ALU.add, None, [modT.b], [A_.b])
                P.tt('dve', A_.t[:], A_.t[:], colp.t[:, goff:goff + 8].unsqueeze(2).to_broadcast([128, 8, 5]),
                     ALU.mult, [A_.b, colp.b], [A_.b])
                P.copy('dve', B_.t[:], modT.t[:, sh:sh + 8, :], [modT.b], [B_.b])
            for half, base in ((0, 16), (1, 40)):
                pv, pb = C.psum(2)
                P.tr([(pv[0:5, kc // 4, (kc % 4) * 128:(kc % 4 + 1) * 128], modT.t[:, base + kc, :], ident.t[:])
                      for kc in range(8)], [modT.b, ident.b], pb)
                P.copy('dve', grows.t[:, half * D:(half + 1) * D].rearrange("p (a b) -> p a b", b=512),
                       pv[0:5, :, :], pb, [grows.b])
            P.dma('sp', [(grows_d[l], grows.t[:])], [grows.b], [C.dbuf(('grows', l))])
            e0 = C.sb(ph, [5, 128], F32, "e0")
            P.dma('sp', [(e0.t[:], c_e5[0])], [], [e0.b])
            for half, gb in ((0, g1bc), (1, g2bc)):
                pv, pb = C.psum(2)
                for j in range(2):
                    P.mm([(pv[:, j, :], e0.t[:], grows.t[:, half * D + j * 512: half * D + (j + 1) * 512], True, True)],
                         [e0.b, grows.b], pb[j:j + 1])
                P.copy('dve', gb.t[:].rearrange("p (a b) -> p a b", b=512), pv, pb, [gb.b])
        P.barrier()

    def rms_rstd(ph_tiles, xt, nblk, junk, ss, rstd):
        for j in range(nblk):
            P.act(junk.t[:], xt.t[:, j, :], AF.Square, [xt.b], [junk.b, ss.b], accum_out=ss.t[:, j:j + 1])
        P.act(rstd.t[:, 0:nblk], ss.t[:, 0:nblk], AF.Sqrt, [ss.b, epsT.b], [rstd.b], scale=1.0 / D, bias=epsT.t[:, 0:1])
        P.op('dve', lambda e: e.reciprocal(out=rstd.t[:, 0:nblk], in_=rstd.t[:, 0:nblk]), [rstd.b], [rstd.b])

    def norm_to_hT(xt, j, rstd, A_, B_, hT, col0, flip):
        P.act(xt.t[:, j, :], xt.t[:, j, :], AF.Copy, [xt.b, rstd.b], [xt.b], scale=rstd.t[:, j:j + 1])
        pv, pb = C.psum(2)
        P.tr([(pv[:, kc // 4, (kc % 4) * 128:(kc % 4 + 1) * 128], xt.t[:, j, kc * 128:(kc + 1) * 128], ident.t[:])
              for kc in range(8)], [xt.b, ident.b], pb)
        for kc in range(8):
            src = pv[:, kc // 4, (kc % 4) * 128:(kc % 4 + 1) * 128]
            dst = hT.t[:, kc, col0:col0 + 128]
            if (kc + flip) % 2 == 0:
                P.act(dst, src, AF.Identity, pb + [A_.b, B_.b], [hT.b], scale=A_.t[:, kc, 0:1], bias=B_.t[:, kc, 0:1])
            else:
                P.ts('dve', dst, src, A_.t[:, kc, 0:1], B_.t[:, kc, 0:1], ALU.mult, ALU.add, pb + [A_.b, B_.b], [hT.b])

    def phase_norm1(l):
        xsrc = x_in if l == 0 else xm_d
        with ExitStack() as ph:
            xts = [C.sb(ph, [128, 4, D], F32, "xt%d" % i) for i in range(2)]
            hTs = [C.sb(ph, [128, 8, 512], BF, "hTo%d" % i) for i in range(2)]
            junk = C.sb(ph, [128, D], F32, "junk")
            ss = C.sb(ph, [128, 4], F32, "ss")
            rstd = C.sb(ph, [128, 4], F32, "rstd")
            for tt in range(L // 512):
                xt, hT = xts[tt % 2], hTs[tt % 2]
                rd = [C.dbuf(('xm', tt * 4 + j)) for j in range(4)] if l > 0 else []
                P.dma('sp', [(xt.t[:], xsrc[tt * 512:(tt + 1) * 512, :].rearrange("(j p) d -> p j d", p=128))], rd, [xt.b])
                rms_rstd(ph, xt, 4, junk, ss, rstd)
                for j in range(4):
                    norm_to_hT(xt, j, rstd, A1, B1, hT, j * 128, j)
                P.dma('pool', [(hT_d[:, :, tt * 512:(tt + 1) * 512].rearrange("k p t -> p k t"), hT.t[:])],
                      [hT.b], [C.dbuf(('hT', tt))])
        P.barrier()

    def load_w_bf16(ph, wdram_rows, ncols, wb, col_off, stg, cnt):
        pass

    def cast_load(dst_ap, dst_buf, src_ap, stg_list, state, ncols):
        i = state[0]
        state[0] += 1
        stg = stg_list[i % len(stg_list)]
        P.dma('sp', [(stg.t[:, 0:ncols], src_ap)], [], [stg.b])
        eng = ('dve', 'act', 'pool')[i % 3] if False else ('dve', 'act')[i % 2]
        P.copy(eng, dst_ap, stg.t[:, 0:ncols], [stg.b], [dst_buf])

    def phase_inproj_T(l):
        groups = [
            (C_Q, 1536, [('q', i) for i in range(6)] + [('k', i) for i in range(6)]),
            (C_X, 2048, [('x', i) for i in range(16)]),
            (C_X + 2048, 2048, [('x', 16 + i) for i in range(16)]),
            (C_G, 2048, [('g', i) for i in range(16)]),
        ]
        with ExitStack() as ph:
            wbs = [C.sb(ph, [128, 8, 2048], BF, "wT%d" % i) for i in range(2)]
            stg = [C.sb(ph, [128, 2048], F32, "stg%d" % i) for i in range(2)]
            hts = [C.sb(ph, [128, 8, 512], BF, "hTi%d" % i) for i in range(2)]
            obs = [C.sb(ph, [128, 512], BF, "ob%d" % i) for i in range(4)]
            xps = [C.sb(ph, [128, 515], BF, "xp%d" % i) for i in range(3)]
            accs = [C.sb(ph, [128, 512], F32, "acc%d" % i) for i in range(2)]
            c3s = [C.sb(ph, [128, 3], F32, "c3%d" % i) for i in range(2)]
            halo = C.sb(ph, [128, 32, 3], BF, "halo")
            dg = C.sb(ph, [128, 16, 4, 128], BF, "dg")
            P.memset('dve', halo.t[:], 0.0, [halo.b])
            st = [0]
            dcnt = [0]
            pend = [None]
            cnt = {'ob': 0, 'xp': 0, 'acc': 0, 'ht': 0}
            for gi, (c0, ncols, chunks) in enumerate(groups):
                wb = wbs[gi % 2]
                for kc in range(8):
                    cast_load(wb.t[:, kc, 0:ncols], wb.b, w_in[l, kc * 128:(kc + 1) * 128, c0:c0 + ncols], stg, st, ncols)
                if DECODE:
                    dec_proj(wb, 0, ncols, c0, accs, dcnt)
                if chunks[0][0] == 'x':
                    for ci, (kind, idx) in enumerate(chunks):
                        for j in range(4):
                            P.ts(('dve', 'pool')[(ci * 4 + j) % 2], dg.t[:, ci, j, :], ident.t[:], cwT.t[:, j, idx:idx + 1], None,
                                 ALU.mult, None, [ident.b, cwT.b], [dg.b])
                for tt in range(L // 512):
                    ht = hts[cnt['ht'] % 2]
                    cnt['ht'] += 1
                    P.dma('sp', [(ht.t[:], hT_d[:, :, tt * 512:(tt + 1) * 512].rearrange("k p t -> p k t"))],
                          [C.dbuf(('hT', tt))], [ht.b])
                    for ci, (kind, idx) in enumerate(chunks):
                        pv, pb = C.psum(1)
                        P.mm([(pv[:, 0, :], wb.t[:, kc, ci * 128:(ci + 1) * 128], ht.t[:, kc, :], kc == 0, kc == 7)
                              for kc in range(8)], [wb.b, ht.b], pb)
                        ob = obs[cnt['ob'] % 4]
                        cnt['ob'] += 1
                        if kind in ('q', 'k'):
                            P.copy('act', ob.t[:], pv[:, 0, :], pb, [ob.b])
                            dst = (qT_d if kind == 'q' else kT_d)[idx * 128:(idx + 1) * 128, tt * 512:(tt + 1) * 512]
                            P.dma('act', [(dst, ob.t[:])], [ob.b], [C.dbuf((kind + 'T', idx, tt))])
                        elif kind == 'g':
                            P.act(ob.t[:], pv[:, 0, :], AF.Sigmoid, pb, [ob.b])
                            P.dma('act', [(gT_d[idx * 128:(idx + 1) * 128, tt * 512:(tt + 1) * 512], ob.t[:])],
                                  [ob.b], [C.dbuf(('gT', idx, tt))])
                        else:
                            xp = xps[cnt['xp'] % 3]
                            cnt['xp'] += 1
                            P.copy('act', xp.t[:, 3:515], pv[:, 0, :], pb, [xp.b])
                            if tt == L // 512 - 1:
                                c3 = c3s[idx % 2]
                                P.copy('act', c3.t[:], pv[:, 0, 509:512], pb, [c3.b])
                                P.dma('pool', [(convp[l, :, idx * 128:(idx + 1) * 128].rearrange("t p -> p t"), c3.t[:])],
                                      [c3.b], [C.dbuf(('convp', l, idx))], slow=True)
                            P.copy('pool', xp.t[:, 0:3], halo.t[:, idx, :], [halo.b], [xp.b])
                            P.copy('pool', halo.t[:, idx, :], xp.t[:, 512:515], [xp.b], [halo.b])
                            if pend[0] is not None:
                                pend[0]()

                            def stage2(xp=xp, ob=ob, ci=ci, idx=idx, tt=tt):
                                pc, pcb = C.psum(1)
                                P.mm([(pc[:, 0, :], dg.t[:, ci, j, :], xp.t[:, j:j + 512], j == 0, j == 3) for j in range(4)],
                                     [dg.b, xp.b], pcb)
                                P.act(ob.t[:], pc[:, 0, :], AF.Silu, pcb + [colp.b], [ob.b], bias=colp.t[:, 64 + idx:65 + idx])
                                P.dma('act', [(xbcT_d[idx * 128:(idx + 1) * 128, tt * 512:(tt + 1) * 512], ob.t[:])],
                                      [ob.b], [C.dbuf(('xbcT', idx, tt))])
                            pend[0] = stage2
                if pend[0] is not None:
                    pend[0]()
                    pend[0] = None
        P.barrier()


    def phase_inproj_tok(l):
        NW = 3616
        with ExitStack() as ph:
            wb = C.sb(ph, [128, 8, NW], BF, "wtok")
            stg = [C.sb(ph, [128, 2048], F32, "stg%d" % i) for i in range(2)]
            hts = [C.sb(ph, [128, 8, 512], BF, "hTk%d" % i) for i in range(2)]
            vts = [C.sb(ph, [128, QKV], BF, "vt%d" % i) for i in range(2)]
            zts = [C.sb(ph, [128, DIN], BF, "zt%d" % i) for i in range(2)]
            dts = [C.sb(ph, [128, 64], F32, "dt%d" % i) for i in range(2)]
            kvf = [C.sb(ph, [128, 2 * QKV], F32, "kvf%d" % i) for i in range(2)]
            st = [0]
            for kc in range(8):
                rows = slice(kc * 128, (kc + 1) * 128)
                cast_load(wb.t[:, kc, 0:1536], wb.b, w_in[l, rows, C_K:C_K + 1536], stg, st, 1536)
                cast_load(wb.t[:, kc, 1536:3584], wb.b, w_in[l, rows, C_Z:C_Z + 2048], stg, st, 2048)
                cast_load(wb.t[:, kc, 3584:3616], wb.b, w_in[l, rows, C_DT:C_DT + 32], stg, st, 32)
            if DECODE:
                dcnt = [0]
                dec_proj(wb, 768, 768, C_V, kvf, dcnt)
                dec_proj(wb, 1536, 2048, C_Z, kvf, dcnt)
                dec_proj(wb, 3584, 32, C_DT, kvf, dcnt)
            for tt in range(L // 512):
                ht = hts[tt % 2]
                P.dma('sp', [(ht.t[:], hT_d[:, :, tt * 512:(tt + 1) * 512].rearrange("k p t -> p k t"))],
                      [C.dbuf(('hT', tt))], [ht.b])
                for j in range(4):
                    tb = tt * 4 + j
                    tok0 = tb * 128
                    vt, zt, dtt = vts[tb % 2], zts[tb % 2], dts[tb % 2]

                    def proj(c0, n, out_ap_fn):
                        pv, pb = C.psum(1)
                        P.mm([(pv[:, 0, 0:n], ht.t[:, kc, j * 128:(j + 1) * 128], wb.t[:, kc, c0:c0 + n], kc == 0, kc == 7)
                              for kc in range(8)], [wb.b, ht.b], pb)
                        return pv[:, 0, 0:n], pb
                    last = tok0 >= L - 2048 and 'kvp' not in SKIP
                    kf = kvf[tb % 2]
                    for (c0, n) in ((768, 512), (1280, 256)):
                        pa, pb = proj(c0, n, None)
                        P.copy('act', vt.t[:, c0 - 768:c0 - 768 + n], pa, pb, [vt.b])
                        if last:
                            P.copy(KVENG, kf.t[:, c0:c0 + n], pa, pb, [kf.b] + (pb if 'pbx' in SKIP else []))
                    P.dma('act', [(v_d[tok0:tok0 + 128, :], vt.t[:])], [vt.b], [C.dbuf(('v', tb))])
                    if last:
                        for (c0, n) in ((0, 512), (512, 256)):
                            pa, pb = proj(c0, n, None)
                            P.copy(KVENG, kf.t[:, c0:c0 + n], pa, pb, [kf.b] + (pb if 'pbx' in SKIP else []))
                        for g in range(3):
                            r0 = tok0 - (L - WINS[g])
                            if r0 < 0:
                                continue
                            if 'kvpdma' in SKIP:
                                continue
                            P.dma('sp', [(kvp[g][l, r0:r0 + 128, 0:256], kf.t[:, g * 256:(g + 1) * 256]),
                                         (kvp[g][l, r0:r0 + 128, 256:512], kf.t[:, 768 + g * 256:768 + (g + 1) * 256])],
                                  [kf.b], [C.dbuf(('kvp', l, tb, g))])
                    for q4 in range(4):
                        pa, pb = proj(1536 + q4 * 512, 512, None)
                        P.act(zt.t[:, q4 * 512:(q4 + 1) * 512], pa, AF.Silu, pb, [zt.b])
                    P.dma('act', [(zs_d[tok0:tok0 + 128, :], zt.t[:])], [zt.b], [C.dbuf(('zs', tb))])
                    if 'dt' in SKIP:
                        continue
                    pa, pb = proj(3584, 32, None)
                    P.tt('dve', dtt.t[:, 0:32], pa, dtb_bc.t[:], ALU.add, pb + [dtb_bc.b], [dtt.b])
                    P.act(dtt.t[:, 0:32], dtt.t[:, 0:32], AF.Exp, [dtt.b], [dtt.b])
                    P.act(dtt.t[:, 0:32], dtt.t[:, 0:32], AF.Ln, [dtt.b], [dtt.b], bias=1.0)
                    P.tt('dve', dtt.t[:, 32:64], dtt.t[:, 0:32], A_bc.t[:], ALU.mult, [dtt.b, A_bc.b], [dtt.b])
                    P.dma('act', [(dta_d[tok0:tok0 + 128, :], dtt.t[:])], [dtt.b], [C.dbuf(('dta', tb))])
        P.barrier()

    def phase_attn(l):
        NSB = L // 2048
        with ExitStack() as ph:
            qA = [[C.sb(ph, [128, 2048], BF, "qA%d%d" % (i, p)) for p in range(2)] for i in range(2)]
            qB = [[C.sb(ph, [128, 2048], BF, "qB%d%d" % (i, p)) for p in range(2)] for i in range(2)]
            kTs = [C.sb(ph, [128, 2, 4096], BF, "kT%d" % i) for i in range(2)]
            vts = [C.sb(ph, [128, 32, 4, 65], BF, "vtl%d" % i) for i in range(2)]
            pTs = [C.sb(ph, [128, 4, 2, 128], BF, "pT%d" % i) for i in range(2)]
            uos = [C.sb(ph, [128, 260], F32, "uo%d" % i) for i in range(4)]
            for i in range(2):
                for p in range(2):
                    P.memset('dve', qA[i][p].t[64:128, :], 0.0, [qA[i][p].b])
                    P.memset('dve', qB[i][p].t[0:64, :], 0.0, [qB[i][p].b])
                P.memset('dve', vts[i].t[:, :, :, 64:65], 1.0, [vts[i].b])
            it = 0
            nq = 0
            for g in range(3):
                dil = DILS[g]
                nj = 16 // dil
                for sb in range(NSB):
                    bi = it % 2
                    it += 1
                    t0 = sb * 2048
                    kT, vt = kTs[bi], vts[bi]
                    for p in range(2):
                        r0 = (g * 2 + p) * 128
                        rdq = [C.dbuf(('qT', g * 2 + p, t)) for t in range(sb * 4, sb * 4 + 4)]
                        P.dma('sp', [(qA[bi][p].t[0:64, :], qT_d[r0:r0 + 64, t0:t0 + 2048])], rdq, [qA[bi][p].b])
                        P.dma('sp', [(qB[bi][p].t[64:128, :], qT_d[r0 + 64:r0 + 128, t0:t0 + 2048])], rdq, [qB[bi][p].b])
                        if sb > 0:
                            rdk = [C.dbuf(('kT', g * 2 + p, t)) for t in range(sb * 4 - 4, sb * 4 + 4)]
                            P.dma('sp', [(kT.t[:, p, :], kT_d[r0:r0 + 128, t0 - 2048:t0 + 2048])], rdk, [kT.b])
                        else:
                            rdk = [C.dbuf(('kT', g * 2 + p, t)) for t in range(0, 4)]
                            P.dma('sp', [(kT.t[:, p, 2048:4096], kT_d[r0:r0 + 128, 0:2048])], rdk, [kT.b])
                    rdv = [C.dbuf(('v', t)) for t in range(max(0, sb * 16 - 16), sb * 16 + 16)]
                    pairs = []
                    for r in range(dil):
                        for kl in range(nj + 1):
                            kbi = sb * nj - 1 + kl
                            if kbi < 0:
                                continue
                            tok = r + dil * kbi * 128
                            src = v_d[tok:tok + dil * 127 + 1:dil, g * 256:(g + 1) * 256].rearrange("t (h d) -> t h d", h=4)
                            pairs.append((vt.t[:, r * (nj + 1) + kl, :, 0:64], src))
                    for i in range(0, len(pairs), 8):
                        P.dma('sp', pairs[i:i + 8], rdv, [vt.b])
                    for r in range(dil):
                        for jl in range(nj):
                            jb = sb * nj + jl
                            has_prev = jb > 0
                            qs = slice(r + dil * jl * 128, r + dil * jl * 128 + dil * 127 + 1, dil)
                            kcur = slice(2048 + qs.start, 2048 + qs.stop, dil)
                            kprv = slice(2048 + qs.start - 128 * dil, 2048 + qs.stop - 128 * dil, dil)
                            pv, pb = C.psum(2)
                            S = pv.rearrange("p a (h q) -> p (a h) q", q=128)
                            items = []
                            for h in range(4):
                                qz = (qA if h % 2 == 0 else qB)[bi][h // 2]
                                for kb in range(2):
                                    if kb == 1 and not has_prev:
                                        continue
                                    ks = kcur if kb == 0 else kprv
                                    items.append((S[:, h * 2 + kb, :], kT.t[:, h // 2, ks], qz.t[:, qs], True, False))
                                    items.append((S[:, h * 2 + kb, :], antib.t[:], hank[g].t[:, h, kb, :], False, True))
                            P.mm(items, [kT.b, qA[bi][0].b, qA[bi][1].b, qB[bi][0].b, qB[bi][1].b, antib.b, hank[g].b], pb)
                            pT = pTs[nq % 2]
                            if has_prev:
                                P.act(pT.t[:].rearrange("p h k q -> p (h k) q"), S, AF.Exp, pb, [pT.b], scale=0.125)
                            else:
                                P.act(pT.t[:, :, 0, :], pv.rearrange("p a (h q) -> p (a h) q", q=256)[:, :, 0:128],
                                      AF.Exp, pb, [pT.b], scale=0.125)
                            pu, pub = C.psum(1)
                            items = []
                            for h in range(4):
                                kbs = (0, 1) if has_prev else (0,)
                                for n_, kb in enumerate(kbs):
                                    vidx = r * (nj + 1) + jl + (1 - kb)
                                    items.append((pu[:, 0, h * 65:(h + 1) * 65], pT.t[:, h, kb, :], vt.t[:, vidx, h, :],
                                                  n_ == 0, n_ == len(kbs) - 1))
                            P.mm(items, [pT.b, vt.b], pub)
                            uo = uos[nq % 4]
                            P.copy('dve', uo.t[:], pu[:, 0, 0:260], pub, [uo.b])
                            tok = t0 + qs.start
                            P.dma('pool', [(U_d[g][tok:tok + dil * 127 + 1:dil, :], uo.t[:])], [uo.b],
                                  [C.dbuf(('U', g, sb, r, jl))])
                            nq += 1
        P.barrier()


    def phase_ssd(l):
        with ExitStack() as ph:
            xin = [C.sb(ph, [128, 32, 128], BF, "xin%d" % i) for i in range(2)]
            dta = [C.sb(ph, [128, 64], F32, "dta%d" % i) for i in range(2)]
            zs = [C.sb(ph, [128, DIN], BF, "zs%d" % i) for i in range(2)]
            xdt_ = [C.sb(ph, [128, DIN], BF, "xdt%d" % i) for i in range(2)]
            xtok_ = [C.sb(ph, [128, DIN], BF, "xtok%d" % i) for i in range(2)]
            btok_ = [C.sb(ph, [128, 1024], BF, "btok%d" % i) for i in range(2)]
            sm_ = [C.sb(ph, [128, 160], F32, "sm%d" % i) for i in range(2)]
            rhsA = C.sb(ph, [128, NH, 128], F32, "rhsA")
            eseg = C.sb(ph, [128, NH, 128], F32, "eseg")
            cbm = C.sb(ph, [128, 8, 128], F32, "cbm")
            mT_ = [C.sb(ph, [128, NH, 128], BF, "mT%d" % i) for i in range(2)]
            yt = C.sb(ph, [128, DIN], F32, "yt")
            y2 = C.sb(ph, [128, DIN], F32, "y2")
            y3 = C.sb(ph, [128, DIN], F32, "y3")
            ynb = C.sb(ph, [128, DIN], BF, "ynb")
            xdtd = C.sb(ph, [128, DIN], BF, "xdtd")
            state = C.sb(ph, [128, DIN], F32, "state")
            stbf = C.sb(ph, [128, DIN], BF, "stbf")
            ssq = C.sb(ph, [128, 8], F32, "ssq")
            junk = C.sb(ph, [128, 256], F32, "junk")
            sTo = [C.sb(ph, [128, 16, 128], BF, "sTo%d" % i) for i in range(2)]
            sng_bc = C.sb(ph, [128, DIN], F32, "sng")
            P.dma('sp', [(sng_bc.t[:], ssd_norm_g[l:l + 1, :].to_broadcast([128, DIN]))], [], [sng_bc.b])
            P.memset('dve', state.t[:], 0.0, [state.b])
            P.memset('dve', stbf.t[:], 0.0, [stbf.b])
            def front(c):
                tok0 = c * 128
                xi, da, z = xin[c % 2], dta[c % 2], zs[c % 2]
                xdt, xtok, btok, sm, mT = xdt_[c % 2], xtok_[c % 2], btok_[c % 2], sm_[c % 2], mT_[c % 2]
                P.dma('sp', [(xi.t[:], xbcT_d[:, tok0:tok0 + 128].rearrange("(a p) t -> p a t", p=128)),
                             (da.t[:], dta_d[tok0:tok0 + 128, :]), (z.t[:], zs_d[tok0:tok0 + 128, :])],
                      [], [xi.b, da.b, z.b])
                dt_ap = da.t[:, 0:32]
                a_ap = da.t[:, 32:64]
                pv, pb = C.psum(2)
                pvb = C.ps[:, C.psb.index(pb[0]):C.psb.index(pb[0]) + 2, :].bitcast(BF)
                P.tr([(pvb[:, cc // 8, (cc % 8) * 128:(cc % 8 + 1) * 128], xi.t[:, cc, :], identb.t[:]) for cc in range(16)],
                     [xi.b, identb.b], pb)
                P.copy('act', xtok.t[:].rearrange("p (a b) -> p a b", b=1024), pvb, pb, [xtok.b])
                P.tt('dve', xdt.t[:].rearrange("p (h d) -> p h d", d=64), xtok.t[:].rearrange("p (h d) -> p h d", d=64),
                     dt_ap.unsqueeze(2).to_broadcast([128, NH, 64]), ALU.mult, [xtok.b, da.b], [xdt.b])
                pv, pb = C.psum(1)
                pvb = C.ps[:, C.psb.index(pb[0]):C.psb.index(pb[0]) + 1, :].bitcast(BF)
                P.tr([(pvb[:, 0, g * 128:(g + 1) * 128], xi.t[:, 16 + g, :], identb.t[:]) for g in range(8)],
                     [xi.b, identb.b], pb)
                P.copy('act', btok.t[:], pvb[:, 0, :], pb, [btok.b])
                pv, pb = C.psum(1)
                P.mm([(pv[:, 0, 0:32], triu.t[:], a_ap, True, True), (pv[:, 0, 32:64], ones.t[:], a_ap, True, True)],
                     [triu.b, ones.b, da.b], pb)
                P.copy('dve', sm.t[:, 0:32], pv[:, 0, 0:32], pb, [sm.b])
                P.copy('dve', sm.t[:, 128:160], pv[:, 0, 32:64], pb, [sm.b])
                P.act(sm.t[:, 32:64], sm.t[:, 0:32], AF.Exp, [sm.b], [sm.b])
                P.tt('dve', sm.t[:, 64:96], sm.t[:, 128:160], sm.t[:, 0:32], ALU.subtract, [sm.b], [sm.b])
                P.act(sm.t[:, 64:96], sm.t[:, 64:96], AF.Exp, [sm.b], [sm.b])
                P.act(sm.t[:, 96:128], sm.t[:, 128:160], AF.Exp, [sm.b], [sm.b])
                P.tt('pool', rhsA.t[:], triu.t[:].unsqueeze(1).to_broadcast([128, NH, 128]),
                     a_ap.unsqueeze(2).to_broadcast([128, NH, 128]), ALU.mult, [triu.b, da.b], [rhsA.b])
                for q4 in range(8):
                    pv, pb = C.psum(1)
                    P.mm([(pv[:, 0, :], su.t[:], rhsA.t[:, q4 * 4:(q4 + 1) * 4, :].rearrange("p h l -> p (h l)"), True, True)],
                         [su.b, rhsA.b], pb)
                    P.act(eseg.t[:, q4 * 4:(q4 + 1) * 4, :].rearrange("p h l -> p (h l)"), pv[:, 0, :], AF.Exp, pb, [eseg.b])
                pv, pb = C.psum(2)
                P.mm([(pv[:, g // 4, (g % 4) * 128:(g % 4 + 1) * 128], xi.t[:, 16 + g, :], xi.t[:, 24 + g, :], True, True)
                      for g in range(8)], [xi.b], pb)
                P.tt('dve', cbm.t[:], pv.rearrange("p a (g l) -> p (a g) l", l=128),
                     triu.t[:].unsqueeze(1).to_broadcast([128, 8, 128]), ALU.mult, pb + [triu.b], [cbm.b])
                P.tt('dve', mT.t[:].rearrange("p (g e) l -> p g e l", e=4), eseg.t[:].rearrange("p (g e) l -> p g e l", e=4),
                     cbm.t[:].unsqueeze(2).to_broadcast([128, 8, 4, 128]), ALU.mult, [eseg.b, cbm.b], [mT.b])
            def back(c):
                tok0 = c * 128
                xi, da, z = xin[c % 2], dta[c % 2], zs[c % 2]
                xdt, xtok, btok, sm, mT = xdt_[c % 2], xtok_[c % 2], btok_[c % 2], sm_[c % 2], mT_[c % 2]
                pvy, pby = C.psum(4)
                P.mm([(pvy[:, h // 8, (h % 8) * 64:(h % 8 + 1) * 64], mT.t[:, h, :], xdt.t[:, h * 64:(h + 1) * 64], True, True)
                      for h in range(NH)], [mT.b, xdt.b], pby)
                pvo, pbo = C.psum(4)
                P.mm([(pvo[:, g // 2, (g % 2) * 256:(g % 2 + 1) * 256], xi.t[:, 24 + g, :], stbf.t[:, g * 256:(g + 1) * 256], True, True)
                      for g in range(8)], [xi.b, stbf.b], pbo)
                P.tt('dve', y2.t[:].rearrange("p (h d) -> p h d", d=64), pvo.rearrange("p a (h d) -> p (a h) d", d=64),
                     sm.t[:, 32:64].unsqueeze(2).to_broadcast([128, NH, 64]), ALU.mult, pbo + [sm.b], [y2.b])
                P.tt('dve', yt.t[:].rearrange("p (a b) -> p a b", b=512), pvy, y2.t[:].rearrange("p (a b) -> p a b", b=512),
                     ALU.add, pby + [y2.b], [yt.b])
                P.tt('pool', y3.t[:].rearrange("p (h d) -> p h d", d=64), xtok.t[:].rearrange("p (h d) -> p h d", d=64),
                     D_bc.t[:].unsqueeze(2).to_broadcast([128, NH, 64]), ALU.mult, [xtok.b, D_bc.b], [y3.b])
                P.tt('dve', yt.t[:], yt.t[:], y3.t[:], ALU.add, [yt.b, y3.b], [yt.b])
                P.tt('dve', yt.t[:], yt.t[:], z.t[:], ALU.mult, [yt.b, z.b], [yt.b])
                for g in range(8):
                    P.act(junk.t[:], yt.t[:, g * 256:(g + 1) * 256], AF.Square, [yt.b], [junk.b, ssq.b],
                          accum_out=ssq.t[:, g:g + 1])
                P.act(ssq.t[:], ssq.t[:], AF.Sqrt, [ssq.b, epsT.b], [ssq.b], scale=1.0 / 256, bias=epsT.t[:, 0:1])
                P.op('dve', lambda e: e.reciprocal(out=ssq.t[:], in_=ssq.t[:]), [ssq.b], [ssq.b])
                P.tt('dve', yt.t[:].rearrange("p (g d) -> p g d", d=256), yt.t[:].rearrange("p (g d) -> p g d", d=256),
                     ssq.t[:].unsqueeze(2).to_broadcast([128, 8, 256]), ALU.mult, [yt.b, ssq.b], [yt.b])
                P.tt('dve', ynb.t[:], yt.t[:], sng_bc.t[:], ALU.mult, [yt.b, sng_bc.b], [ynb.b])
                pv, pb = C.psum(2)
                pvb = C.ps[:, C.psb.index(pb[0]):C.psb.index(pb[0]) + 2, :].bitcast(BF)
                P.tr([(pvb[:, cc // 8, (cc % 8) * 128:(cc % 8 + 1) * 128], ynb.t[:, cc * 128:(cc + 1) * 128], identb.t[:])
                      for cc in range(16)], [ynb.b, identb.b], pb)
                so = sTo[c % 2]
                P.copy('act', so.t[:].rearrange("p (a b) l -> p a (b l)", a=2), pvb, pb, [so.b])
                P.dma('pool', [(sT_d[:, tok0:tok0 + 128].rearrange("(a p) t -> p a t", p=128), so.t[:])], [so.b],
                      [C.dbuf(('sT', c))])
                P.tt('pool', xdtd.t[:].rearrange("p (h d) -> p h d", d=64), xdt.t[:].rearrange("p (h d) -> p h d", d=64),
                     sm.t[:, 64:96].unsqueeze(2).to_broadcast([128, NH, 64]), ALU.mult, [xdt.b, sm.b], [xdtd.b])
                pvs, pbs = C.psum(4)
                P.mm([(pvs[:, g // 2, (g % 2) * 256:(g % 2 + 1) * 256], btok.t[:, g * 128:(g + 1) * 128],
                       xdtd.t[:, g * 256:(g + 1) * 256], True, True) for g in range(8)], [btok.b, xdtd.b], pbs)
                P.tt('dve', state.t[:].rearrange("p (h d) -> p h d", d=64), state.t[:].rearrange("p (h d) -> p h d", d=64),
                     sm.t[:, 96:128].unsqueeze(2).to_broadcast([128, NH, 64]), ALU.mult, [state.b, sm.b], [state.b])
                P.tt('dve', state.t[:].rearrange("p (a b) -> p a b", b=512), pvs, state.t[:].rearrange("p (a b) -> p a b", b=512),
                     ALU.add, pbs + [state.b], [state.b])
                P.copy('act', stbf.t[:], state.t[:], [state.b], [stbf.b])

            front(0)
            for c in range(NB):
                if c + 1 < NB:
                    front(c + 1)
                back(c)
            for cc in range(16):
                pv, pb = C.psum(1)
                P.tr([(pv[:, 0, 0:128], state.t[:, cc * 128:(cc + 1) * 128], ident.t[:])], [state.b, ident.b], pb)
                o_ = sTo[cc % 2]
                of = yt
                P.copy('dve', yt.t[:, cc * 128:(cc + 1) * 128], pv[:, 0, 0:128], pb, [yt.b])
            P.dma('pool', [(ssmp[l].rearrange("(a p) n -> p a n", p=128), yt.t[:].rearrange("p (a n) -> p a n", n=128))],
                  [yt.b], [C.dbuf(('ssmp', l))])
        P.barrier()


    def phase_out(l):
        with ExitStack() as ph:
            woa = C.sb(ph, [128, 2, D], BF, "woa")
            wos = C.sb(ph, [128, 16, D], BF, "wos")
            wo = C.sb(ph, [128, 8, D], BF, "wo")
            stg = [C.sb(ph, [128, 2048], F32, "stg%d" % i) for i in range(2)]
            st = [0]
            for kc in range(2):
                cast_load(woa.t[:, kc, :], woa.b, w_o_attn[l, kc * 128:(kc + 1) * 128, :], stg, st, D)
            for kc in range(16):
                cast_load(wos.t[:, kc, :], wos.b, w_o_ssd[l, kc * 128:(kc + 1) * 128, :], stg, st, D)
            for kc in range(8):
                cast_load(wo.t[:, kc, :], wo.b, w_out[l, kc * 128:(kc + 1) * 128, :], stg, st, D)
            us = [C.sb(ph, [128, 3, 260], F32, "us%d" % i) for i in range(2)]
            rz = C.sb(ph, [128, 4], F32, "rz")
            att = C.sb(ph, [128, 256], F32, "att")
            aT = [C.sb(ph, [128, 2, 512], BF, "aT%d" % i) for i in range(2)]
            sT = [C.sb(ph, [128, 16, 512], BF, "sT%d" % i) for i in range(1)]
            gT = [C.sb(ph, [128, 16, 512], BF, "gT%d" % i) for i in range(1)]
            mg = C.sb(ph, [128, 8, 512], BF, "mg")
            t1 = [C.sb(ph, [128, 512], F32, "t1%d" % i) for i in range(2)]
            t2 = [C.sb(ph, [128, 512], F32, "t2%d" % i) for i in range(2)]
            xts = [C.sb(ph, [128, 4, D], F32, "xo%d" % i) for i in range(2)]
            xsrc = x_in if l == 0 else xm_d
            if DECODE:
                dec_out(ph, l, woa, wos, wo, xts[1], t1)
            for tt in range(L // 512):
                a_, s_, g_, xt = aT[tt % 2], sT[0], gT[0], xts[tt % 2]
                ts_ = slice(tt * 512, (tt + 1) * 512)
                P.dma('sp', [(s_.t[:], sT_d[:, ts_].rearrange("(a p) t -> p a t", p=128)),
                             (g_.t[:], gT_d[:, ts_].rearrange("(a p) t -> p a t", p=128)),
                             (xt.t[:], xsrc[ts_, :].rearrange("(j p) d -> p j d", p=128))], [], [s_.b, g_.b, xt.b])
                for j in range(4):
                    tok0 = tt * 512 + j * 128
                    u = us[j % 2]
                    P.dma('sp', [(u.t[:, g, :], U_d[g][tok0:tok0 + 128, :]) for g in range(3)], [], [u.b])
                    P.tt('dve', u.t[:, 0, :], u.t[:, 0, :], u.t[:, 1, :], ALU.add, [u.b], [u.b])
                    P.tt('dve', u.t[:, 0, :], u.t[:, 0, :], u.t[:, 2, :], ALU.add, [u.b], [u.b])
                    uv = u.t[:, 0, :].rearrange("p (h d) -> p h d", d=65)
                    P.op('dve', lambda e, uv=uv: e.reciprocal(out=rz.t[:].unsqueeze(2), in_=uv[:, :, 64:65]), [u.b], [rz.b])
                    P.tt('dve', att.t[:].rearrange("p (h d) -> p h d", d=64), uv[:, :, 0:64],
                         rz.t[:].unsqueeze(2).to_broadcast([128, 4, 64]), ALU.mult, [u.b, rz.b], [att.b])
                    pv, pb = C.psum(1)
                    P.tr([(pv[:, 0, cc * 128:(cc + 1) * 128], att.t[:, cc * 128:(cc + 1) * 128], ident.t[:]) for cc in range(2)],
                         [att.b, ident.b], pb)
                    P.copy('act', a_.t[:, :, j * 128:(j + 1) * 128], pv[:, 0, 0:256].rearrange("p (c t) -> p c t", t=128),
                           pb, [a_.b])
                for fc in range(8):
                    fs = slice(fc * 128, (fc + 1) * 128)
                    pa, pab = C.psum(1)
                    P.mm([(pa[:, 0, :], woa.t[:, kc, fs], a_.t[:, kc, :], kc == 0, kc == 1) for kc in range(2)],
                         [woa.b, a_.b], pab)
                    pbs_, pbb = C.psum(1)
                    P.mm([(pbs_[:, 0, :], wos.t[:, kc, fs], s_.t[:, kc, :], kc == 0, kc == 15) for kc in range(16)],
                         [wos.b, s_.b], pbb)
                    P.tt('dve', t1[fc % 2].t[:], pa[:, 0, :], g_.t[:, fc, :], ALU.mult, pab + [g_.b], [t1[fc % 2].b])
                    P.tt('dve', t2[fc % 2].t[:], pbs_[:, 0, :], g_.t[:, 8 + fc, :], ALU.mult, pbb + [g_.b], [t2[fc % 2].b])
                    P.tt('dve', mg.t[:, fc, :], t1[fc % 2].t[:], t2[fc % 2].t[:], ALU.add, [t1[fc % 2].b, t2[fc % 2].b], [mg.b])
                for j in range(4):
                    for n in range(2):
                        ns = slice(n * 512, (n + 1) * 512)
                        pv, pb = C.psum(1)
                        P.mm([(pv[:, 0, :], mg.t[:, kc, j * 128:(j + 1) * 128], wo.t[:, kc, ns], kc == 0, kc == 7)
                              for kc in range(8)], [mg.b, wo.b], pb)
                        P.tt('dve', t1[n].t[:], pv[:, 0, :], g1bc.t[:, ns], ALU.mult, pb + [g1bc.b], [t1[n].b])
                        P.tt('dve', xt.t[:, j, ns], xt.t[:, j, ns], t1[n].t[:], ALU.add, [xt.b, t1[n].b], [xt.b])
                P.dma('pool', [(x1_d[ts_, :].rearrange("(j p) d -> p j d", p=128), xt.t[:])], [xt.b], [C.dbuf(('x1', tt))])
        P.barrier()

    def phase_mlp(l):
        last = (l == DEPTH - 1)
        hidTd = C.sb(es_layer[0], [128, 32, NS], BF, "hidTd") if DECODE else None
        with ExitStack() as ph:
            wu = C.sb(ph, [128, 8, DFF], BF, "wu")
            stg = [C.sb(ph, [128, 1024], F32, "stg%d" % i) for i in range(2)]
            st = [0]
            for kc in range(8):
                for hf in range(4):
                    cast_load(wu.t[:, kc, hf * 1024:(hf + 1) * 1024], wu.b,
                              w_up[l, kc * 128:(kc + 1) * 128, hf * 1024:(hf + 1) * 1024], stg, st, 1024)
            xts = [C.sb(ph, [128, 4, D], F32, "xm%d" % i) for i in range(2)]
            h2 = [C.sb(ph, [128, 8, 512], BF, "h2%d" % i) for i in range(2)]
            rl = [C.sb(ph, [128, 512], F32, "rl%d" % i) for i in range(3)]
            ho = [C.sb(ph, [128, 512], BF, "ho%d" % i) for i in range(4)]
            junk = C.sb(ph, [128, D], BF, "junk")
            sss = [C.sb(ph, [128, 4], F32, "ss%d" % i) for i in range(2)]
            rstds = [C.sb(ph, [128, 4], F32, "rstd%d" % i) for i in range(2)]
            if DECODE:
                h2dT = C.sb(ph, [128, 8, NS], BF, "h2dT")
                dec_norm_T(ph, xd.t[:], xd.b, A2, B2, h2dT, xts[1])
                for n in range(8):
                    pv, pb = C.psum(1)
                    P.mm([(pv[0:NS, 0, :], h2dT.t[:, kc, :], wu.t[:, kc, n * 512:(n + 1) * 512], kc == 0, kc == 7)
                          for kc in range(8)], [h2dT.b, wu.b], pb)
                    P.act(rl[0].t[0:NS, :], pv[0:NS, 0, :], AF.Relu, pb, [rl[0].b])
                    P.tt('dve', rl[1].t[0:NS, :], rl[0].t[0:NS, :], pv[0:NS, 0, :], ALU.mult, pb + [rl[0].b], [rl[1].b])
                    pt, ptb = C.psum(1)
                    P.tr([(pt[:, 0, i * NS:(i + 1) * NS], rl[1].t[0:NS, i * 128:(i + 1) * 128], ident.t[0:NS, 0:NS])
                          for i in range(4)], [rl[1].b, ident.b], ptb)
                    P.copy('act', hidTd.t[:, n * 4:(n + 1) * 4, :].rearrange("p c b -> p (c b)"), pt[:, 0, 0:4 * NS], ptb,
                           [hidTd.b])
            nr = 0
            for tt in range(L // 512):
                xt, h2t, ss, rstd = xts[tt % 2], h2[tt % 2], sss[tt % 2], rstds[tt % 2]
                ts_ = slice(tt * 512, (tt + 1) * 512)
                P.dma('sp', [(xt.t[:], x1_d[ts_, :].rearrange("(j p) d -> p j d", p=128))], [], [xt.b])
                rms_rstd(ph, xt, 4, junk, ss, rstd)
                for j in range(4):
                    norm_to_hT(xt, j, rstd, A2, B2, h2t, j * 128, j)
                for hc in range(32):
                    pv, pb = C.psum(1)
                    P.mm([(pv[:, 0, :], wu.t[:, kc, hc * 128:(hc + 1) * 128], h2t.t[:, kc, :], kc == 0, kc == 7)
                          for kc in range(8)], [wu.b, h2t.b], pb)
                    r_ = rl[nr % 3]
                    o_ = ho[nr % 4]
                    nr += 1
                    P.act(r_.t[:], pv[:, 0, :], AF.Relu, pb, [r_.b])
                    P.tt('dve', o_.t[:], r_.t[:], pv[:, 0, :], ALU.mult, pb + [r_.b], [o_.b])
                    P.dma('sp', [(hidT_d[hc * 128:(hc + 1) * 128, ts_], o_.t[:])], [o_.b], [Buf()])
        P.barrier()
        with ExitStack() as ph:
            wd = C.sb(ph, [128, 32, D], BF, "wd")
            stg = [C.sb(ph, [128, 512], F32, "stg%d" % i) for i in range(2)]
            st = [0]
            for kc in range(32):
                for hf in range(2):
                    cast_load(wd.t[:, kc, hf * 512:(hf + 1) * 512], wd.b,
                              w_down[l, kc * 128:(kc + 1) * 128, hf * 512:(hf + 1) * 512], stg, st, 512)
            xts = [C.sb(ph, [128, 4, D], F32, "xm%d" % i) for i in range(2)]
            hts = [C.sb(ph, [128, 32, 512], BF, "hid%d" % i) for i in range(2)]
            t1 = [C.sb(ph, [128, 512], F32, "tm%d" % i) for i in range(3)]
            junk = C.sb(ph, [128, D], BF, "junk")
            ss = C.sb(ph, [128, 4], F32, "ss")
            rstd = C.sb(ph, [128, 4], F32, "rstd")
            if DECODE:
                g2d = TB(xts[1].t[0:NS, 0, :])
                g2d.b = xts[1].b
                P.dma('sp', [(g2d.t[:], grows_d[l, 1:1 + NS, D:2 * D])], [], [g2d.b])
                for n in range(2):
                    ns = slice(n * 512, (n + 1) * 512)
                    pv, pb = C.psum(1)
                    P.mm([(pv[0:NS, 0, :], hidTd.t[:, hc, :], wd.t[:, hc, ns], hc == 0, hc == 31) for hc in range(32)],
                         [hidTd.b, wd.b], pb)
                    P.tt('dve', t1[0].t[0:NS, :], pv[0:NS, 0, :], g2d.t[:, ns], ALU.mult, pb + [g2d.b], [t1[0].b])
                    P.tt('dve', xd.t[:, ns], xd.t[:, ns], t1[0].t[0:NS, :], ALU.add, [xd.b, t1[0].b], [xd.b])
                if last:
                    fss = C.sb(ph, [NS, 2], F32, "fss")
                    yo = g2d.t[:]
                    P.act(yo, xd.t[:], AF.Square, [xd.b], [g2d.b, fss.b], accum_out=fss.t[:, 0:1])
                    P.act(fss.t[:, 1:2], fss.t[:, 0:1], AF.Sqrt, [fss.b, epsT.b], [fss.b], scale=1.0 / D, bias=epsT.t[0:NS, 0:1])
                    P.op('dve', lambda e: e.reciprocal(out=fss.t[:, 1:2], in_=fss.t[:, 1:2]), [fss.b], [fss.b])
                    P.ts('dve', yo, xd.t[:], fss.t[:, 1:2], None, ALU.mult, None, [xd.b, fss.b], [g2d.b])
                    P.tt('dve', yo, yo, fg_bc.t[0:NS, :], ALU.mult, [g2d.b, fg_bc.b], [g2d.b])
                    P.dma('pool', [(ys_out, yo)], [g2d.b], [Buf()])
            nt = 0
            for tt in range(L // 512):
                xt, hid = xts[tt % 2], hts[tt % 2]
                ts_ = slice(tt * 512, (tt + 1) * 512)
                P.dma('sp', [(xt.t[:], x1_d[ts_, :].rearrange("(j p) d -> p j d", p=128)),
                             (hid.t[:], hidT_d[:, ts_].rearrange("(a p) t -> p a t", p=128))], [], [xt.b, hid.b])
                for j in range(4):
                    for n in range(2):
                        ns = slice(n * 512, (n + 1) * 512)
                        pv, pb = C.psum(1)
                        P.mm([(pv[:, 0, :], hid.t[:, hc, j * 128:(j + 1) * 128], wd.t[:, hc, ns], hc == 0, hc == 31)
                              for hc in range(32)], [hid.b, wd.b], pb)
                        tm = t1[nt % 3]
                        nt += 1
                        P.tt('dve', tm.t[:], pv[:, 0, :], g2bc.t[:, ns], ALU.mult, pb + [g2bc.b], [tm.b])
                        P.tt('dve', xt.t[:, j, ns], xt.t[:, j, ns], tm.t[:], ALU.add, [xt.b, tm.b], [xt.b])
                if not last:
                    P.dma('act', [(xm_d[ts_, :].rearrange("(j p) d -> p j d", p=128), xt.t[:])], [xt.b], [Buf()])
                else:
                    rms_rstd(ph, xt, 4, junk, ss, rstd)
                    for j in range(4):
                        P.ts('dve', xt.t[:, j, :], xt.t[:, j, :], rstd.t[:, j:j + 1], None, ALU.mult, None,
                             [xt.b, rstd.b], [xt.b])
                        P.tt('dve', xt.t[:, j, :], xt.t[:, j, :], fg_bc.t[:], ALU.mult, [xt.b, fg_bc.b], [xt.b])
                    P.dma('act', [(y_out[ts_, :].rearrange("(j p) d -> p j d", p=128), xt.t[:])], [xt.b], [Buf()])
        P.barrier()

    def dec_norm_T(ph, xsrc_ap, xsrc_buf, A_, B_, outT, scr):
        ss = C.sb(ph, [NS, 2], F32, "dss")
        tmpT = C.sb(ph, [128, 8, NS], F32, "dtmpT")
        xn = scr.t[0:NS, 0:D] if len(scr.t.shape) == 2 else scr.t[0:NS, 0, :]
        P.act(xn, xsrc_ap, AF.Square, [xsrc_buf], [scr.b, ss.b], accum_out=ss.t[:, 0:1])
        P.act(ss.t[:, 1:2], ss.t[:, 0:1], AF.Sqrt, [ss.b, epsT.b], [ss.b], scale=1.0 / D, bias=epsT.t[0:NS, 0:1])
        P.op('dve', lambda e: e.reciprocal(out=ss.t[:, 1:2], in_=ss.t[:, 1:2]), [ss.b], [ss.b])
        P.act(xn, xsrc_ap, AF.Copy, [xsrc_buf, ss.b], [scr.b], scale=ss.t[:, 1:2])
        pv, pb = C.psum(1)
        P.tr([(pv[:, 0, kc * NS:(kc + 1) * NS], xn[:, kc * 128:(kc + 1) * 128], ident.t[0:NS, 0:NS]) for kc in range(8)],
             [scr.b, ident.b], pb)
        P.tt('dve', tmpT.t[:], pv[:, 0, 0:8 * NS].rearrange("p (k b) -> p k b", b=NS), A_.t[:, :, 1:1 + NS], ALU.mult,
             pb + [A_.b], [tmpT.b])
        P.tt('dve', outT.t[:], tmpT.t[:], B_.t[:, :, 1:1 + NS], ALU.add, [tmpT.b, B_.b], [outT.b])
        return ss

    def dec_proj(wb, wcol, ncols, dcol, tmps, cnt):
        for n0 in range(0, ncols, 512):
            n = min(512, ncols - n0)
            pv, pb = C.psum(1)
            P.mm([(pv[0:NS, 0, 0:n], hdT.t[:, kc, :], wb.t[:, kc, wcol + n0:wcol + n0 + n], kc == 0, kc == 7)
                  for kc in range(8)], [hdT.b, wb.b], pb)
            t = tmps[cnt[0] % len(tmps)]
            cnt[0] += 1
            P.copy('act', t.t[0:NS, 0:n], pv[0:NS, 0, 0:n], pb, [t.b])
            P.dma('pool', [(dproj_d[:, dcol + n0:dcol + n0 + n], t.t[0:NS, 0:n])], [t.b], [Buf()])

    def phase_dec_norm1(l):
        with ExitStack() as ph:
            scr = C.sb(ph, [NS, D], F32, "dscr")
            dec_norm_T(ph, xd.t[:], xd.b, A1, B1, hdT, scr)
        P.barrier()

    def phase_dec_mixers(l):
        with ExitStack() as ph:
            qkvd = C.sb(ph, [NS, 2304], F32, "qkvd")
            xnew = C.sb(ph, [NS, CONV], F32, "xnew")
            xc = C.sb(ph, [NS, CONV], F32, "xc")
            zd = C.sb(ph, [NS, DIN], F32, "zd")
            dtd = C.sb(ph, [NS, 64], F32, "dtd")
            P.dma('sp', [(qkvd.t[:], dproj_d[:, 0:2304]), (xnew.t[:], dproj_d[:, C_X:C_X + CONV]),
                         (zd.t[:], dproj_d[:, C_Z:C_Z + DIN]), (dtd.t[:, 0:32], dproj_d[:, C_DT:C_DT + 32])], [],
                  [qkvd.b, xnew.b, zd.b, dtd.b])
            P.dma('pool', [(convs[l, :, 0:2, :], conv_in[l, :, 1:3, :]), (convs[l, :, 2, :], xnew.t[:])], [xnew.b], [Buf()])
            cst = C.sb(ph, [NS, 3, 1024], F32, "cst")
            cw4 = C.sb(ph, [NS, 4, 1024], F32, "cw4")
            cb4 = C.sb(ph, [NS, 1024], F32, "cb4")
            ctmp = C.sb(ph, [NS, 1024], F32, "ctmp")
            for q4 in range(4):
                cs_ = slice(q4 * 1024, (q4 + 1) * 1024)
                P.dma('sp', [(cst.t[:], conv_in[l, :, :, cs_]),
                             (cw4.t[:], conv_w[l:l + 1, :, cs_].to_broadcast([NS, 4, 1024])),
                             (cb4.t[:], conv_b[l:l + 1, cs_].to_broadcast([NS, 1024]))], [], [cst.b, cw4.b, cb4.b])
                P.tt('dve', xc.t[:, cs_], xnew.t[:, cs_], cw4.t[:, 3, :], ALU.mult, [xnew.b, cw4.b], [xc.b])
                P.tt('dve', xc.t[:, cs_], xc.t[:, cs_], cb4.t[:], ALU.add, [xc.b, cb4.b], [xc.b])
                for j in range(3):
                    P.tt('dve', ctmp.t[:], cst.t[:, j, :], cw4.t[:, j, :], ALU.mult, [cst.b, cw4.b], [ctmp.b])
                    P.tt('dve', xc.t[:, cs_], xc.t[:, cs_], ctmp.t[:], ALU.add, [xc.b, ctmp.b], [xc.b])
            P.act(xc.t[:], xc.t[:], AF.Silu, [xc.b], [xc.b])
            P.tt('dve', dtd.t[:, 0:32], dtd.t[:, 0:32], dtb_bc.t[0:NS, :], ALU.add, [dtd.b, dtb_bc.b], [dtd.b])
            P.act(dtd.t[:, 0:32], dtd.t[:, 0:32], AF.Exp, [dtd.b], [dtd.b])
            P.act(dtd.t[:, 0:32], dtd.t[:, 0:32], AF.Ln, [dtd.b], [dtd.b], bias=1.0)
            P.tt('dve', dtd.t[:, 32:64], dtd.t[:, 0:32], A_bc.t[0:NS, :], ALU.mult, [dtd.b, A_bc.b], [dtd.b])
            P.act(dtd.t[:, 32:64], dtd.t[:, 32:64], AF.Exp, [dtd.b], [dtd.b])
            rows = C.sb(ph, [NS, 2, DIN], F32, "drows")
            P.tt('dve', rows.t[:, 0, :].rearrange("p (h d) -> p h d", d=64), xc.t[:, 0:DIN].rearrange("p (h d) -> p h d", d=64),
                 dtd.t[:, 0:32].unsqueeze(2).to_broadcast([NS, NH, 64]), ALU.mult, [xc.b, dtd.b], [rows.b])
            P.copy('dve', rows.t[:, 1, :].rearrange("p (h d) -> p h d", d=64),
                   dtd.t[:, 32:64].unsqueeze(2).to_broadcast([NS, NH, 64]), [dtd.b], [rows.b])
            colD = C.sb(ph, [128, 2, 16, NS], F32, "colD")
            pv, pb = C.psum(1)
            P.tr([(pv[:, 0, (w * 16 + cc) * NS:(w * 16 + cc + 1) * NS], rows.t[:, w, cc * 128:(cc + 1) * 128], ident.t[0:NS, 0:NS])
                  for w in range(2) for cc in range(16)], [rows.b, ident.b], pb)
            P.copy('act', colD.t[:].rearrange("p w c b -> p (w c b)"), pv[:, 0, 0:32 * NS], pb, [colD.b])
            stt_ = [C.sb(ph, [128, 16, 128], F32, "dst%d" % i) for i in range(2)]
            t1 = C.sb(ph, [128, 16, 128], F32, "dt1")
            ycol = C.sb(ph, [128, 16, NS], F32, "ycol")
            for b in range(NS):
                st_ = stt_[b % 2]
                P.dma('sp', [(st_.t[:], ssm_in[l, b].rearrange("(a p) n -> p a n", p=128))], [], [st_.b])
                pv, pb = C.psum(4)
                for j in range(4):
                    P.mm([(pv[:, j, :], sel4.t[:, b, :], xc.t[:, DIN + j * 512:DIN + (j + 1) * 512], True, True)],
                         [sel4.b, xc.b], pb[j:j + 1])
                Bv = pv[:, 0:2, :].rearrange("p a (g n) -> p (a g) n", n=128)
                Cv = pv[:, 2:4, :].rearrange("p a (g n) -> p (a g) n", n=128)
                P.tt('dve', t1.t[:].rearrange("p (g e) n -> p g e n", e=2), Bv.unsqueeze(2).to_broadcast([128, 8, 2, 128]),
                     colD.t[:, 0, :, b].rearrange("p (g e) -> p g e", e=2).unsqueeze(3).to_broadcast([128, 8, 2, 128]),
                     ALU.mult, pb[0:2] + [colD.b], [t1.b])
                P.tt('dve', st_.t[:], st_.t[:], colD.t[:, 1, :, b].unsqueeze(2).to_broadcast([128, 16, 128]), ALU.mult,
                     [st_.b, colD.b], [st_.b])
                P.tt('dve', st_.t[:], st_.t[:], t1.t[:], ALU.add, [st_.b, t1.b], [st_.b])
                P.dma('pool', [(ssms[l, b].rearrange("(a p) n -> p a n", p=128), st_.t[:])], [st_.b], [Buf()])
                P.tt('dve', t1.t[:].rearrange("p (g e) n -> p g e n", e=2), st_.t[:].rearrange("p (g e) n -> p g e n", e=2),
                     Cv.unsqueeze(2).to_broadcast([128, 8, 2, 128]), ALU.mult, pb[2:4] + [st_.b], [t1.b])
                P.op('dve', lambda e, b=b: e.tensor_reduce(out=ycol.t[:, :, b], in_=t1.t[:], axis=AX.X, op=ALU.add),
                     [t1.b], [ycol.b])
            pv, pb = C.psum(4)
            P.tr([(pv[0:NS, cc // 4, (cc % 4) * 128:(cc % 4 + 1) * 128], ycol.t[:, cc, :], ident.t[:]) for cc in range(16)],
                 [ycol.b, ident.b], pb)
            yd = C.sb(ph, [NS, DIN], F32, "yd")
            P.tt('dve', rows.t[:, 0, :].rearrange("p (h d) -> p h d", d=64), xc.t[:, 0:DIN].rearrange("p (h d) -> p h d", d=64),
                 D_bc.t[0:NS, :].unsqueeze(2).to_broadcast([NS, NH, 64]), ALU.mult, [xc.b, D_bc.b], [rows.b])
            P.tt('dve', yd.t[:].rearrange("p (a b) -> p a b", b=512), pv[0:NS, :, :], rows.t[:, 0, :].rearrange("p (a b) -> p a b", b=512),
                 ALU.add, pb + [rows.b], [yd.b])
            P.act(zd.t[:], zd.t[:], AF.Silu, [zd.b], [zd.b])
            P.tt('dve', yd.t[:], yd.t[:], zd.t[:], ALU.mult, [yd.b, zd.b], [yd.b])
            ssq = C.sb(ph, [NS, 8], F32, "dssq")
            for g in range(8):
                P.act(rows.t[:, 1, 0:256], yd.t[:, g * 256:(g + 1) * 256], AF.Square, [yd.b], [rows.b, ssq.b],
                      accum_out=ssq.t[:, g:g + 1])
            P.act(ssq.t[:], ssq.t[:], AF.Sqrt, [ssq.b, epsT.b], [ssq.b], scale=1.0 / 256, bias=epsT.t[0:NS, 0:1])
            P.op('dve', lambda e: e.reciprocal(out=ssq.t[:], in_=ssq.t[:]), [ssq.b], [ssq.b])
            P.tt('dve', yd.t[:].rearrange("p (g d) -> p g d", d=256), yd.t[:].rearrange("p (g d) -> p g d", d=256),
                 ssq.t[:].unsqueeze(2).to_broadcast([NS, 8, 256]), ALU.mult, [yd.b, ssq.b], [yd.b])
            P.dma('sp', [(rows.t[:, 1, :], ssd_norm_g[l:l + 1, :].to_broadcast([NS, DIN]))], [], [rows.b])
            P.tt('dve', yd.t[:], yd.t[:], rows.t[:, 1, :], ALU.mult, [yd.b, rows.b], [yd.b])
            pv, pb = C.psum(1)
            P.tr([(pv[:, 0, cc * NS:(cc + 1) * NS], yd.t[:, cc * 128:(cc + 1) * 128], ident.t[0:NS, 0:NS]) for cc in range(16)],
                 [yd.b, ident.b], pb)
            P.copy('act', sdT.t[:].rearrange("p c b -> p (c b)"), pv[:, 0, 0:16 * NS], pb, [sdT.b])
            for g in range(3):
                w = WINS[g]
                P.dma('pool', [(kvs[g][l, :, 0:w - 1, :], kvc[g][l, :, 1:w, :]),
                               (kvs[g][l, :, w - 1, 0:256], qkvd.t[:, 768 + g * 256:768 + (g + 1) * 256]),
                               (kvs[g][l, :, w - 1, 256:512], qkvd.t[:, 1536 + g * 256:1536 + (g + 1) * 256])],
                      [qkvd.b], [Buf()])
            prod0 = C.sb(ph, [NS, 768], F32, "prod0")
            e0 = C.sb(ph, [NS, 24], F32, "e0")
            P.dma('sp', [(e0.t[:, 12:24], rel_bias[0:1, :].to_broadcast([NS, 12]))], [], [e0.b])
            P.tt('dve', prod0.t[:], qkvd.t[:, 0:768], qkvd.t[:, 768:1536], ALU.mult, [qkvd.b], [prod0.b])
            P.op('dve', lambda e: e.tensor_reduce(out=e0.t[:, 0:12], in_=prod0.t[:].rearrange("p (h d) -> p h d", d=64),
                                                  axis=AX.X, op=ALU.add), [prod0.b], [e0.b])
            P.stt(e0.t[:, 0:12], e0.t[:, 0:12], 0.125, e0.t[:, 12:24], ALU.mult, ALU.add, [e0.b], [e0.b])
            P.act(e0.t[:, 0:12], e0.t[:, 0:12], AF.Exp, [e0.b], [e0.b])
            P.tt('dve', prod0.t[:].rearrange("p (h d) -> p h d", d=64), qkvd.t[:, 1536:2304].rearrange("p (h d) -> p h d", d=64),
                 e0.t[:, 0:12].unsqueeze(2).to_broadcast([NS, 12, 64]), ALU.mult, [qkvd.b, e0.b], [prod0.b])
            u0 = C.sb(ph, [NS, 260], F32, "u0")
            P.tt('dve', u0.t[:, 0:256], prod0.t[:, 0:256], prod0.t[:, 256:512], ALU.add, [prod0.b], [u0.b])
            P.tt('dve', u0.t[:, 0:256], u0.t[:, 0:256], prod0.t[:, 512:768], ALU.add, [prod0.b, u0.b], [u0.b])
            P.tt('dve', u0.t[:, 256:260], e0.t[:, 0:4], e0.t[:, 4:8], ALU.add, [e0.b], [u0.b])
            P.tt('dve', u0.t[:, 256:260], u0.t[:, 256:260], e0.t[:, 8:12], ALU.add, [e0.b, u0.b], [u0.b])
            kvt = [C.sb(ph, [128, 516], F32, "kvt%d" % i) for i in range(2)]
            for i in range(2):
                P.memset('dve', kvt[i].t[:, 512:516], 1.0, [kvt[i].b])
            prod = C.sb(ph, [128, 256], F32, "dprod")
            sc = C.sb(ph, [128, 8], F32, "dsc")
            mk = C.sb(ph, [4, NS, 260], F32, "mk")
            nk = 0
            for b in range(NS):
                pq, pqb = C.psum(2)
                P.mm([(pq[:, 0, :], sel4.t[:, b, :], qkvd.t[:, 0:512], True, True)], [sel4.b, qkvd.b], pqb[0:1])
                P.mm([(pq[:, 1, 0:256], sel4.t[:, b, :], qkvd.t[:, 512:768], True, True)], [sel4.b, qkvd.b], pqb[1:2])
                pu, pub = C.psum(1)
                for g in range(3):
                    kt = kvt[nk % 2]
                    nk += 1
                    P.dma('sp', [(kt.t[:, 0:512], kvc[g][l, b, 0:WINS[g]:DILS[g], :])], [], [kt.b])
                    qv = pq[:, g // 2, (g % 2) * 256:(g % 2) * 256 + 256]
                    P.tt('dve', prod.t[:], kt.t[:, 0:256], qv, ALU.mult, [kt.b] + pqb[g // 2:g // 2 + 1], [prod.b])
                    P.op('dve', lambda e: e.tensor_reduce(out=sc.t[:, 0:4], in_=prod.t[:].rearrange("p (h d) -> p h d", d=64),
                                                          axis=AX.X, op=ALU.add), [prod.b], [sc.b])
                    P.stt(sc.t[:, 0:4], sc.t[:, 0:4], 0.125, biasd.t[:, g * 4:(g + 1) * 4], ALU.mult, ALU.add,
                          [sc.b, biasd.b], [sc.b])
                    P.act(sc.t[:, 4:8], sc.t[:, 0:4], AF.Exp, [sc.b], [sc.b])
                    P.mm([(pu[0:4, 0, 0:260], sc.t[:, 4:8], kt.t[:, 256:516], g == 0, g == 2)], [sc.b, kt.b], pub)
                P.tt('dve', mk.t[:, b, :], pu[0:4, 0, 0:260], dmask.t[:], ALU.mult, pub + [dmask.b], [mk.b])
            po, pob = C.psum(1)
            P.mm([(po[0:NS, 0, 0:260], colsel.t[:, b, :], mk.t[:, b, :], b == 0, b == NS - 1) for b in range(NS)],
                 [colsel.b, mk.b], pob)
            P.tt('dve', u0.t[:], u0.t[:], po[0:NS, 0, 0:260], ALU.add, [u0.b] + pob, [u0.b])
            P.op('dve', lambda e: e.reciprocal(out=u0.t[:, 256:260], in_=u0.t[:, 256:260]), [u0.b], [u0.b])
            att = C.sb(ph, [NS, 256], F32, "datt")
            P.tt('dve', att.t[:].rearrange("p (h d) -> p h d", d=64), u0.t[:, 0:256].rearrange("p (h d) -> p h d", d=64),
                 u0.t[:, 256:260].unsqueeze(2).to_broadcast([NS, 4, 64]), ALU.mult, [u0.b], [att.b])
            pv, pb = C.psum(1)
            P.tr([(pv[:, 0, cc * NS:(cc + 1) * NS], att.t[:, cc * 128:(cc + 1) * 128], ident.t[0:NS, 0:NS]) for cc in range(2)],
                 [att.b, ident.b], pb)
            P.copy('act', adT.t[:].rearrange("p c b -> p (c b)"), pv[:, 0, 0:2 * NS], pb, [adT.b])
        P.barrier()

    def dec_out(ph, l, woa, wos, wo, scr, t1s):
        gd = scr.t[0:NS, 0:2, :].rearrange("p a d -> p (a d)")
        g1d = scr.t[0:NS, 2, :]
        md = scr.t[0:NS, 3, :]
        P.dma('sp', [(scr.t[0:NS, 0:2, :], dproj_d[:, C_G:C_G + DIN].rearrange("p (a d) -> p a d", a=2)),
                     (g1d, grows_d[l, 1:1 + NS, 0:D])], [], [scr.b])
        P.act(gd, gd, AF.Sigmoid, [scr.b], [scr.b])
        for n in range(2):
            ns = slice(n * 512, (n + 1) * 512)
            pa, pab = C.psum(1)
            P.mm([(pa[0:NS, 0, :], adT.t[:, kc, :], woa.t[:, kc, ns], kc == 0, kc == 1) for kc in range(2)], [adT.b, woa.b], pab)
            P.tt('dve', t1s[0].t[0:NS, :], pa[0:NS, 0, :], gd[:, n * 512:(n + 1) * 512], ALU.mult, pab + [scr.b], [t1s[0].b])
            ps_, psb_ = C.psum(1)
            P.mm([(ps_[0:NS, 0, :], sdT.t[:, kc, :], wos.t[:, kc, ns], kc == 0, kc == 15) for kc in range(16)], [sdT.b, wos.b], psb_)
            P.tt('dve', t1s[1].t[0:NS, :], ps_[0:NS, 0, :], gd[:, D + n * 512:D + (n + 1) * 512], ALU.mult, psb_ + [scr.b],
                 [t1s[1].b])
            P.tt('dve', md[:, ns], t1s[0].t[0:NS, :], t1s[1].t[0:NS, :], ALU.add, [t1s[0].b, t1s[1].b], [scr.b])
        mT = C.sb(ph, [128, 8, NS], BF, "dmT")
        pv, pb = C.psum(1)
        P.tr([(pv[:, 0, kc * NS:(kc + 1) * NS], md[:, kc * 128:(kc + 1) * 128], ident.t[0:NS, 0:NS]) for kc in range(8)],
             [scr.b, ident.b], pb)
        P.copy('act', mT.t[:].rearrange("p k b -> p (k b)"), pv[:, 0, 0:8 * NS], pb, [mT.b])
        for n in range(2):
            ns = slice(n * 512, (n + 1) * 512)
            pv, pb = C.psum(1)
            P.mm([(pv[0:NS, 0, :], mT.t[:, kc, :], wo.t[:, kc, ns], kc == 0, kc == 7) for kc in range(8)], [mT.b, wo.b], pb)
            P.tt('dve', t1s[0].t[0:NS, :], pv[0:NS, 0, :], g1d[:, ns], ALU.mult, pb + [scr.b], [t1s[0].b])
            P.tt('dve', xd.t[:, ns], xd.t[:, ns], t1s[0].t[0:NS, :], ALU.add, [xd.b, t1s[0].b], [xd.b])

    def dec_mlp(ph, l, wu, wd, scr, t1s, last):
        h2dT = C.sb(ph, [128, 8, NS], BF, "h2dT")
        hidT = C.sb(ph, [128, 32, NS], BF, "hidT")
        dec_norm_T(ph, xd.t[:], xd.b, A2, B2, h2dT, scr)
        g2d = scr.t[0:NS, 1, :]
        P.dma('sp', [(g2d, grows_d[l, 1:1 + NS, D:2 * D])], [], [scr.b])
        for n in range(8):
            pv, pb = C.psum(1)
            P.mm([(pv[0:NS, 0, :], h2dT.t[:, kc, :], wu.t[:, kc, n * 512:(n + 1) * 512], kc == 0, kc == 7) for kc in range(8)],
                 [h2dT.b, wu.b], pb)
            P.act(t1s[0].t[0:NS, :], pv[0:NS, 0, :], AF.Relu, pb, [t1s[0].b])
            P.tt('dve', t1s[1].t[0:NS, :], t1s[0].t[0:NS, :], pv[0:NS, 0, :], ALU.mult, pb + [t1s[0].b], [t1s[1].b])
            pt, ptb = C.psum(1)
            P.tr([(pt[:, 0, i * NS:(i + 1) * NS], t1s[1].t[0:NS, i * 128:(i + 1) * 128], ident.t[0:NS, 0:NS]) for i in range(4)],
                 [t1s[1].b, ident.b], ptb)
            P.copy('act', hidT.t[:, n * 4:(n + 1) * 4, :].rearrange("p c b -> p (c b)"), pt[:, 0, 0:4 * NS], ptb, [hidT.b])
        for n in range(2):
            ns = slice(n * 512, (n + 1) * 512)
            pv, pb = C.psum(1)
            P.mm([(pv[0:NS, 0, :], hidT.t[:, hc, :], wd.t[:, hc, ns], hc == 0, hc == 31) for hc in range(32)], [hidT.b, wd.b], pb)
            P.tt('dve', t1s[0].t[0:NS, :], pv[0:NS, 0, :], g2d[:, ns], ALU.mult, pb + [scr.b], [t1s[0].b])
            P.tt('dve', xd.t[:, ns], xd.t[:, ns], t1s[0].t[0:NS, :], ALU.add, [xd.b, t1s[0].b], [xd.b])
        if last:
            ss = C.sb(ph, [NS, 2], F32, "fss")
            yo = scr.t[0:NS, 0, :]
            P.act(yo, xd.t[:], AF.Square, [xd.b], [scr.b, ss.b], accum_out=ss.t[:, 0:1])
            P.act(ss.t[:, 1:2], ss.t[:, 0:1], AF.Sqrt, [ss.b, epsT.b], [ss.b], scale=1.0 / D, bias=epsT.t[0:NS, 0:1])
            P.op('dve', lambda e: e.reciprocal(out=ss.t[:, 1:2], in_=ss.t[:, 1:2]), [ss.b], [ss.b])
            P.ts('dve', yo, xd.t[:], ss.t[:, 1:2], None, ALU.mult, None, [xd.b, ss.b], [scr.b])
            P.tt('dve', yo, yo, fg_bc.t[0:NS, :], ALU.mult, [scr.b, fg_bc.b], [scr.b])
            P.dma('pool', [(ys_out, yo)], [scr.b], [Buf()])

    es_layer = [es]
    for l in range(DEPTH):
        phase_params(l)
        if 'params_only' in stages:
            break
        if DECODE:
            phase_dec_norm1(l)
        phase_norm1(l)
        if 'norm1_only' in stages:
            break
        phase_inproj_T(l)
        if 'inT_only' in stages:
            break
        phase_inproj_tok(l)
        if 'intok_only' in stages:
            break
        phase_attn(l)
        if 'attn_only' in stages:
            break
        phase_ssd(l)
        if 'ssd_only' in stages:
            break
        if DECODE:
            phase_dec_mixers(l)
        phase_out(l)
        if 'out_only' in stages:
            break
        phase_mlp(l)
        if 'l0_only' in stages:
            break

    if 'params_only' in stages:
        for nm, tb, shp in (("dbg_A1", A1, [128, 40]), ("dbg_B1", B1, [128, 40]), ("dbg_A2", A2, [128, 40]),
                            ("dbg_B2", B2, [128, 40])):
            o_ = dout(nm, shp)
            P.dma('sp', [(o_, tb.t[:].rearrange("p a b -> p (a b)"))], [tb.b], [C.dbuf(nm)])
        for nm, tb, shp in (("dbg_g1bc", g1bc, [128, D]), ("dbg_g2bc", g2bc, [128, D]), ("dbg_cwT", cwT, [128, 128]),
                            ("dbg_colp", colp, [128, 96]), ("dbg_Abc", A_bc, [128, NH])):
            o_ = dout(nm, shp)
            src_ = tb.t[:].rearrange("p a b -> p (a b)") if len(tb.t.shape) == 3 else tb.t[:]
            P.dma('sp', [(o_, src_)], [tb.b], [C.dbuf(nm)])

    P.finalize(nc, es)


def prep_inputs(inputs):
    f = lambda a: np.ascontiguousarray(np.asarray(a, dtype=np.float32))
    consts = make_consts()
    shared = {}
    for k in ("rel_bias", "w_ada", "b_ada", "norm1_g", "norm2_g", "w_in", "conv_w", "conv_b", "dt_bias",
              "a_log", "d_skip", "ssd_norm_g", "w_o_attn", "w_o_ssd", "w_out", "w_up", "w_down", "final_g"):
        shared[k] = f(inputs[k])
    shared.update(consts)
    maps = []
    for core in range(8):
        b = core // 4
        s0 = core * NS
        m = dict(shared)
        m["x"] = f(inputs["x_prompt"][b][:L])
        m["c5"] = f(np.concatenate([inputs["c_prompt"][b:b + 1], inputs["c_sample"][s0:s0 + NS]], axis=0))
        m["xs"] = f(inputs["x_sample"][s0:s0 + NS, 0])
        caches = (inputs["cache_kv_g0"], inputs["cache_kv_g1"], inputs["cache_kv_g2"])
        for g in range(3):
            kv = np.asarray(caches[g])[:, s0:s0 + NS]
            m["kvc%d" % g] = f(kv.reshape(DEPTH, NS, WINS[g], 512))
        m["ssm_in"] = f(np.asarray(inputs["state_ssm"])[:, s0:s0 + NS].reshape(DEPTH, NS, NH * 64, 128))
        m["conv_in"] = f(np.asarray(inputs["state_conv"])[:, s0:s0 + NS])
        maps.append(m)
    return maps


_NC_CACHE = {}


def kernel(**inputs):
    maps = prep_inputs(inputs)
    if "nc" not in _NC_CACHE:
        _NC_CACHE["nc"] = build()
    nc = _NC_CACHE["nc"]
    res = run_bass_kernel_spmd(nc, maps, core_ids=list(range(8)))
    R = res.results
    B = 2
    y_prompt = np.stack([R[0]["y"], R[4]["y"]], axis=0)
    y_sample = np.concatenate([R[c]["ys"] for c in range(8)], axis=0).reshape(32, 1, D)
    outs = [y_prompt.astype(np.float32), y_sample.astype(np.float32)]
    for g in range(3):
        kvg = np.stack([R[0]["kvp%d" % g], R[4]["kvp%d" % g]], axis=1)
        outs.append(kvg.reshape(DEPTH, B, WINS[g], 2, 4, 64).astype(np.float32))
    outs.append(np.stack([R[0]["ssmp"], R[4]["ssmp"]], axis=1).reshape(DEPTH, B, NH, 64, 128).astype(np.float32))
    outs.append(np.stack([R[0]["convp"], R[4]["convp"]], axis=1).reshape(DEPTH, B, 3, CONV).astype(np.float32))
    for g in range(3):
        kvg = np.concatenate([R[c]["kvs%d" % g] for c in range(8)], axis=1)
        outs.append(kvg.reshape(DEPTH, 32, WINS[g], 2, 4, 64).astype(np.float32))
    outs.append(np.concatenate([R[c]["ssms"] for c in range(8)], axis=1).reshape(DEPTH, 32, NH, 64, 128)
                .astype(np.float32))
    outs.append(np.concatenate([R[c]["convs"] for c in range(8)], axis=1).reshape(DEPTH, 32, 3, CONV)
                .astype(np.float32))
    return tuple(outs)
```
